# Optimizing a Trainium2 kernel written in Bass

```python
import math
import jax
import jax.numpy as jnp
from jax import lax
import numpy as np

D_MODEL = 2048
BATCH = 2
SEQ = 8192
DEPTH = 4

N_MIXERS = 3
MIX_WIDTH = D_MODEL
HEAD_DIM = 128
MEM_LEN = 256
MEM_HEADS = 4
MEM_WIDTH = MEM_HEADS * HEAD_DIM
MIXER_WIDTH = MIX_WIDTH - MEM_WIDTH
D_FF = 4 * D_MODEL
EPS = 1e-6
S5_GROUP = 16
S5_GROUPS = MIXER_WIDTH // S5_GROUP
S5_STATE = 64
S5_CHUNK = 128
GDN_HEADS = MIXER_WIDTH // HEAD_DIM
GDN_CONV = 4
GDN_CHUNK = 64
FOX_HEADS = MIXER_WIDTH // HEAD_DIM
FOX_BLOCK = 128
N_S5 = (DEPTH + 2) // N_MIXERS
N_GDN = (DEPTH + 1) // N_MIXERS
N_FOX = DEPTH // N_MIXERS
S5_IN = MIXER_WIDTH + MEM_WIDTH
GDN_IN = 4 * MIXER_WIDTH + 2 * GDN_HEADS + MEM_WIDTH
FOX_IN = 3 * MIXER_WIDTH + FOX_HEADS + MEM_WIDTH

kernel_name = 'hybrid_s5_gdn_fox_memory_trunk'


def _rmsnorm(x, gain):
    x32 = x.astype(jnp.float32)
    y = x32 * lax.rsqrt(jnp.mean(x32 * x32, axis=-1, keepdims=True) + EPS)
    return (y * gain.astype(jnp.float32)).astype(x.dtype)


def _l2norm(x):
    return x * lax.rsqrt(jnp.sum(x * x, axis=-1, keepdims=True) + EPS)


def _complex_affine_combine(e1, e2):
    a1r, a1i, b1r, b1i = e1
    a2r, a2i, b2r, b2i = e2
    return (a1r * a2r - a1i * a2i,
            a1r * a2i + a1i * a2r,
            a2r * b1r - a2i * b1i + b2r,
            a2r * b1i + a2i * b1r + b2i)


def _s5_mixer(u, lam_re, lam_im, log_dt, b_re, b_im, c_re, c_im, d_skip, w_glu, b_glu):
    bsz, seq, _ = u.shape
    f32 = jnp.float32
    n_chunks = seq // S5_CHUNK
    l_re, l_im = lam_re.astype(f32), lam_im.astype(f32)
    dt = jnp.exp(log_dt.astype(f32))[:, None]
    mag = jnp.exp(l_re * dt)
    a_re, a_im = mag * jnp.cos(l_im * dt), mag * jnp.sin(l_im * dt)
    den = l_re * l_re + l_im * l_im
    z_re = ((a_re - 1.0) * l_re + a_im * l_im) / den
    z_im = (a_im * l_re - (a_re - 1.0) * l_im) / den
    br, bi = b_re.astype(f32), b_im.astype(f32)
    bb_re = z_re[..., None] * br - z_im[..., None] * bi
    bb_im = z_re[..., None] * bi + z_im[..., None] * br
    cr, ci = c_re.astype(f32), c_im.astype(f32)
    blk = (bsz, S5_CHUNK, S5_GROUPS, S5_STATE)
    a_blk_re, a_blk_im = jnp.broadcast_to(a_re, blk), jnp.broadcast_to(a_im, blk)
    u32 = u.astype(f32)
    u_chunks = u32.reshape(bsz, n_chunks, S5_CHUNK, S5_GROUPS, S5_GROUP).transpose(1, 0, 2, 3, 4)

    def step(carry, uc):
        s_re, s_im = carry
        bu_re = jnp.einsum('blgc,gpc->blgp', uc, bb_re)
        bu_im = jnp.einsum('blgc,gpc->blgp', uc, bb_im)
        p_re, p_im, h_re, h_im = lax.associative_scan(
            _complex_affine_combine, (a_blk_re, a_blk_im, bu_re, bu_im), axis=1)
        h_re = h_re + p_re * s_re[:, None] - p_im * s_im[:, None]
        h_im = h_im + p_re * s_im[:, None] + p_im * s_re[:, None]
        y = jnp.einsum('blgp,gcp->blgc', h_re, cr) - jnp.einsum('blgp,gcp->blgc', h_im, ci)
        return (h_re[:, -1], h_im[:, -1]), y

    zeros = jnp.zeros((bsz, S5_GROUPS, S5_STATE), f32)
    _, y = lax.scan(step, (zeros, zeros), u_chunks)
    y = y.transpose(1, 0, 2, 3, 4).reshape(bsz, seq, MIXER_WIDTH)
    y = jax.nn.gelu(y + d_skip.astype(f32) * u32).astype(u.dtype)
    return y * jax.nn.sigmoid(y @ w_glu + b_glu)


def _causal_depthwise_conv(x, w):
    k = w.shape[0]
    return lax.conv_general_dilated(
        x, w[:, None, :].astype(x.dtype), window_strides=(1,), padding=[(k - 1, 0)],
        dimension_numbers=('NWC', 'WIO', 'NWC'), feature_group_count=x.shape[-1])


def _gdn_mixer(proj, conv_w, a_log, dt_bias, o_norm):
    bsz, seq, _ = proj.shape
    f32 = jnp.float32
    wd, nh, hd, cl = MIXER_WIDTH, GDN_HEADS, HEAD_DIM, GDN_CHUNK
    nc = seq // cl
    qkv = jax.nn.silu(_causal_depthwise_conv(proj[..., :3 * wd], conv_w)).astype(f32)
    gate = proj[..., 3 * wd:4 * wd].astype(f32).reshape(bsz, seq, nh, hd)
    a_in = proj[..., 4 * wd:4 * wd + nh].astype(f32)
    b_in = proj[..., 4 * wd + nh:4 * wd + 2 * nh].astype(f32)
    q = _l2norm(qkv[..., :wd].reshape(bsz, seq, nh, hd)) * hd ** -0.5
    k = _l2norm(qkv[..., wd:2 * wd].reshape(bsz, seq, nh, hd))
    v = qkv[..., 2 * wd:].reshape(bsz, seq, nh, hd)
    beta = jax.nn.sigmoid(b_in)
    g = -jnp.exp(a_log.astype(f32)) * jax.nn.softplus(a_in + dt_bias.astype(f32))

    def chunks(t):
        return t.reshape(bsz, nc, cl, nh, -1).transpose(0, 3, 1, 2, 4)

    q, k, v = chunks(q), chunks(k), chunks(v)
    beta = chunks(beta[..., None])
    gc = jnp.cumsum(chunks(g[..., None])[..., 0], axis=-1)
    idx = jnp.arange(cl)
    lower = idx[:, None] >= idx[None, :]
    strict = idx[:, None] > idx[None, :]
    decay = jnp.exp(jnp.where(lower, gc[..., :, None] - gc[..., None, :], -jnp.inf))
    kb, vb = k * beta, v * beta
    lmat = jnp.where(strict, jnp.einsum('bhncd,bhnsd->bhncs', kb, k) * decay, 0.0)
    rhs = jnp.concatenate([vb, kb * jnp.exp(gc)[..., None]], axis=-1)
    sol = lax.linalg.triangular_solve(lmat + jnp.eye(cl, dtype=f32), rhs, left_side=True,
                                      lower=True, unit_diagonal=True)
    u_c, w_c = sol[..., :hd], sol[..., hd:]
    attn_in = jnp.where(lower, jnp.einsum('bhncd,bhnsd->bhncs', q, k) * decay, 0.0)
    q_dec = q * jnp.exp(gc)[..., None]
    k_dec = k * jnp.exp(gc[..., -1:] - gc)[..., None]
    g_last = jnp.exp(gc[..., -1])
    xs = tuple(jnp.moveaxis(t, 2, 0) for t in (u_c, w_c, attn_in, q_dec, k_dec, g_last))

    def step(state, inp):
        u_i, w_i, a_i, qd_i, kd_i, gl_i = inp
        v_new = u_i - jnp.einsum('bhck,bhkv->bhcv', w_i, state)
        out = jnp.einsum('bhck,bhkv->bhcv', qd_i, state) + jnp.einsum('bhcs,bhsv->bhcv', a_i, v_new)
        state = state * gl_i[..., None, None] + jnp.einsum('bhck,bhcv->bhkv', kd_i, v_new)
        return state, out

    _, o = lax.scan(step, jnp.zeros((bsz, nh, hd, hd), f32), xs)
    o = o.transpose(1, 0, 3, 2, 4).reshape(bsz, seq, nh, hd)
    o = _rmsnorm(o, o_norm) * jax.nn.silu(gate)
    return o.reshape(bsz, seq, wd).astype(proj.dtype)


def _fox_mixer(proj, b_f):
    bsz, seq, _ = proj.shape
    f32 = jnp.float32
    wd, nh, hd = MIXER_WIDTH, FOX_HEADS, HEAD_DIM

    def heads(t):
        return t.reshape(bsz, seq, nh, hd).transpose(0, 2, 1, 3)

    q, k, v = heads(proj[..., :wd]), heads(proj[..., wd:2 * wd]), heads(proj[..., 2 * wd:3 * wd])
    f_logit = proj[..., 3 * wd:3 * wd + nh].astype(f32) + b_f.astype(f32)
    cum_f = jnp.cumsum(jax.nn.log_sigmoid(f_logit), axis=1).transpose(0, 2, 1)
    scale = hd ** -0.5
    q_off = jnp.arange(FOX_BLOCK)
    outs = []
    for blk in range(seq // FOX_BLOCK):
        q0, q1 = blk * FOX_BLOCK, (blk + 1) * FOX_BLOCK
        logits = (jnp.einsum('bhqd,bhkd->bhqk', q[:, :, q0:q1], k[:, :, :q1]).astype(f32) * scale
                  + cum_f[:, :, q0:q1, None] - cum_f[:, :, None, :q1])
        causal = (q0 + q_off)[:, None] >= jnp.arange(q1)[None, :]
        p = jax.nn.softmax(jnp.where(causal, logits, -jnp.inf), axis=-1).astype(v.dtype)
        outs.append(jnp.einsum('bhqk,bhkd->bhqd', p, v[:, :, :q1]))
    o = jnp.concatenate(outs, axis=2)
    return o.transpose(0, 2, 1, 3).reshape(bsz, seq, wd)


def _memory_attention(q_proj, mem_k, mem_v):
    bsz, seq, _ = q_proj.shape
    q = q_proj.reshape(bsz, seq, MEM_HEADS, HEAD_DIM)
    logits = jnp.einsum('bshd,bmhd->bhsm', q, mem_k).astype(jnp.float32) * HEAD_DIM ** -0.5
    p = jax.nn.softmax(logits, axis=-1).astype(mem_v.dtype)
    return jnp.einsum('bhsm,bmhd->bshd', p, mem_v).reshape(bsz, seq, MEM_WIDTH)


def setup_inputs(seed: int = 0) -> dict:
    key = jax.random.key(seed)
    keys = iter(jax.random.split(key, 32))
    f32 = jnp.float32

    def normal(shape, scale):
        return scale * jax.random.normal(next(keys), shape, f32)

    def gain(shape):
        return 1.0 + normal(shape, 0.05)

    def uniform(shape, lo, hi):
        return jax.random.uniform(next(keys), shape, f32, lo, hi)

    x = normal((BATCH, SEQ, D_MODEL), 1.0)
    mem = normal((BATCH, MEM_LEN, D_MODEL), 1.0)
    mem_norm = gain((D_MODEL,))
    w_mem_kv = normal((D_MODEL, 2 * MEM_WIDTH), D_MODEL ** -0.5)
    norm1 = gain((DEPTH, D_MODEL))
    w_out = normal((DEPTH, MIX_WIDTH, D_MODEL), MIX_WIDTH ** -0.5)
    norm2 = gain((DEPTH, D_MODEL))
    w_up = normal((DEPTH, D_MODEL, D_FF), D_MODEL ** -0.5)
    w_down = normal((DEPTH, D_FF, D_MODEL), D_FF ** -0.5)
    norm_f = gain((D_MODEL,))
    s5_w_in = normal((N_S5, D_MODEL, S5_IN), D_MODEL ** -0.5)
    s5_lam_re = -0.5 * jnp.exp(normal((N_S5, S5_GROUPS, S5_STATE), 0.05))
    s5_lam_im = math.pi * jnp.arange(S5_STATE, dtype=f32) + normal((N_S5, S5_GROUPS, S5_STATE), 0.01)
    s5_log_dt = uniform((N_S5, S5_GROUPS), math.log(1e-3), math.log(1e-1))
    s5_b_re = normal((N_S5, S5_GROUPS, S5_STATE, S5_GROUP), (2 * S5_GROUP) ** -0.5)
    s5_b_im = normal((N_S5, S5_GROUPS, S5_STATE, S5_GROUP), (2 * S5_GROUP) ** -0.5)
    s5_c_re = normal((N_S5, S5_GROUPS, S5_GROUP, S5_STATE), S5_STATE ** -0.5)
    s5_c_im = normal((N_S5, S5_GROUPS, S5_GROUP, S5_STATE), S5_STATE ** -0.5)
    s5_d_skip = normal((N_S5, MIXER_WIDTH), 1.0)
    s5_w_glu = normal((N_S5, MIXER_WIDTH, MIXER_WIDTH), MIXER_WIDTH ** -0.5)
    s5_b_glu = normal((N_S5, MIXER_WIDTH), 0.01)
    gdn_w_in = normal((N_GDN, D_MODEL, GDN_IN), D_MODEL ** -0.5)
    gdn_conv_w = normal((N_GDN, GDN_CONV, 3 * MIXER_WIDTH), GDN_CONV ** -0.5)
    gdn_a_log = jnp.log(uniform((N_GDN, GDN_HEADS), 1.0, 16.0))
    dt = jnp.exp(uniform((N_GDN, GDN_HEADS), math.log(1e-3), math.log(1e-1)))
    gdn_dt_bias = dt + jnp.log(-jnp.expm1(-dt))
    gdn_o_norm = gain((N_GDN, HEAD_DIM))
    fox_w_in = normal((N_FOX, D_MODEL, FOX_IN), D_MODEL ** -0.5)
    fox_b_f = uniform((N_FOX, FOX_HEADS), 1.0, 6.0)
    return {'x': x, 'mem': mem, 'mem_norm': mem_norm, 'w_mem_kv': w_mem_kv,
            'norm1': norm1, 'w_out': w_out, 'norm2': norm2, 'w_up': w_up, 'w_down': w_down,
            'norm_f': norm_f,
            's5_w_in': s5_w_in, 's5_lam_re': s5_lam_re, 's5_lam_im': s5_lam_im,
            's5_log_dt': s5_log_dt, 's5_b_re': s5_b_re, 's5_b_im': s5_b_im,
            's5_c_re': s5_c_re, 's5_c_im': s5_c_im, 's5_d_skip': s5_d_skip,
            's5_w_glu': s5_w_glu, 's5_b_glu': s5_b_glu,
            'gdn_w_in': gdn_w_in, 'gdn_conv_w': gdn_conv_w, 'gdn_a_log': gdn_a_log,
            'gdn_dt_bias': gdn_dt_bias, 'gdn_o_norm': gdn_o_norm,
            'fox_w_in': fox_w_in, 'fox_b_f': fox_b_f}


def reference(x, mem, mem_norm, w_mem_kv, norm1, w_out, norm2, w_up, w_down, norm_f,
              s5_w_in, s5_lam_re, s5_lam_im, s5_log_dt, s5_b_re, s5_b_im, s5_c_re, s5_c_im,
              s5_d_skip, s5_w_glu, s5_b_glu,
              gdn_w_in, gdn_conv_w, gdn_a_log, gdn_dt_bias, gdn_o_norm,
              fox_w_in, fox_b_f):
    bsz = x.shape[0]
    mkv = _rmsnorm(mem, mem_norm) @ w_mem_kv
    mem_k = mkv[..., :MEM_WIDTH].reshape(bsz, MEM_LEN, MEM_HEADS, HEAD_DIM)
    mem_v = mkv[..., MEM_WIDTH:].reshape(bsz, MEM_LEN, MEM_HEADS, HEAD_DIM)
    h = x
    for i in range(DEPTH):
        kind, j = i % N_MIXERS, i // N_MIXERS
        a = _rmsnorm(h, norm1[i])
        if kind == 0:
            proj = a @ s5_w_in[j]
            mix = _s5_mixer(proj[..., :-MEM_WIDTH], s5_lam_re[j], s5_lam_im[j], s5_log_dt[j],
                            s5_b_re[j], s5_b_im[j], s5_c_re[j], s5_c_im[j], s5_d_skip[j],
                            s5_w_glu[j], s5_b_glu[j])
        elif kind == 1:
            proj = a @ gdn_w_in[j]
            mix = _gdn_mixer(proj[..., :-MEM_WIDTH], gdn_conv_w[j], gdn_a_log[j],
                             gdn_dt_bias[j], gdn_o_norm[j])
        else:
            proj = a @ fox_w_in[j]
            mix = _fox_mixer(proj[..., :-MEM_WIDTH], fox_b_f[j])
        read = _memory_attention(proj[..., -MEM_WIDTH:], mem_k, mem_v)
        h = h + jnp.concatenate([mix, read], axis=-1) @ w_out[i]
        a = _rmsnorm(h, norm2[i])
        h = h + jnp.square(jax.nn.relu(a @ w_up[i])) @ w_down[i]
    return _rmsnorm(h, norm_f)
```

```python
import contextlib
import math
import numpy as np
import concourse.bass as bass
import concourse.mybir as mybir
from concourse.bass_utils import run_bass_kernel_spmd

F32 = mybir.dt.float32
BF16 = mybir.dt.bfloat16
AF = mybir.ActivationFunctionType
ALU = mybir.AluOpType
AX = mybir.AxisListType


class Tl:
    __slots__ = ("ap", "lw", "rd", "name", "dsem")

    def __init__(self, ap, name=""):
        self.ap = ap
        self.lw = None
        self.rd = []
        self.name = name
        self.dsem = None

    def __getitem__(self, idx):
        return self.ap[idx]


class Eng:
    def __init__(self, k, e, name):
        self.k = k
        self.e = e
        self.name = name
        self.sem = k.new_sem("e_" + name)
        self.count = 0
        self.known = {}


class K:
    def __init__(self, nc):
        self.nc = nc
        self.ctx = contextlib.ExitStack()
        self.sems = {}
        self.dma_tot = {}
        self.nsem = 0
        self.pe = Eng(self, nc.tensor, "pe")
        self.act = Eng(self, nc.scalar, "act")
        self.dve = Eng(self, nc.vector, "dve")
        self.pool = Eng(self, nc.gpsimd, "pool")
        self.sp = Eng(self, nc.sync, "sp")
        self.engs = [self.pe, self.act, self.dve, self.pool, self.sp]
        self.ndma = 0
        self.dma_rr = 0
        self.dma_pool = [self.new_sem("dma%d" % i) for i in range(24)]
        self.dma_last_ev = {}

    def new_sem(self, name):
        h = self.ctx.enter_context(self.nc.semaphore(name))
        key = self.nsem
        self.nsem += 1
        self.sems[key] = h
        self.dma_tot[key] = 0
        return key

    def sb(self, name, shape, dt):
        t = self.ctx.enter_context(self.nc.sbuf_tensor(name, list(shape), dt))
        return Tl(t.ap() if hasattr(t, "ap") and callable(getattr(t, "ap")) else t, name)

    def ps(self, name, shape, dt=F32):
        t = self.ctx.enter_context(self.nc.psum_tensor(name, list(shape), dt))
        return Tl(t.ap() if hasattr(t, "ap") and callable(getattr(t, "ap")) else t, name)

    def sub(self, tl, ap, name=""):
        return Tl(ap, name or tl.name)

    def _wait(self, eng, deps):
        need = {}
        for d in deps:
            if d is None:
                continue
            s, v = d
            if s in self.dma_tot and self.dma_tot[s] > 0:
                v = self.dma_tot[s]
            if need.get(s, 0) < v:
                need[s] = v
        for s, v in need.items():
            if s == eng.sem and eng is self.pe:
                continue
            if eng.known.get(s, 0) < v:
                eng.e.wait_ge(self.sems[s], v)
                eng.known[s] = v

    def _deps(self, reads, writes):
        deps = []
        for t in reads:
            if t.lw is not None:
                deps.append(t.lw)
        for t in writes:
            if t.lw is not None:
                deps.append(t.lw)
            deps.extend(t.rd)
        return deps

    def _commit(self, ev, reads, writes):
        for t in reads:
            t.rd.append(ev)
            if len(t.rd) > 64:
                best = {}
                for s, v in t.rd:
                    if best.get(s, 0) < v:
                        best[s] = v
                t.rd = list(best.items())
        for t in writes:
            t.lw = ev
            t.rd = []

    def op(self, eng, fn, reads=(), writes=()):
        self._wait(eng, self._deps(reads, writes))
        ins = fn(eng.e)
        eng.count += 1
        ins.then_inc(self.sems[eng.sem], 1)
        ev = (eng.sem, eng.count)
        self._commit(ev, reads, writes)
        return ev

    def dma(self, out_ap, in_ap, reads=(), writes=(), q=None, **kw):
        q = q or self.sp
        self._wait(q, self._deps(reads, writes))
        s = self.dma_pool[self.dma_rr % len(self.dma_pool)]
        self.dma_rr += 1
        ins = q.e.dma_start(out=out_ap, in_=in_ap, **kw)
        ins.then_inc(self.sems[s], 16)
        self.dma_tot[s] += 16
        ev = (s, self.dma_tot[s])
        self._commit(ev, reads, writes)
        return ev

    def finish(self, tiles):
        deps = []
        for t in tiles:
            if t.lw is not None:
                deps.append(t.lw)
            deps.extend(t.rd)
        self._wait(self.sp, deps)

    def close(self):
        self.ctx.close()


D = 2048
DFF = 8192
KC = 16
TT = 512
NTOK = 2048
SEQ = 8192
MEMW = 512
MIXW = 1536
EPS = 1e-6
SCALE = 128 ** -0.5


class Ring:
    def __init__(self, tiles):
        self.t = tiles
        self.i = 0

    def next(self):
        t = self.t[self.i % len(self.t)]
        self.i += 1
        return t


def make_consts(k):
    c = {}
    c["ones_f"] = k.sb("ones_f", [128, 128], F32)
    c["ones_b"] = k.sb("ones_b", [128, 128], BF16)
    c["ident"] = k.sb("ident", [128, 128], F32)
    c["eps"] = k.sb("eps_c", [128, 1], F32)
    k.op(k.dve, lambda e: e.memset(c["eps"].ap, EPS), [], [c["eps"]])
    k.op(k.dve, lambda e: e.memset(c["ones_f"].ap, 1.0), [], [c["ones_f"]])
    k.op(k.dve, lambda e: e.memset(c["ones_b"].ap, 1.0), [], [c["ones_b"]])
    k.op(k.pool, lambda e: e.memset(c["ident"].ap, 1.0), [], [c["ident"]])
    k.op(k.pool, lambda e: e.affine_select(out=c["ident"].ap, in_=c["ident"].ap, pattern=[[-1, 128]],
                                           compare_op=ALU.is_equal, fill=0.0, base=0, channel_multiplier=1),
         [c["ident"]], [c["ident"]])
    c["ident_b"] = k.sb("ident_b", [128, 128], BF16)
    k.op(k.dve, lambda e: e.tensor_copy(c["ident_b"].ap, c["ident"].ap), [c["ident"]], [c["ident_b"]])
    return c


def linear(k, psr, wring, xT, kcs, w_ap, chunks, cb, ntok=TT, xsl=None):
    groups = []
    cur = []
    for i, (c0, wd) in enumerate(chunks):
        if cur and (len(cur) == 4 or chunks[cur[-1]][0] + chunks[cur[-1]][1] != c0):
            groups.append(cur)
            cur = []
        cur.append(i)
    if cur:
        groups.append(cur)
    KB = 4
    for g in groups:
        g0 = chunks[g[0]][0]
        g1 = chunks[g[-1]][0] + chunks[g[-1]][1]
        ncol = g1 - g0
        pst = [psr.next() for _ in g]
        for kb in range(0, kcs, KB):
            nk = min(KB, kcs - kb)
            slab = wring.next()
            k.dma(slab[:, 0:nk, 0:ncol],
                  w_ap[kb * 128:(kb + nk) * 128, g0:g1].rearrange("(kc p) n -> p kc n", p=128),
                  writes=[slab], q=k.pool)
            for kk in range(nk):
                kc = kb + kk
                for j, ci in enumerate(g):
                    c0, wd = chunks[ci]
                    rhs = xT[:, kc, 0:ntok] if xsl is None else xsl(kc)
                    k.op(k.pe, lambda e, j=j, kk=kk, c0=c0, wd=wd, rhs=rhs, kc=kc: e.matmul(
                        pst[j][0:wd, 0:ntok], lhsT=slab[:, kk, c0 - g0:c0 - g0 + wd], rhs=rhs,
                        start=(kc == 0), stop=(kc == kcs - 1)), [slab, xT], [pst[j]])
        for j, ci in enumerate(g):
            cb(ci, pst[j], chunks[ci][1])


def rmsnorm_T(k, c, psr, hT, gainT, outT, sqring, rstd, ntok=TT, kcs=KC, dmodel=D):
    ps = psr.next()
    for kc in range(kcs):
        sq = sqring.next()
        k.op(k.act, lambda e, kc=kc, sq=sq: e.activation(out=sq[:, 0:ntok], in_=hT[:, kc, 0:ntok], func=AF.Square),
             [hT], [sq])
        k.op(k.pe, lambda e, kc=kc, sq=sq: e.matmul(ps[:, 0:ntok], lhsT=c["ones_f"].ap, rhs=sq[:, 0:ntok],
                                                    start=(kc == 0), stop=(kc == kcs - 1)), [sq, c["ones_f"]], [ps])
    k.op(k.act, lambda e: e.activation(out=rstd[:, 0:ntok], in_=ps[:, 0:ntok], func=AF.Sqrt,
                                       bias=c["eps"].ap, scale=1.0 / dmodel), [ps, c["eps"]], [rstd])
    k.op(k.dve, lambda e: e.reciprocal(rstd[:, 0:ntok], rstd[:, 0:ntok]), [rstd], [rstd])
    for kc in range(kcs):
        k.op(k.dve, lambda e, kc=kc: e.scalar_tensor_tensor(out=outT[:, kc, 0:ntok], in0=hT[:, kc, 0:ntok],
                                                            scalar=gainT[:, kc:kc + 1], in1=rstd[:, 0:ntok],
                                                            op0=ALU.mult, op1=ALU.mult), [hT, gainT, rstd], [outT])


def load_vec_T(k, name, v_ap, n):
    t = k.sb(name + "_sb", [128, n // 128], F32)
    with k.nc.allow_non_contiguous_dma(reason="small param vector"):
        k.dma(t.ap, v_ap.rearrange("(kc p) -> p kc", p=128), writes=[t])
    return t


def in_chunks(kind):
    if kind == "s5":
        nm, small = 12, 0
    elif kind == "gdn":
        nm, small = 48, 24
    else:
        nm, small = 36, 12
    ch = [(i * 128, 128) for i in range(nm)]
    if small:
        ch.append((nm * 128, small))
    base = nm * 128 + small
    ch += [(base + i * 128, 128) for i in range(4)]
    return ch, nm, small


def build_tok(prev, nxt, final):
    nc = bass.Bass("TRN2", target_bir_lowering=False)
    k = K(nc)
    dt = nc.dram_tensor
    hT_d = dt("hT", [D, NTOK], F32, kind="ExternalInput").ap()
    if prev:
        mixT_d = dt("mixT", [D, NTOK], F32, kind="ExternalInput").ap()
        if prev == "s5":
            w_glu_d = dt("w_glu", [MIXW, MIXW], F32, kind="ExternalInput").ap()
            b_glu_d = dt("b_glu", [MIXW], F32, kind="ExternalInput").ap()
        w_out_d = dt("w_out", [D, D], F32, kind="ExternalInput").ap()
        norm2_d = dt("norm2", [D], F32, kind="ExternalInput").ap()
        w_up_d = dt("w_up", [D, DFF], F32, kind="ExternalInput").ap()
        w_down_d = dt("w_down", [DFF, D], F32, kind="ExternalInput").ap()
    if nxt:
        chunks, nm, small = in_chunks(nxt)
        win = chunks[-1][0] + 128
        norm1_d = dt("norm1", [D], F32, kind="ExternalInput").ap()
        w_in_d = dt("w_in", [D, win], F32, kind="ExternalInput").ap()
        memT_d = dt("memT", [D, 256], F32, kind="ExternalInput").ap()
        mem_norm_d = dt("mem_norm", [D], F32, kind="ExternalInput").ap()
        w_kv_d = dt("w_mem_kv", [D, 1024], F32, kind="ExternalInput").ap()
        hT_o = dt("hT_out", [D, NTOK], F32, kind="ExternalOutput").ap()
        projT_o = dt("projT", [nm * 128, NTOK], F32, kind="ExternalOutput").ap()
        readT_o = dt("readT", [MEMW, NTOK], F32, kind="ExternalOutput").ap()
        if small:
            smallT_o = dt("smallT", [small, NTOK], F32, kind="ExternalOutput").ap()
    if final:
        norm_f_d = dt("norm_f", [D], F32, kind="ExternalInput").ap()
        outT_o = dt("outT", [D, NTOK], F32, kind="ExternalOutput").ap()

    c = make_consts(k)
    hT = k.sb("hTt", [128, KC, TT], F32)
    aT = k.sb("aT", [128, KC, TT], BF16)
    gT = k.sb("gT", [128, 64, TT], BF16)
    mixT = k.sb("mixTt", [128, KC, TT], BF16)
    rstd = k.sb("rstd", [128, TT], F32)
    sqring = Ring([k.sb("sq%d" % i, [128, TT], F32) for i in range(2)])
    string = Ring([k.sb("stg%d" % i, [128, TT], F32) for i in range(4)])
    wring = Ring([k.sb("wsl%d" % i, [128, 4, 512], BF16) for i in range(6)])
    psr = Ring([k.ps("ps%d" % i, [128, 512]) for i in range(8)])
    outs = []

    if prev:
        norm2 = load_vec_T(k, "norm2", norm2_d, D)
        if prev == "s5":
            b_glu = load_vec_T(k, "b_glu", b_glu_d, MIXW)
            y32 = gT.ap.bitcast(F32)
    if final:
        norm_f = load_vec_T(k, "norm_f", norm_f_d, D)
    if nxt:
        norm1 = load_vec_T(k, "norm1", norm1_d, D)
        mem_norm = load_vec_T(k, "mem_norm", mem_norm_d, D)
        memkT = k.sb("memkT", [128, 4, 256], BF16)
        memv = k.sb("memv", [128, 2, 512], BF16)
        qmT = k.sb("qmT", [128, 4, TT], BF16)
        pT = [k.sb("pT%d" % i, [128, TT], BF16) for i in range(2)]
        rdT = k.sb("rdT", [128, 4, TT], F32)
        k.dma(hT[:, :, 0:256], memT_d.rearrange("(kc p) n -> p kc n", p=128), writes=[hT])
        rmsnorm_T(k, c, psr, hT, mem_norm, aT, sqring, rstd, ntok=256)

        def cb_k(i, ps, wd):
            k.op(k.act, lambda e: e.copy(memkT[:, i, :], ps[:, 0:256]), [ps], [memkT])
        linear(k, psr, wring, aT, KC, w_kv_d, [(i * 128, 128) for i in range(4)], cb_k, ntok=256)
        for mb in range(2):
            ps = psr.next()
            for kb in range(0, KC, 4):
                slab = wring.next()
                k.dma(slab[:, :, :], w_kv_d[kb * 128:(kb + 4) * 128, 512:1024].rearrange("(kc p) n -> p kc n", p=128),
                      writes=[slab], q=k.pool)
                for kk in range(4):
                    kc = kb + kk
                    k.op(k.pe, lambda e, kc=kc, kk=kk, slab=slab: e.matmul(
                        ps.ap, lhsT=aT[:, kc, mb * 128:(mb + 1) * 128], rhs=slab[:, kk, :],
                        start=(kc == 0), stop=(kc == KC - 1)), [aT, slab], [ps])
            k.op(k.act, lambda e, mb=mb, ps=ps: e.copy(memv[:, mb, :], ps.ap), [ps], [memv])

    for tt in range(NTOK // TT):
        t0 = tt * TT
        k.dma(hT.ap, hT_d[:, t0:t0 + TT].rearrange("(kc p) n -> p kc n", p=128), writes=[hT])
        if prev:
            if prev == "s5":
                def yv(oc):
                    return y32[:, 2 * oc:2 * oc + 2, :].rearrange("p a b -> p (a b)")
                for oc in range(12):
                    k.dma(yv(oc), mixT_d[oc * 128:(oc + 1) * 128, t0:t0 + TT], writes=[gT])
                k.dma(aT[:, 0:12, :], mixT_d[0:MIXW, t0:t0 + TT].rearrange("(kc p) n -> p kc n", p=128), writes=[aT], q=k.pool)
                k.dma(mixT[:, 12:16, :], mixT_d[MIXW:D, t0:t0 + TT].rearrange("(kc p) n -> p kc n", p=128), writes=[mixT], q=k.pool)

                def cb_glu(i, ps, wd):
                    st = string.next()
                    k.op(k.act, lambda e: e.activation(out=st.ap, in_=ps.ap, func=AF.Sigmoid, bias=b_glu[:, i:i + 1], scale=1.0),
                         [ps, b_glu], [st])
                    k.op(k.dve, lambda e: e.tensor_tensor(mixT[:, i, :], yv(i), st.ap, ALU.mult), [gT, st], [mixT])
                linear(k, psr, wring, aT, 12, w_glu_d, [(i * 128, 128) for i in range(12)], cb_glu)
            else:
                k.dma(mixT.ap, mixT_d[:, t0:t0 + TT].rearrange("(kc p) n -> p kc n", p=128), writes=[mixT], q=k.pool)

            def cb_res(i, ps, wd):
                k.op(k.dve, lambda e: e.tensor_tensor(hT[:, i, :], hT[:, i, :], ps.ap, ALU.add), [hT, ps], [hT])
            linear(k, psr, wring, mixT, KC, w_out_d, [(i * 128, 128) for i in range(KC)], cb_res)
            rmsnorm_T(k, c, psr, hT, norm2, aT, sqring, rstd)

            def cb_up(i, ps, wd):
                st = string.next()
                k.op(k.act, lambda e: e.activation(out=st.ap, in_=ps.ap, func=AF.Relu), [ps], [st])
                k.op(k.dve, lambda e: e.tensor_tensor(gT[:, i, :], st.ap, st.ap, ALU.mult), [st], [gT])
            linear(k, psr, wring, aT, KC, w_up_d, [(i * 128, 128) for i in range(64)], cb_up)
            linear(k, psr, wring, gT, 64, w_down_d, [(i * 128, 128) for i in range(KC)], cb_res)
        if nxt:
            ho = Tl(hT_o, "hT_o")
            k.dma(hT_o[:, t0:t0 + TT].rearrange("(kc p) n -> p kc n", p=128), hT.ap, reads=[hT], writes=[ho])
            outs.append(ho)
            rmsnorm_T(k, c, psr, hT, norm1, aT, sqring, rstd)

            def cb_in(i, ps, wd):
                if i < nm:
                    st = string.next()
                    k.op(k.act, lambda e: e.copy(st.ap, ps.ap), [ps], [st])
                    o = Tl(projT_o, "projT_o")
                    k.dma(projT_o[i * 128:(i + 1) * 128, t0:t0 + TT], st.ap, reads=[st], writes=[o])
                    outs.append(o)
                elif small and i == nm:
                    st = string.next()
                    k.op(k.act, lambda e: e.copy(st[0:wd, :], ps[0:wd, :]), [ps], [st])
                    o = Tl(smallT_o, "smallT_o")
                    k.dma(smallT_o[:, t0:t0 + TT], st[0:wd, :], reads=[st], writes=[o])
                    outs.append(o)
                else:
                    j = i - nm - (1 if small else 0)
                    k.op(k.act, lambda e: e.copy(qmT[:, j, :], ps.ap), [ps], [qmT])
            linear(k, psr, wring, aT, KC, w_in_d, chunks, cb_in)
            for hm in range(4):
                for mb in range(2):
                    ps = psr.next()
                    k.op(k.pe, lambda e, ps=ps, mb=mb: e.matmul(ps.ap, lhsT=memkT[:, hm, mb * 128:(mb + 1) * 128],
                                                                rhs=qmT[:, hm, :], start=True, stop=True),
                         [memkT, qmT], [ps])
                    k.op(k.act, lambda e, ps=ps, mb=mb: e.activation(out=pT[mb].ap, in_=ps.ap, func=AF.Exp, scale=SCALE),
                         [ps], [pT[mb]])
                pso = psr.next()
                psz = psr.next()
                for mb in range(2):
                    k.op(k.pe, lambda e, mb=mb: e.matmul(pso.ap, lhsT=memv[:, mb, hm * 128:(hm + 1) * 128], rhs=pT[mb].ap,
                                                         start=(mb == 0), stop=(mb == 1)), [memv, pT[mb]], [pso])
                for mb in range(2):
                    k.op(k.pe, lambda e, mb=mb: e.matmul(psz.ap, lhsT=c["ones_b"].ap, rhs=pT[mb].ap,
                                                         start=(mb == 0), stop=(mb == 1)), [c["ones_b"], pT[mb]], [psz])
                k.op(k.dve, lambda e: e.reciprocal(rstd.ap, psz.ap), [psz], [rstd])
                k.op(k.dve, lambda e: e.tensor_tensor(rdT[:, hm, :], pso.ap, rstd.ap, ALU.mult), [pso, rstd], [rdT])
            o = Tl(readT_o, "readT_o")
            k.dma(readT_o[:, t0:t0 + TT].rearrange("(kc p) n -> p kc n", p=128), rdT.ap, reads=[rdT], writes=[o])
            outs.append(o)
        if final:
            ps = psr.next()
            for kc in range(KC):
                sq = sqring.next()
                k.op(k.act, lambda e, kc=kc, sq=sq: e.activation(out=sq.ap, in_=hT[:, kc, :], func=AF.Square), [hT], [sq])
                k.op(k.pe, lambda e, kc=kc, sq=sq: e.matmul(ps.ap, lhsT=c["ones_f"].ap, rhs=sq.ap, start=(kc == 0),
                                                            stop=(kc == KC - 1)), [sq, c["ones_f"]], [ps])
            k.op(k.act, lambda e: e.activation(out=rstd.ap, in_=ps.ap, func=AF.Sqrt, bias=c["eps"].ap, scale=1.0 / D),
                 [ps, c["eps"]], [rstd])
            k.op(k.dve, lambda e: e.reciprocal(rstd.ap, rstd.ap), [rstd], [rstd])
            for kc in range(KC):
                st = string.next()
                k.op(k.dve, lambda e, kc=kc, st=st: e.scalar_tensor_tensor(out=st.ap, in0=hT[:, kc, :],
                                                                          scalar=norm_f[:, kc:kc + 1], in1=rstd.ap,
                                                                          op0=ALU.mult, op1=ALU.mult),
                     [hT, norm_f, rstd], [st])
                o = Tl(outT_o, "outT_o")
                k.dma(outT_o[kc * 128:(kc + 1) * 128, t0:t0 + TT], st.ap, reads=[st], writes=[o])
                outs.append(o)
    k.finish(outs)
    k.close()
    return nc


NEGBIG = -30000.0


def build_fox(nh=3, seq=SEQ):
    nc = bass.Bass("TRN2", target_bir_lowering=False)
    k = K(nc)
    dt = nc.dram_tensor
    qT_d = dt("qT", [nh, 128, seq], F32, kind="ExternalInput").ap()
    kT_d = dt("kT", [nh, 128, seq], F32, kind="ExternalInput").ap()
    v_d = dt("v", [nh, seq, 128], F32, kind="ExternalInput").ap()
    fl_d = dt("fl", [nh, seq], F32, kind="ExternalInput").ap()
    bf_d = dt("b_f", [nh], F32, kind="ExternalInput").ap()
    oT_o = dt("oT", [nh, 128, seq], F32, kind="ExternalOutput").ap()
    nb = seq // 128
    nqt = seq // TT
    c = make_consts(k)
    qTb = k.sb("qTb", [128, seq], BF16)
    kTb = k.sb("kTb", [128, seq], BF16)
    vb = k.sb("vb", [128, nb, 128], BF16)
    rowA = k.sb("rowA", [1, seq], F32)
    rowB = k.sb("rowB", [1, seq], F32)
    nbf = k.sb("nbf", [1, 4], F32)
    ncfT = k.sb("ncfT", [128, nb], F32)
    cfq = k.sb("cfq", [128, TT], F32)
    cfqm = [k.sb("cfqm%d" % i, [128, TT], F32) for i in range(4)]
    neg = [k.sb("neg%d" % i, [128, TT], F32) for i in range(4)]
    ering = Ring([k.sb("ein%d" % i, [128, TT], F32) for i in range(3)])
    pring = Ring([k.sb("pT%d" % i, [128, TT], BF16) for i in range(3)])
    oring = Ring([k.sb("ost%d" % i, [128, TT], F32) for i in range(2)])
    rz = k.sb("rz", [128, TT], F32)
    psS = Ring([k.ps("psS%d" % i, [128, TT]) for i in range(4)])
    psO = Ring([k.ps("psO%d" % i, [128, TT]) for i in range(2)])
    psZ = Ring([k.ps("psZ%d" % i, [128, TT]) for i in range(2)])
    outs = []
    for d in range(4):
        k.op(k.pool, lambda e, d=d: e.memset(neg[d].ap, 0.0), [], [neg[d]])
        k.op(k.pool, lambda e, d=d: e.affine_select(out=neg[d].ap, in_=neg[d].ap, pattern=[[1, TT]],
                                                    compare_op=ALU.is_ge, fill=NEGBIG, base=-d * 128,
                                                    channel_multiplier=-1), [neg[d]], [neg[d]])
    for h in range(nh):
        k.dma(qTb.ap, qT_d[h], writes=[qTb], q=k.pool)
        k.dma(kTb.ap, kT_d[h], writes=[kTb], q=k.pool)
        k.dma(vb.ap, v_d[h].rearrange("(j p) d -> p j d", p=128), writes=[vb], q=k.pool)
        k.dma(rowA.ap, fl_d[h:h + 1, :], writes=[rowA])
        k.dma(nbf[0:1, 0:1], bf_d[h:h + 1].rearrange("(a b) -> a b", a=1), writes=[nbf])
        k.op(k.dve, lambda e: e.tensor_scalar(nbf[0:1, 1:2], nbf[0:1, 0:1], -1.0, None, ALU.mult), [nbf], [nbf])
        k.op(k.act, lambda e: e.activation(out=rowB.ap, in_=rowA.ap, func=AF.Exp, bias=nbf[0:1, 1:2], scale=-1.0),
             [rowA, nbf], [rowB])
        k.op(k.act, lambda e: e.activation(out=rowB.ap, in_=rowB.ap, func=AF.Ln, bias=c["ones_f"][0:1, 0:1], scale=1.0),
             [rowB, c["ones_f"]], [rowB])
        k.op(k.dve, lambda e: e.tensor_tensor_scan(rowA.ap, c["ones_f"][0:1, 0:1].broadcast_to([1, seq]), rowB.ap, 0.0,
                                                   ALU.mult, ALU.subtract), [rowB, c["ones_f"]], [rowA])
        ps = psS.next()
        for j in range(nb):
            k.op(k.pe, lambda e, j=j: e.matmul(ps[:, j:j + 1], lhsT=rowA[0:1, j * 128:(j + 1) * 128],
                                               rhs=c["ones_f"][0:1, 0:1], start=True, stop=True),
                 [rowA, c["ones_f"]], [ps])
        k.op(k.dve, lambda e: e.tensor_scalar(ncfT.ap, ps[:, 0:nb], -1.0, None, ALU.mult), [ps], [ncfT])
        for qt in range(nqt):
            q0 = qt * TT
            ps = psS.next()
            k.op(k.pe, lambda e: e.matmul(ps.ap, lhsT=c["ones_f"][0:1, :], rhs=rowA[0:1, q0:q0 + TT], start=True, stop=True),
                 [rowA, c["ones_f"]], [ps])
            k.op(k.act, lambda e: e.copy(cfq.ap, ps.ap), [ps], [cfq])
            for d in range(4):
                k.op(k.pool, lambda e, d=d: e.tensor_tensor(cfqm[d].ap, cfq.ap, neg[d].ap, ALU.add), [cfq, neg[d]], [cfqm[d]])
            po = psO.next()
            pz = psZ.next()
            njb = 4 * (qt + 1)
            for j in range(njb):
                ps = psS.next()
                k.op(k.pe, lambda e, j=j, ps=ps: e.matmul(ps.ap, lhsT=kTb[:, j * 128:(j + 1) * 128], rhs=qTb[:, q0:q0 + TT],
                                                          start=True, stop=True), [kTb, qTb], [ps])
                add = cfq if j < 4 * qt else cfqm[j - 4 * qt]
                ein = ering.next()
                k.op(k.dve, lambda e, ps=ps, add=add, ein=ein: e.scalar_tensor_tensor(
                    out=ein.ap, in0=ps.ap, scalar=SCALE, in1=add.ap, op0=ALU.mult, op1=ALU.add), [ps, add], [ein])
                pT = pring.next()
                k.op(k.act, lambda e, j=j, ein=ein, pT=pT: e.activation(out=pT.ap, in_=ein.ap, func=AF.Exp,
                                                                       bias=ncfT[:, j:j + 1], scale=1.0),
                     [ein, ncfT], [pT])
                k.op(k.pe, lambda e, j=j, pT=pT: e.matmul(po.ap, lhsT=vb[:, j, :], rhs=pT.ap, start=(j == 0),
                                                          stop=(j == njb - 1)), [vb, pT], [po])
                k.op(k.pe, lambda e, j=j, pT=pT: e.matmul(pz.ap, lhsT=c["ones_b"].ap, rhs=pT.ap, start=(j == 0),
                                                          stop=(j == njb - 1)), [c["ones_b"], pT], [pz])
            k.op(k.dve, lambda e: e.reciprocal(rz.ap, pz.ap), [pz], [rz])
            ost = oring.next()
            k.op(k.dve, lambda e, ost=ost: e.tensor_tensor(ost.ap, po.ap, rz.ap, ALU.mult), [po, rz], [ost])
            o = Tl(oT_o, "oT_o")
            k.dma(oT_o[h, :, q0:q0 + TT], ost.ap, reads=[ost], writes=[o])
            outs.append(o)
    k.finish(outs)
    k.close()
    return nc


CL = 64


def build_gdn(nh=3, seq=SEQ):
    nc = bass.Bass("TRN2", target_bir_lowering=False)
    k = K(nc)
    dt = nc.dram_tensor
    nch = seq // CL
    ngrp = seq // TT
    qkvT_d = dt("qkvT", [nh, 3, 128, seq], F32, kind="ExternalInput").ap()
    gateT_d = dt("gateT", [nh, 128, seq], F32, kind="ExternalInput").ap()
    cw_d = dt("cw", [nh, 3, 128, 4], F32, kind="ExternalInput").ap()
    abrow_d = dt("abrow", [nh, 2, seq], F32, kind="ExternalInput").ap()
    abcol_d = dt("abcol", [nh, 2, CL, nch], F32, kind="ExternalInput").ap()
    hp_d = dt("hp", [nh, 2], F32, kind="ExternalInput").ap()
    onorm_d = dt("o_norm", [128], F32, kind="ExternalInput").ap()
    oT_o = dt("oT", [nh, 128, seq], F32, kind="ExternalOutput").ap()

    c = make_consts(k)
    ident = c["ident"]
    raw = k.sb("raw", [128, seq], F32)
    acc = k.sb("acc", [128, seq], F32)
    rows = k.sb("rows", [65, seq], F32)
    qT = k.sb("qT", [128, seq], BF16)
    kT = k.sb("kT", [128, seq], BF16)
    vT = k.sb("vT", [128, seq], BF16)
    cw = k.sb("cw_sb", [128, 3, 4], F32)
    hp = k.sb("hp_sb", [128, 4], F32)
    onorm = k.sb("onorm_sb", [128, 1], F32)
    acol = k.sb("acol", [CL, nch], F32)
    bcol = k.sb("bcol", [CL, nch], F32)
    gccol = k.sb("gccol", [CL, nch], F32)
    egcol = k.sb("egcol", [CL, nch], F32)
    edcol = k.sb("edcol", [CL, nch], F32)
    glast = k.sb("glast", [128, nch], F32)
    triU = k.sb("triU", [CL, CL], F32)
    maskU = k.sb("maskU", [CL, CL], F32)
    maskL = k.sb("maskL", [CL, CL], F32)
    sU01 = k.sb("sU01", [CL, CL], F32)
    S = k.sb("S", [128, 128], F32)
    Sb = k.sb("Sb", [128, 128], BF16)
    sq = Ring([k.sb("sq%d" % i, [128, TT], F32) for i in range(2)])
    rstd = k.sb("rstd", [128, TT], F32)
    qdG = k.sb("qdG", [128, TT], BF16)
    kbG = k.sb("kbG", [128, TT], BF16)
    GBs = k.sb("GBs", [CL, TT], F32)
    gst = Ring([k.sb("gst%d" % i, [128, TT], F32) for i in range(2)])
    ost = Ring([k.sb("ost%d" % i, [128, TT], F32) for i in range(2)])

    def small(name, shape, dtp, n=2):
        return Ring([k.sb("%s%d" % (name, i), shape, dtp) for i in range(n)])
    vb_r = small("vb", [CL, 128], BF16)
    kbd_r = small("kbd", [CL, 128], BF16)
    kdec_r = small("kdec", [CL, 128], BF16)
    tmp_r = small("tmp", [CL, CL], F32, 4)
    dec_r = small("dec", [CL, CL], F32)
    decT_r = small("decT", [CL, CL], F32)
    decTs_r = small("decTs", [CL, CL], F32)
    P_r = small("P", [CL, CL], F32, 3)
    Pt_r = small("Pt", [CL, CL], F32, 3)
    Tt_r = small("Tt", [CL, CL], F32, 3)
    Ttb_r = small("Ttb", [CL, CL], BF16)
    attT_r = small("attT", [CL, CL], BF16)
    nwT_r = small("nwT", [128, CL], BF16)
    vnew_r = small("vnew", [CL, 128], BF16)
    psA = Ring([k.ps("psA%d" % i, [128, TT]) for i in range(2)])
    psB = Ring([k.ps("psB%d" % i, [128, 128]) for i in range(4)])
    psT = Ring([k.ps("psT%d" % i, [CL, 128], BF16) for i in range(2)])
    outs = []

    k.op(k.pool, lambda e: e.memset(triU.ap, 1.0), [], [triU])
    k.op(k.pool, lambda e: e.affine_select(out=triU.ap, in_=triU.ap, pattern=[[1, CL]], compare_op=ALU.is_ge, fill=0.0,
                                           base=0, channel_multiplier=-1), [triU], [triU])
    k.op(k.pool, lambda e: e.memset(maskU.ap, 0.0), [], [maskU])
    k.op(k.pool, lambda e: e.affine_select(out=maskU.ap, in_=maskU.ap, pattern=[[1, CL]], compare_op=ALU.is_ge,
                                           fill=NEGBIG, base=0, channel_multiplier=-1), [maskU], [maskU])
    k.op(k.pool, lambda e: e.memset(maskL.ap, 0.0), [], [maskL])
    k.op(k.pool, lambda e: e.affine_select(out=maskL.ap, in_=maskL.ap, pattern=[[-1, CL]], compare_op=ALU.is_ge,
                                           fill=-NEGBIG, base=-1, channel_multiplier=1), [maskL], [maskL])
    k.op(k.pool, lambda e: e.memset(sU01.ap, 1.0), [], [sU01])
    k.op(k.pool, lambda e: e.affine_select(out=sU01.ap, in_=sU01.ap, pattern=[[1, CL]], compare_op=ALU.is_ge, fill=0.0,
                                           base=-1, channel_multiplier=-1), [sU01], [sU01])
    with nc.allow_non_contiguous_dma(reason="small params"):
        k.dma(onorm.ap, onorm_d.rearrange("(p a) -> p a", a=1), writes=[onorm])

    for h in range(nh):
        with nc.allow_non_contiguous_dma(reason="small params"):
            k.dma(cw.ap, cw_d[h].rearrange("t p j -> p t j"), writes=[cw])
            k.dma(hp[:, 0:2], hp_d[h:h + 1, :].partition_broadcast(128), writes=[hp])
        k.op(k.act, lambda e: e.activation(out=hp[:, 2:3], in_=hp[:, 0:1], func=AF.Exp), [hp], [hp])
        k.op(k.dve, lambda e: e.tensor_scalar(hp[:, 2:3], hp[:, 2:3], -1.0, None, ALU.mult), [hp], [hp])
        k.dma(acol.ap, abcol_d[h, 0], writes=[acol])
        k.dma(bcol.ap, abcol_d[h, 1], writes=[bcol])
        k.op(k.act, lambda e: e.activation(out=bcol.ap, in_=bcol.ap, func=AF.Sigmoid), [bcol], [bcol])
        k.op(k.act, lambda e: e.activation(out=acol.ap, in_=acol.ap, func=AF.Exp, bias=hp[0:CL, 1:2], scale=1.0), [acol, hp], [acol])
        k.op(k.act, lambda e: e.activation(out=acol.ap, in_=acol.ap, func=AF.Ln, bias=c["ones_f"][0:CL, 0:1], scale=1.0),
             [acol, c["ones_f"]], [acol])
        k.op(k.dve, lambda e: e.tensor_scalar(acol.ap, acol.ap, hp[0:CL, 2:3], None, ALU.mult), [acol, hp], [acol])
        ps = psB.next()
        k.op(k.pe, lambda e: e.matmul(ps[0:CL, 0:nch], lhsT=triU.ap, rhs=acol.ap, start=True, stop=True), [triU, acol], [ps])
        k.op(k.dve, lambda e: e.tensor_copy(gccol.ap, ps[0:CL, 0:nch]), [ps], [gccol])
        ps2 = psB.next()
        k.op(k.pe, lambda e: e.matmul(ps2[:, 0:nch], lhsT=c["ones_f"][0:CL, :], rhs=acol.ap, start=True, stop=True),
             [c["ones_f"], acol], [ps2])
        k.op(k.act, lambda e: e.activation(out=glast.ap, in_=ps2[:, 0:nch], func=AF.Exp), [ps2], [glast])
        k.op(k.dve, lambda e: e.tensor_tensor(edcol.ap, ps2[0:CL, 0:nch], gccol.ap, ALU.subtract), [ps2, gccol], [edcol])
        k.op(k.act, lambda e: e.activation(out=edcol.ap, in_=edcol.ap, func=AF.Exp), [edcol], [edcol])
        k.op(k.act, lambda e: e.activation(out=egcol.ap, in_=gccol.ap, func=AF.Exp), [gccol], [egcol])
        k.op(k.dve, lambda e: e.tensor_tensor(egcol.ap, egcol.ap, bcol.ap, ALU.mult), [egcol, bcol], [egcol])
        k.dma(rows[32:33, :], abrow_d[h, 0:1, :], writes=[rows])
        k.dma(rows[0:1, :], abrow_d[h, 1:2, :], writes=[rows])
        k.op(k.act, lambda e: e.activation(out=rows[0:1, :], in_=rows[0:1, :], func=AF.Sigmoid), [rows], [rows])
        k.op(k.act, lambda e: e.activation(out=rows[32:33, :], in_=rows[32:33, :], func=AF.Exp, bias=hp[32:33, 1:2], scale=1.0),
             [rows, hp], [rows])
        k.op(k.act, lambda e: e.activation(out=rows[32:33, :], in_=rows[32:33, :], func=AF.Ln, bias=c["ones_f"][32:33, 0:1],
                                           scale=1.0), [rows, c["ones_f"]], [rows])
        k.op(k.dve, lambda e: e.tensor_scalar(rows[32:33, :], rows[32:33, :], hp[32:33, 2:3], None, ALU.mult), [rows, hp], [rows])
        k.op(k.pool, lambda e: e.memset(raw[32:33, :], 1.0), [raw], [raw])
        k.op(k.pool, lambda e: e.memset(raw[32:33, :].rearrange("p (n c) -> p n c", c=CL)[:, :, 0:1], 0.0), [raw], [raw])
        k.op(k.dve, lambda e: e.tensor_tensor_scan(rows[32:33, :], raw[32:33, :], rows[32:33, :], 0.0, ALU.mult, ALU.add),
             [rows, raw], [rows])
        k.op(k.act, lambda e: e.activation(out=rows[64:65, :], in_=rows[32:33, :], func=AF.Exp), [rows], [rows])
        for ti, dst in enumerate((qT, kT, vT)):
            k.dma(raw.ap, qkvT_d[h, ti], writes=[raw])
            k.op(k.dve, lambda e: e.tensor_scalar(acc.ap, raw.ap, cw[:, ti, 3:4], None, ALU.mult), [raw, cw], [acc])
            for s in (1, 2, 3):
                k.op(k.dve, lambda e, s=s: e.scalar_tensor_tensor(out=acc[:, s:seq], in0=raw[:, 0:seq - s],
                                                                 scalar=cw[:, ti, 3 - s:4 - s], in1=acc[:, s:seq],
                                                                 op0=ALU.mult, op1=ALU.add), [raw, cw, acc], [acc])
            if ti == 2:
                k.op(k.act, lambda e: e.activation(out=vT.ap, in_=acc.ap, func=AF.Silu), [acc], [vT])
                continue
            k.op(k.act, lambda e: e.activation(out=acc.ap, in_=acc.ap, func=AF.Silu), [acc], [acc])
            for g in range(ngrp):
                sl = slice(g * TT, (g + 1) * TT)
                s_ = sq.next()
                k.op(k.act, lambda e, s_=s_, sl=sl: e.activation(out=s_.ap, in_=acc[:, sl], func=AF.Square), [acc], [s_])
                ps = psA.next()
                k.op(k.pe, lambda e, s_=s_, ps=ps: e.matmul(ps.ap, lhsT=c["ones_f"].ap, rhs=s_.ap, start=True, stop=True),
                     [s_, c["ones_f"]], [ps])
                k.op(k.act, lambda e, ps=ps: e.activation(out=rstd.ap, in_=ps.ap, func=AF.Sqrt, bias=c["eps"].ap, scale=1.0),
                     [ps, c["eps"]], [rstd])
                k.op(k.dve, lambda e: e.reciprocal(rstd.ap, rstd.ap), [rstd], [rstd])
                if ti == 0:
                    k.op(k.dve, lambda e, sl=sl: e.scalar_tensor_tensor(out=qT[:, sl], in0=acc[:, sl], scalar=SCALE, in1=rstd.ap,
                                                                       op0=ALU.mult, op1=ALU.mult), [acc, rstd], [qT])
                else:
                    k.op(k.dve, lambda e, sl=sl: e.tensor_tensor(kT[:, sl], acc[:, sl], rstd.ap, ALU.mult), [acc, rstd], [kT])
        k.op(k.dve, lambda e: e.memset(S.ap, 0.0), [], [S])
        k.op(k.act, lambda e: e.copy(Sb.ap, S.ap), [S], [Sb])
        for g in range(ngrp):
            sl = slice(g * TT, (g + 1) * TT)
            ps = psA.next()
            k.op(k.pe, lambda e, ps=ps: e.matmul(ps.ap, lhsT=c["ones_f"][64:65, :], rhs=rows[64:65, sl], start=True, stop=True),
                 [rows, c["ones_f"]], [ps])
            k.op(k.dve, lambda e, ps=ps: e.tensor_tensor(qdG.ap, qT[:, sl], ps.ap, ALU.mult), [qT, ps], [qdG])
            ps = psA.next()
            k.op(k.pe, lambda e, ps=ps: e.matmul(ps.ap, lhsT=c["ones_f"][0:1, :], rhs=rows[0:1, sl], start=True, stop=True),
                 [rows, c["ones_f"]], [ps])
            k.op(k.dve, lambda e, ps=ps: e.tensor_tensor(kbG.ap, kT[:, sl], ps.ap, ALU.mult), [kT, ps], [kbG])
            ps = psA.next()
            k.op(k.pe, lambda e, ps=ps: e.matmul(ps[0:CL, :], lhsT=c["ones_f"][32:33, 0:CL], rhs=rows[32:33, sl], start=True, stop=True),
                 [rows, c["ones_f"]], [ps])
            k.op(k.act, lambda e, ps=ps: e.copy(GBs.ap, ps[0:CL, :]), [ps], [GBs])
            for ci in range(TT // CL):
                n = g * (TT // CL) + ci
                cs = slice(g * TT + ci * CL, g * TT + (ci + 1) * CL)
                gs = slice(ci * CL, (ci + 1) * CL)
                pk = psT.next()
                k.op(k.pe, lambda e: e.transpose(pk.ap, kT[:, cs], c["ident_b"].ap), [kT, c["ident_b"]], [pk])
                kbd = kbd_r.next()
                kdec = kdec_r.next()
                k.op(k.dve, lambda e: e.tensor_scalar(kbd.ap, pk.ap, egcol[:, n:n + 1], None, ALU.mult), [pk, egcol], [kbd])
                k.op(k.act, lambda e: e.activation(out=kdec.ap, in_=pk.ap, func=AF.Copy, scale=edcol[:, n:n + 1]), [pk, edcol], [kdec])
                pv = psT.next()
                k.op(k.pe, lambda e: e.transpose(pv.ap, vT[:, cs], c["ident_b"].ap), [vT, c["ident_b"]], [pv])
                vb = vb_r.next()
                k.op(k.act, lambda e: e.activation(out=vb.ap, in_=pv.ap, func=AF.Copy, scale=bcol[:, n:n + 1]), [pv, bcol], [vb])
                t1 = tmp_r.next()
                k.op(k.dve, lambda e: e.scalar_tensor_tensor(out=t1.ap, in0=GBs[:, gs], scalar=gccol[:, n:n + 1], in1=maskU.ap,
                                                             op0=ALU.subtract, op1=ALU.add), [GBs, gccol, maskU], [t1])
                decT = decT_r.next()
                k.op(k.act, lambda e: e.activation(out=decT.ap, in_=t1.ap, func=AF.Exp), [t1], [decT])
                t2 = tmp_r.next()
                k.op(k.dve, lambda e: e.scalar_tensor_tensor(out=t2.ap, in0=GBs[:, gs], scalar=gccol[:, n:n + 1], in1=maskL.ap,
                                                             op0=ALU.subtract, op1=ALU.add), [GBs, gccol, maskL], [t2])
                dec = dec_r.next()
                k.op(k.act, lambda e: e.activation(out=dec.ap, in_=t2.ap, func=AF.Exp, scale=-1.0), [t2], [dec])
                decTs = decTs_r.next()
                k.op(k.pool, lambda e: e.tensor_tensor(decTs.ap, decT.ap, sU01.ap, ALU.mult), [decT, sU01], [decTs])
                pL = psB.next()
                k.op(k.pe, lambda e: e.matmul(pL[0:CL, 0:CL], lhsT=kbG[:, gs], rhs=kT[:, cs], start=True, stop=True), [kbG, kT], [pL])
                P = P_r.next()
                k.op(k.dve, lambda e: e.scalar_tensor_tensor(out=P.ap, in0=pL[0:CL, 0:CL], scalar=-1.0, in1=dec.ap,
                                                             op0=ALU.mult, op1=ALU.mult), [pL, dec], [P])
                pLt = psB.next()
                k.op(k.pe, lambda e: e.matmul(pLt[0:CL, 0:CL], lhsT=kT[:, cs], rhs=kbG[:, gs], start=True, stop=True), [kbG, kT], [pLt])
                Pt = Pt_r.next()
                k.op(k.dve, lambda e: e.scalar_tensor_tensor(out=Pt.ap, in0=pLt[0:CL, 0:CL], scalar=-1.0, in1=decTs.ap,
                                                             op0=ALU.mult, op1=ALU.mult), [pLt, decTs], [Pt])
                pA = psB.next()
                k.op(k.pe, lambda e: e.matmul(pA[0:CL, 0:CL], lhsT=kT[:, cs], rhs=qT[:, cs], start=True, stop=True), [qT, kT], [pA])
                attT = attT_r.next()
                k.op(k.dve, lambda e: e.tensor_tensor(attT.ap, pA[0:CL, 0:CL], decT.ap, ALU.mult), [pA, decT], [attT])
                Tt = Tt_r.next()
                k.op(k.pool, lambda e: e.tensor_tensor(Tt.ap, Pt.ap, ident[0:CL, 0:CL], ALU.add), [Pt, ident], [Tt])
                for lev in range(1, 6):
                    p1 = psB.next()
                    k.op(k.pe, lambda e: e.matmul(p1[0:CL, 0:CL], lhsT=Pt.ap, rhs=P.ap, start=True, stop=True), [Pt, P], [p1])
                    Pn = P_r.next()
                    k.op(k.act, lambda e: e.copy(Pn.ap, p1[0:CL, 0:CL]), [p1], [Pn])
                    if lev < 5:
                        p2 = psB.next()
                        k.op(k.pe, lambda e: e.matmul(p2[0:CL, 0:CL], lhsT=P.ap, rhs=Pt.ap, start=True, stop=True), [Pt, P], [p2])
                        Ptn = Pt_r.next()
                        k.op(k.act, lambda e: e.copy(Ptn.ap, p2[0:CL, 0:CL]), [p2], [Ptn])
                    else:
                        Ptn = None
                    p3 = psB.next()
                    k.op(k.pe, lambda e: e.matmul(p3[0:CL, 0:CL], lhsT=Pn.ap, rhs=Tt.ap, start=True, stop=True), [Pn, Tt], [p3])
                    Ttn = Tt_r.next()
                    k.op(k.dve, lambda e: e.tensor_tensor(Ttn.ap, Tt.ap, p3[0:CL, 0:CL], ALU.add), [Tt, p3], [Ttn])
                    P, Pt, Tt = Pn, Ptn, Ttn
                Ttb = Ttb_r.next()
                k.op(k.act, lambda e: e.copy(Ttb.ap, Tt.ap), [Tt], [Ttb])
                pw = psB.next()
                k.op(k.pe, lambda e: e.matmul(pw[:, 0:CL], lhsT=kbd.ap, rhs=Ttb.ap, start=True, stop=True), [kbd, Ttb], [pw])
                nwT = nwT_r.next()
                k.op(k.act, lambda e: e.activation(out=nwT.ap, in_=pw[:, 0:CL], func=AF.Copy, scale=-1.0), [pw], [nwT])
                pvn = psB.next()
                k.op(k.pe, lambda e: e.matmul(pvn[0:CL, :], lhsT=Ttb.ap, rhs=vb.ap, start=True, stop=False), [Ttb, vb], [pvn])
                k.op(k.pe, lambda e: e.matmul(pvn[0:CL, :], lhsT=nwT.ap, rhs=Sb.ap, start=False, stop=True), [nwT, Sb], [pvn])
                vnew = vnew_r.next()
                k.op(k.act, lambda e: e.copy(vnew.ap, pvn[0:CL, :]), [pvn], [vnew])
                po = psB.next()
                k.op(k.pe, lambda e: e.matmul(po[:, 0:CL], lhsT=Sb.ap, rhs=qdG[:, gs], start=True, stop=False), [Sb, qdG], [po])
                k.op(k.pe, lambda e: e.matmul(po[:, 0:CL], lhsT=vnew.ap, rhs=attT.ap, start=False, stop=True), [vnew, attT], [po])
                k.op(k.dve, lambda e: e.tensor_copy(acc[:, cs], po[:, 0:CL]), [po], [acc])
                pd = psB.next()
                k.op(k.pe, lambda e: e.matmul(pd.ap, lhsT=kdec.ap, rhs=vnew.ap, start=True, stop=True), [kdec, vnew], [pd])
                k.op(k.dve, lambda e: e.scalar_tensor_tensor(out=S.ap, in0=S.ap, scalar=glast[:, n:n + 1], in1=pd.ap,
                                                             op0=ALU.mult, op1=ALU.add), [S, glast, pd], [S])
                k.op(k.act, lambda e: e.copy(Sb.ap, S.ap), [S], [Sb])
        for g in range(ngrp):
            sl = slice(g * TT, (g + 1) * TT)
            gt = gst.next()
            k.dma(gt.ap, gateT_d[h, :, sl], writes=[gt])
            k.op(k.act, lambda e, gt=gt: e.activation(out=gt.ap, in_=gt.ap, func=AF.Silu), [gt], [gt])
            s_ = sq.next()
            k.op(k.act, lambda e, s_=s_: e.activation(out=s_.ap, in_=acc[:, sl], func=AF.Square), [acc], [s_])
            ps = psA.next()
            k.op(k.pe, lambda e, s_=s_, ps=ps: e.matmul(ps.ap, lhsT=c["ones_f"].ap, rhs=s_.ap, start=True, stop=True),
                 [s_, c["ones_f"]], [ps])
            k.op(k.act, lambda e, ps=ps: e.activation(out=rstd.ap, in_=ps.ap, func=AF.Sqrt, bias=c["eps"].ap, scale=1.0 / 128),
                 [ps, c["eps"]], [rstd])
            k.op(k.dve, lambda e: e.reciprocal(rstd.ap, rstd.ap), [rstd], [rstd])
            o_ = ost.next()
            k.op(k.dve, lambda e, o_=o_: e.scalar_tensor_tensor(out=o_.ap, in0=acc[:, sl], scalar=onorm[:, 0:1], in1=rstd.ap,
                                                               op0=ALU.mult, op1=ALU.mult), [acc, onorm, rstd], [o_])
            k.op(k.dve, lambda e, o_=o_, gt=gt: e.tensor_tensor(o_.ap, o_.ap, gt.ap, ALU.mult), [o_, gt], [o_])
            o = Tl(oT_o, "oT_o")
            k.dma(oT_o[h, :, sl], o_.ap, reads=[o_], writes=[o])
            outs.append(o)
    k.finish(outs)
    k.close()
    return nc


PCH = 128
TWO_PI = 2.0 * math.pi
C1_2PI = 6.28125
C2_2PI = TWO_PI - 6.28125


def build_s5(ng=24, seq=SEQ):
    nc = bass.Bass("TRN2", target_bir_lowering=False)
    k = K(nc)
    dt = nc.dram_tensor
    nfc = ng // 8
    ntile = seq // TT
    NJ = PCH + 1
    uT_d = dt("uT", [nfc, 128, seq], F32, kind="ExternalInput").ap()
    lre_d = dt("lam_re", [ng, 64], F32, kind="ExternalInput").ap()
    lim_d = dt("lam_im", [ng, 64], F32, kind="ExternalInput").ap()
    ldt_d = dt("log_dt", [1, ng], F32, kind="ExternalInput").ap()
    bre_d = dt("b_re", [ng, 64, 16], F32, kind="ExternalInput").ap()
    bim_d = dt("b_im", [ng, 64, 16], F32, kind="ExternalInput").ap()
    cre_d = dt("c_re", [ng * 16, 64], F32, kind="ExternalInput").ap()
    cim_d = dt("c_im", [ng * 16, 64], F32, kind="ExternalInput").ap()
    dsk_d = dt("d_skip", [nfc * 128], F32, kind="ExternalInput").ap()
    yT_o = dt("yT", [nfc, 128, seq], F32, kind="ExternalOutput").ap()

    c = make_consts(k)
    ident = c["ident"]
    uTb = [k.sb("uTb%d" % i, [128, seq], BF16) for i in range(nfc)]
    GRall = k.sb("GRall", [128, ng, TT], F32)
    GR = [Tl(GRall[:, g, :], "GR%d" % g) for g in range(ng)]
    cosT = k.sb("cosT", [128, ng, NJ], F32)
    sinT = k.sb("sinT", [128, ng, NJ], F32)
    W1 = [k.sb("W1_%d" % g, [128, 128], BF16) for g in range(ng)]
    W2 = [k.sb("W2_%d" % g, [128, 128], BF16) for g in range(ng)]
    CA = [k.sb("CA_%d" % g, [128, 128], BF16) for g in range(ng)]
    CB = [k.sb("CB_%d" % g, [128, 128], BF16) for g in range(ng)]
    ROT = [k.sb("ROT_%d" % g, [128, 128], F32) for g in range(ng)]
    init = [k.sb("init_%d" % g, [128, 1], F32) for g in range(ng)]
    LRe = k.sb("LRe", [128, ng], F32)
    LIm = k.sb("LIm", [128, ng], F32)
    DT = k.sb("DT", [128, ng], F32)
    RHO = k.sb("RHO", [128, ng], F32)
    TH = k.sb("TH", [128, ng], F32)
    pp = [k.sb("pp%d" % i, [128, ng], F32) for i in range(8)]
    sgnA = k.sb("sgnA", [128, 1], F32)
    sgnB = k.sb("sgnB", [128, 1], F32)
    gmask = k.sb("gmask", [128, 8], F32)
    SW = k.sb("SW", [128, 128], F32)
    E1 = k.sb("E1", [128, 128], F32)
    dsk = load_vec_T(k, "dsk", dsk_d, nfc * 128)
    BA = k.sb("BA", [128, ng, 16], F32)
    BB = k.sb("BB", [128, ng, 16], F32)
    BP1 = k.sb("BP1", [128, ng, 16], F32)
    BP2 = k.sb("BP2", [128, ng, 16], F32)
    u32 = Ring([k.sb("u32_%d" % i, [128, TT], F32) for i in range(2 * nfc)])
    xs = Ring([k.sb("xs%d" % i, [128, TT], F32) for i in range(4)])
    tt_r = Ring([k.sb("tt%d" % i, [128, TT], F32) for i in range(2)])
    z_r = Ring([k.sb("z%d" % i, [128, TT], BF16) for i in range(4)])
    ost = Ring([k.sb("ost%d" % i, [128, TT], F32) for i in range(2)])
    psX = Ring([k.ps("psX%d" % i, [128, TT]) for i in range(4)])
    psY = Ring([k.ps("psY%d" % i, [128, TT]) for i in range(2)])
    psI = Ring([k.ps("psI%d" % i, [128, 128]) for i in range(2)])
    outs = []

    with nc.allow_non_contiguous_dma(reason="small params"):
        for half in range(2):
            k.dma(LRe[half * 64:(half + 1) * 64, :], lre_d.rearrange("g p -> p g"), writes=[LRe])
            k.dma(LIm[half * 64:(half + 1) * 64, :], lim_d.rearrange("g p -> p g"), writes=[LIm])
        k.dma(DT.ap, ldt_d.partition_broadcast(128), writes=[DT])
        k.dma(BA[0:64], bre_d.rearrange("g p c -> p g c"), writes=[BA])
        k.dma(BA[64:128], bim_d.rearrange("g p c -> p g c"), writes=[BA])
        k.dma(BB[0:64], bim_d.rearrange("g p c -> p g c"), writes=[BB])
        k.dma(BB[64:128], bre_d.rearrange("g p c -> p g c"), writes=[BB])
    for i in range(nfc):
        k.dma(uTb[i].ap, uT_d[i], writes=[uTb[i]], q=k.pool)
    k.op(k.act, lambda e: e.activation(out=DT.ap, in_=DT.ap, func=AF.Exp), [DT], [DT])
    k.op(k.dve, lambda e: e.tensor_tensor(RHO.ap, LRe.ap, DT.ap, ALU.mult), [LRe, DT], [RHO])
    k.op(k.act, lambda e: e.activation(out=RHO.ap, in_=RHO.ap, func=AF.Exp), [RHO], [RHO])
    k.op(k.dve, lambda e: e.tensor_tensor(TH.ap, LIm.ap, DT.ap, ALU.mult), [LIm, DT], [TH])
    k.op(k.dve, lambda e: e.memset(sgnA.ap, 1.0), [], [sgnA])
    k.op(k.dve, lambda e: e.memset(sgnA[0:64, :], -1.0), [sgnA], [sgnA])
    k.op(k.dve, lambda e: e.tensor_scalar(sgnB.ap, sgnA.ap, -1.0, None, ALU.mult), [sgnA], [sgnB])
    nel = ng * NJ
    flat = GRall.ap.rearrange("p g t -> p (g t)")
    ANG = flat[:, 0:nel].rearrange("p (g j) -> p g j", g=ng)
    TF = flat[:, nel:2 * nel].rearrange("p (g j) -> p g j", g=ng)
    NI = flat[:, 2 * nel:3 * nel].bitcast(mybir.dt.int32).rearrange("p (g j) -> p g j", g=ng)
    JJ = flat[:, 3 * nel:3 * nel + NJ]
    k.op(k.pool, lambda e: e.iota(JJ, [[1, NJ]], base=0, channel_multiplier=0, allow_small_or_imprecise_dtypes=True),
         [], [GRall])
    k.op(k.dve, lambda e: e.tensor_tensor(ANG, TH.ap.unsqueeze(2).broadcast_to([128, ng, NJ]),
                                          JJ.unsqueeze(1).broadcast_to([128, ng, NJ]), ALU.mult), [TH, GRall], [GRall])
    for shift, outT in ((0.0, sinT), (0.5 * math.pi, cosT)):
        k.op(k.dve, lambda e: e.tensor_scalar(TF, ANG, shift, 1.0 / TWO_PI, ALU.add, ALU.mult), [GRall], [GRall])
        k.op(k.dve, lambda e: e.tensor_copy(NI, TF), [GRall], [GRall])
        k.op(k.dve, lambda e: e.tensor_copy(TF, NI), [GRall], [GRall])
        k.op(k.dve, lambda e: e.scalar_tensor_tensor(out=outT.ap, in0=TF, scalar=-C1_2PI, in1=ANG, op0=ALU.mult, op1=ALU.add),
             [GRall], [outT])
        k.op(k.dve, lambda e: e.scalar_tensor_tensor(out=outT.ap, in0=TF, scalar=-C2_2PI, in1=outT.ap, op0=ALU.mult, op1=ALU.add),
             [GRall, outT], [outT])
        k.op(k.dve, lambda e: e.tensor_scalar(outT.ap, outT.ap, shift, math.pi, ALU.add, ALU.min), [outT], [outT])
        k.op(k.dve, lambda e: e.tensor_scalar(outT.ap, outT.ap, -math.pi, None, ALU.max), [outT], [outT])
        k.op(k.act, lambda e: e.activation(out=outT.ap, in_=outT.ap, func=AF.Sin), [outT], [outT])
    cth, sth = cosT[:, :, 1], sinT[:, :, 1]
    are, aim, am1, den, zre, zim, t0_, t1_ = pp

    def tt(out, a, b, op, rd, wr):
        k.op(k.dve, lambda e: e.tensor_tensor(out, a, b, op), rd, wr)
    tt(are.ap, RHO.ap, cth, ALU.mult, [RHO, cosT], [are])
    tt(aim.ap, RHO.ap, sth, ALU.mult, [RHO, sinT], [aim])
    k.op(k.dve, lambda e: e.tensor_scalar(am1.ap, are.ap, -1.0, None, ALU.add), [are], [am1])
    tt(den.ap, LRe.ap, LRe.ap, ALU.mult, [LRe], [den])
    tt(t0_.ap, LIm.ap, LIm.ap, ALU.mult, [LIm], [t0_])
    tt(den.ap, den.ap, t0_.ap, ALU.add, [den, t0_], [den])
    k.op(k.dve, lambda e: e.reciprocal(den.ap, den.ap), [den], [den])
    tt(zre.ap, am1.ap, LRe.ap, ALU.mult, [am1, LRe], [zre])
    tt(t0_.ap, aim.ap, LIm.ap, ALU.mult, [aim, LIm], [t0_])
    tt(zre.ap, zre.ap, t0_.ap, ALU.add, [zre, t0_], [zre])
    tt(zre.ap, zre.ap, den.ap, ALU.mult, [zre, den], [zre])
    tt(zim.ap, aim.ap, LRe.ap, ALU.mult, [aim, LRe], [zim])
    tt(t0_.ap, am1.ap, LIm.ap, ALU.mult, [am1, LIm], [t0_])
    tt(zim.ap, zim.ap, t0_.ap, ALU.subtract, [zim, t0_], [zim])
    tt(zim.ap, zim.ap, den.ap, ALU.mult, [zim, den], [zim])
    k.op(k.dve, lambda e: e.tensor_scalar(t0_.ap, zim.ap, sgnA[:, 0:1], None, ALU.mult), [zim, sgnA], [t0_])
    k.op(k.dve, lambda e: e.tensor_scalar(t1_.ap, zre.ap, sgnB[:, 0:1], None, ALU.mult), [zre, sgnB], [t1_])

    def bc(t):
        return t.ap.unsqueeze(2).broadcast_to([128, ng, 16])
    tmpB = k.sb("tmpB", [128, ng, 16], F32)
    tt(BP1.ap, BA.ap, bc(zre), ALU.mult, [BA, zre], [BP1])
    tt(tmpB.ap, BB.ap, bc(t0_), ALU.mult, [BB, t0_], [tmpB])
    tt(BP1.ap, BP1.ap, tmpB.ap, ALU.add, [BP1, tmpB], [BP1])
    tt(BP2.ap, BB.ap, bc(t1_), ALU.mult, [BB, t1_], [BP2])
    tt(tmpB.ap, BA.ap, bc(zim), ALU.mult, [BA, zim], [tmpB])
    tt(BP2.ap, BP2.ap, tmpB.ap, ALU.add, [BP2, tmpB], [BP2])
    k.op(k.pool, lambda e: e.memset(gmask.ap, 1.0), [], [gmask])
    k.op(k.pool, lambda e: e.affine_select(out=gmask.ap, in_=gmask.ap, pattern=[[-16, 8]], compare_op=ALU.is_ge, fill=0.0,
                                           base=0, channel_multiplier=1), [gmask], [gmask])
    k.op(k.pool, lambda e: e.affine_select(out=gmask.ap, in_=gmask.ap, pattern=[[16, 8]], compare_op=ALU.is_ge, fill=0.0,
                                           base=15, channel_multiplier=-1), [gmask], [gmask])
    k.op(k.pool, lambda e: e.memset(SW.ap, 1.0), [], [SW])
    k.op(k.pool, lambda e: e.affine_select(out=SW.ap, in_=SW.ap, pattern=[[-1, 128]], compare_op=ALU.is_equal, fill=0.0,
                                           base=64, channel_multiplier=1), [SW], [SW])
    k.op(k.pool, lambda e: e.memset(E1.ap, 1.0), [], [E1])
    k.op(k.pool, lambda e: e.affine_select(out=E1.ap, in_=E1.ap, pattern=[[-1, 128]], compare_op=ALU.is_equal, fill=0.0,
                                           base=-64, channel_multiplier=1), [E1], [E1])
    k.op(k.pool, lambda e: e.tensor_tensor(SW.ap, SW.ap, E1.ap, ALU.subtract), [SW, E1], [SW])
    cin = k.sb("cin", [128, 128], F32)
    for fc in range(nfc):
        for src, Wl in ((BP1, W1), (BP2, W2)):
            ps = psI.next()
            k.op(k.pe, lambda e: e.transpose(ps.ap, src[:, fc * 8:(fc + 1) * 8, :].rearrange("p g c -> p (g c)"), ident.ap),
                 [src, ident], [ps])
            for gg in range(8):
                g = fc * 8 + gg
                k.op(k.dve, lambda e: e.tensor_scalar(Wl[g].ap, ps.ap, gmask[:, gg:gg + 1], None, ALU.mult), [ps, gmask], [Wl[g]])
        for which, Cl in ((0, CA), (1, CB)):
            first, second = (cre_d, cim_d) if which == 0 else (cim_d, cre_d)
            k.dma(cin[:, 0:64], first[fc * 128:(fc + 1) * 128, :], writes=[cin])
            k.dma(cin[:, 64:128], second[fc * 128:(fc + 1) * 128, :], writes=[cin])
            if which == 0:
                k.op(k.dve, lambda e: e.tensor_scalar(cin[:, 64:128], cin[:, 64:128], -1.0, None, ALU.mult), [cin], [cin])
            else:
                k.op(k.dve, lambda e: e.tensor_scalar(cin.ap, cin.ap, -1.0, None, ALU.mult), [cin], [cin])
            ps = psI.next()
            k.op(k.pe, lambda e: e.transpose(ps.ap, cin.ap, ident.ap), [cin, ident], [ps])
            for gg in range(8):
                g = fc * 8 + gg
                k.op(k.pool, lambda e: e.memset(Cl[g].ap, 0.0), [], [Cl[g]])
                k.op(k.act, lambda e: e.copy(Cl[g][:, gg * 16:(gg + 1) * 16], ps[:, gg * 16:(gg + 1) * 16]), [ps, Cl[g]], [Cl[g]])
    for g in range(ng):
        k.op(k.dve, lambda e: e.tensor_scalar(E1.ap, ident.ap, cosT[:, g, PCH:PCH + 1], None, ALU.mult), [ident, cosT, E1], [E1])
        k.op(k.dve, lambda e: e.scalar_tensor_tensor(out=ROT[g].ap, in0=SW.ap, scalar=sinT[:, g, PCH:PCH + 1], in1=E1.ap,
                                                     op0=ALU.mult, op1=ALU.add), [SW, sinT, E1], [ROT[g]])
        k.op(k.pool, lambda e: e.memset(init[g].ap, 0.0), [], [init[g]])
    for g in range(ng):
        GR[g].lw = GRall.lw
        GR[g].rd = list(GRall.rd)

    npc = TT // PCH
    for ti in range(ntile):
        t0 = ti * TT
        uts = []
        for fc in range(nfc):
            ut = u32.next()
            k.dma(ut.ap, uT_d[fc, :, t0:t0 + TT], writes=[ut])
            uts.append(ut)
        for g in range(ng):
            fc = g // 8
            xss = []
            for Wl in (W1, W2):
                ps = psX.next()
                k.op(k.pe, lambda e: e.matmul(ps.ap, lhsT=Wl[g].ap, rhs=uTb[fc][:, t0:t0 + TT], start=True, stop=True),
                     [Wl[g], uTb[fc]], [ps])
                x = xs.next()
                k.op(k.act, lambda e: e.copy(x.ap, ps.ap), [ps], [x])
                xss.append(x)
            cb_ = cosT[:, g, 0:PCH].unsqueeze(1).broadcast_to([128, npc, PCH])
            sb_ = sinT[:, g, 0:PCH].unsqueeze(1).broadcast_to([128, npc, PCH])
            ta = tt_r.next()
            tb = tt_r.next()

            def v3(ap):
                return ap.rearrange("p (a b) -> p a b", a=npc)
            k.op(k.pool, lambda e: e.tensor_tensor(v3(ta.ap), v3(xss[0].ap), cb_, ALU.mult), [xss[0], cosT], [ta])
            k.op(k.pool, lambda e: e.tensor_tensor(v3(tb.ap), v3(xss[1].ap), sb_, ALU.mult), [xss[1], sinT], [tb])
            k.op(k.pool, lambda e: e.tensor_tensor(GR[g].ap, ta.ap, tb.ap, ALU.add), [ta, tb], [GR[g]])
        for pc in range(npc):
            cs = slice(pc * PCH, (pc + 1) * PCH)
            for g in range(ng):
                k.op(k.dve, lambda e: e.tensor_tensor_scan(GR[g][:, cs], RHO[:, g:g + 1].broadcast_to([128, PCH]), GR[g][:, cs],
                                                           init[g][:, 0:1], ALU.mult, ALU.add), [GR[g], RHO, init[g]], [GR[g]])
                pi = psI.next()
                k.op(k.pe, lambda e: e.matmul(pi[:, 0:1], lhsT=ROT[g].ap, rhs=GR[g][:, (pc + 1) * PCH - 1:(pc + 1) * PCH],
                                              start=True, stop=True), [ROT[g], GR[g]], [pi])
                k.op(k.act, lambda e: e.copy(init[g].ap, pi[:, 0:1]), [pi], [init[g]])
        for fc in range(nfc):
            py = psY.next()
            for gg in range(8):
                g = fc * 8 + gg
                cb_ = cosT[:, g, 0:PCH].unsqueeze(1).broadcast_to([128, npc, PCH])
                sb_ = sinT[:, g, 0:PCH].unsqueeze(1).broadcast_to([128, npc, PCH])
                z1 = z_r.next()
                z2 = z_r.next()
                k.op(k.dve, lambda e: e.tensor_tensor(v3(z1.ap), v3(GR[g].ap), cb_, ALU.mult), [GR[g], cosT], [z1])
                k.op(k.dve, lambda e: e.tensor_tensor(v3(z2.ap), v3(GR[g].ap), sb_, ALU.mult), [GR[g], sinT], [z2])
                k.op(k.pe, lambda e: e.matmul(py.ap, lhsT=CA[g].ap, rhs=z1.ap, start=(gg == 0), stop=False), [CA[g], z1], [py])
                k.op(k.pe, lambda e: e.matmul(py.ap, lhsT=CB[g].ap, rhs=z2.ap, start=False, stop=(gg == 7)), [CB[g], z2], [py])
            o_ = ost.next()
            k.op(k.dve, lambda e: e.scalar_tensor_tensor(out=o_.ap, in0=uts[fc].ap, scalar=dsk[:, fc:fc + 1], in1=py.ap,
                                                         op0=ALU.mult, op1=ALU.add), [uts[fc], dsk, py], [o_])
            k.op(k.act, lambda e: e.activation(out=o_.ap, in_=o_.ap, func=AF.Gelu), [o_], [o_])
            o = Tl(yT_o, "yT_o")
            k.dma(yT_o[fc, :, t0:t0 + TT], o_.ap, reads=[o_], writes=[o])
            outs.append(o)
    k.finish(outs)
    k.close()
    return nc


_PROGS = {}


def _prog(key, fn):
    if key not in _PROGS:
        _PROGS[key] = fn()
    return _PROGS[key]


def _run(nc, in_maps):
    res = run_bass_kernel_spmd(nc, in_maps, core_ids=list(range(8)))
    return res.results


KINDS = ["s5", "gdn", "fox"]
_C = np.ascontiguousarray


def kernel(x, mem, mem_norm, w_mem_kv, norm1, w_out, norm2, w_up, w_down, norm_f,
           s5_w_in, s5_lam_re, s5_lam_im, s5_log_dt, s5_b_re, s5_b_im, s5_c_re, s5_c_im,
           s5_d_skip, s5_w_glu, s5_b_glu,
           gdn_w_in, gdn_conv_w, gdn_a_log, gdn_dt_bias, gdn_o_norm,
           fox_w_in, fox_b_f):
    f = lambda a: np.asarray(a, dtype=np.float32)
    x, mem = f(x), f(mem)
    depth = norm1.shape[0]
    W = MIXW
    hT = [_C(x[c // 4, (c % 4) * NTOK:(c % 4 + 1) * NTOK, :].T) for c in range(8)]
    memT = [_C(mem[b].T) for b in range(2)]
    mixT = None
    prev = None
    for i in range(depth + 1):
        nxt = KINDS[i % 3] if i < depth else None
        j = i // 3
        final = (i == depth)
        nc = _prog(("tok", prev, nxt, final), lambda: build_tok(prev, nxt, final))
        in_maps = []
        for c in range(8):
            m = {"hT": hT[c]}
            if prev:
                m.update(mixT=mixT[c], w_out=f(w_out[i - 1]), norm2=f(norm2[i - 1]), w_up=f(w_up[i - 1]), w_down=f(w_down[i - 1]))
                if prev == "s5":
                    jp = (i - 1) // 3
                    m.update(w_glu=f(s5_w_glu[jp]), b_glu=f(s5_b_glu[jp]))
            if nxt:
                w_in = {"s5": s5_w_in, "gdn": gdn_w_in, "fox": fox_w_in}[nxt][j]
                m.update(norm1=f(norm1[i]), w_in=f(w_in), memT=memT[c // 4], mem_norm=f(mem_norm), w_mem_kv=f(w_mem_kv))
            if final:
                m.update(norm_f=f(norm_f))
            in_maps.append(m)
        r = _run(nc, in_maps)
        if final:
            out = np.empty((2, SEQ, D), np.float32)
            for c in range(8):
                out[c // 4, (c % 4) * NTOK:(c % 4 + 1) * NTOK, :] = r[c]["outT"].T
            return out
        hT = [r[c]["hT_out"] for c in range(8)]
        readT = [r[c]["readT"] for c in range(8)]
        projB = [np.concatenate([r[b * 4 + q]["projT"] for q in range(4)], axis=1) for b in range(2)]
        if nxt != "s5":
            smallB = [np.concatenate([r[b * 4 + q]["smallT"] for q in range(4)], axis=1) for b in range(2)]
        mixB = [np.empty((W, SEQ), np.float32) for _ in range(2)]
        if nxt == "s5":
            ncm = _prog(("s5",), build_s5)
            in_maps = []
            for c in range(8):
                b, sub = c // 4, c % 4
                g0 = sub * 24
                in_maps.append({
                    "uT": _C(projB[b][sub * 384:(sub + 1) * 384].reshape(3, 128, SEQ)),
                    "lam_re": _C(f(s5_lam_re[j])[g0:g0 + 24]), "lam_im": _C(f(s5_lam_im[j])[g0:g0 + 24]),
                    "log_dt": _C(f(s5_log_dt[j])[g0:g0 + 24][None, :]),
                    "b_re": _C(f(s5_b_re[j])[g0:g0 + 24]), "b_im": _C(f(s5_b_im[j])[g0:g0 + 24]),
                    "c_re": _C(f(s5_c_re[j])[g0:g0 + 24].reshape(384, 64)), "c_im": _C(f(s5_c_im[j])[g0:g0 + 24].reshape(384, 64)),
                    "d_skip": _C(f(s5_d_skip[j])[sub * 384:(sub + 1) * 384])})
            rm = _run(ncm, in_maps)
            for c in range(8):
                mixB[c // 4][(c % 4) * 384:(c % 4 + 1) * 384] = rm[c]["yT"].reshape(384, SEQ)
        elif nxt == "fox":
            ncm = _prog(("fox",), build_fox)
            in_maps = []
            for c in range(8):
                b, sub = c // 4, c % 4
                hs = [sub * 3 + t for t in range(3)]
                P = projB[b]
                in_maps.append({
                    "qT": _C(np.stack([P[h * 128:(h + 1) * 128] for h in hs])),
                    "kT": _C(np.stack([P[W + h * 128:W + (h + 1) * 128] for h in hs])),
                    "v": _C(np.stack([P[2 * W + h * 128:2 * W + (h + 1) * 128].T for h in hs])),
                    "fl": _C(np.stack([smallB[b][h] for h in hs])),
                    "b_f": _C(f(fox_b_f[j])[hs[0]:hs[0] + 3])})
            rm = _run(ncm, in_maps)
            for c in range(8):
                mixB[c // 4][(c % 4) * 384:(c % 4 + 1) * 384] = rm[c]["oT"].reshape(384, SEQ)
        else:
            cwj, alj, dbj, onj = f(gdn_conv_w[j]), f(gdn_a_log[j]), f(gdn_dt_bias[j]), f(gdn_o_norm[j])
            for rnd, nh in ((0, 2), (1, 1)):
                ncm = _prog(("gdn", nh), lambda: build_gdn(nh=nh))
                in_maps = []
                hsl = []
                for c in range(8):
                    b, sub = c // 4, c % 4
                    hs = [sub * 3 + t for t in range(3)][(0 if rnd == 0 else 2):(2 if rnd == 0 else 3)]
                    hsl.append(hs)
                    P, Sm = projB[b], smallB[b]
                    in_maps.append({
                        "qkvT": _C(np.stack([np.stack([P[t * W + h * 128:t * W + (h + 1) * 128] for t in range(3)]) for h in hs])),
                        "gateT": _C(np.stack([P[3 * W + h * 128:3 * W + (h + 1) * 128] for h in hs])),
                        "cw": _C(np.stack([np.stack([cwj[:, t * W + h * 128:t * W + (h + 1) * 128].T for t in range(3)]) for h in hs])),
                        "abrow": _C(np.stack([np.stack([Sm[h], Sm[12 + h]]) for h in hs])),
                        "abcol": _C(np.stack([np.stack([Sm[h].reshape(-1, CL).T, Sm[12 + h].reshape(-1, CL).T]) for h in hs])),
                        "hp": _C(np.stack([np.stack([alj[h], dbj[h]]) for h in hs]).astype(np.float32)),
                        "o_norm": onj})
                rm = _run(ncm, in_maps)
                for c in range(8):
                    for t, h in enumerate(hsl[c]):
                        mixB[c // 4][h * 128:(h + 1) * 128] = rm[c]["oT"][t]
        mixT = [_C(np.concatenate([mixB[c // 4][:, (c % 4) * NTOK:(c % 4 + 1) * NTOK], readT[c]], axis=0)) for c in range(8)]
        prev = nxt
```

```python
import contextlib
import math
import numpy as np
import concourse.bass as bass
import concourse.mybir as mybir
from concourse.bass_utils import run_bass_kernel_spmd

F32 = mybir.dt.float32
BF16 = mybir.dt.bfloat16
AF = mybir.ActivationFunctionType
ALU = mybir.AluOpType
AX = mybir.AxisListType


class Tl:
    __slots__ = ("ap", "lw", "rd", "name", "dsem", "pend")

    def __init__(self, ap, name=""):
        self.ap = ap
        self.lw = None
        self.rd = []
        self.name = name
        self.dsem = None
        self.pend = None

    def __getitem__(self, idx):
        return self.ap[idx]


class Eng:
    def __init__(self, k, e, name):
        self.k = k
        self.e = e
        self.name = name
        self.sem = k.new_sem("e_" + name)
        self.count = 0
        self.known = {}
        self.pending = []


class K:
    def __init__(self, nc):
        self.nc = nc
        self.ctx = contextlib.ExitStack()
        self.sems = {}
        self.dma_tot = {}
        self.nsem = 0
        self.pe = Eng(self, nc.tensor, "pe")
        self.act = Eng(self, nc.scalar, "act")
        self.dve = Eng(self, nc.vector, "dve")
        self.pool = Eng(self, nc.gpsimd, "pool")
        self.sp = Eng(self, nc.sync, "sp")
        self.engs = [self.pe, self.act, self.dve, self.pool, self.sp]
        self.ndma = 0
        self.dma_rr = 0
        self.dma_pool = [self.new_sem("dma%d" % i) for i in range(24)]
        self.dma_last_ev = {}

    def new_sem(self, name):
        h = self.ctx.enter_context(self.nc.semaphore(name))
        key = self.nsem
        self.nsem += 1
        self.sems[key] = h
        self.dma_tot[key] = 0
        return key

    def sb(self, name, shape, dt):
        t = self.ctx.enter_context(self.nc.sbuf_tensor(name, list(shape), dt))
        return Tl(t.ap() if hasattr(t, "ap") and callable(getattr(t, "ap")) else t, name)

    def ps(self, name, shape, dt=F32):
        t = self.ctx.enter_context(self.nc.psum_tensor(name, list(shape), dt))
        return Tl(t.ap() if hasattr(t, "ap") and callable(getattr(t, "ap")) else t, name)

    def sub(self, tl, ap, name=""):
        return Tl(ap, name or tl.name)

    def _wait(self, eng, deps):
        need = {}
        for d in deps:
            if d is None:
                continue
            s, v = d
            if s in self.dma_tot and self.dma_tot[s] > 0:
                v = self.dma_tot[s]
            if need.get(s, 0) < v:
                need[s] = v
        for s, v in need.items():
            if s == eng.sem and eng is self.pe:
                continue
            if eng.known.get(s, 0) < v:
                eng.e.wait_ge(self.sems[s], v)
                eng.known[s] = v

    def _deps(self, reads, writes, eng=None):
        deps = []
        for t in list(reads) + list(writes):
            if t.pend is not None and t.pend is not eng:
                raise RuntimeError("tile %s has uncommitted accesses on %s" % (t.name, t.pend.name))
        for t in reads:
            if t.lw is not None:
                deps.append(t.lw)
        for t in writes:
            if t.lw is not None:
                deps.append(t.lw)
            deps.extend(t.rd)
        return deps

    def _commit(self, ev, reads, writes):
        for t in reads:
            t.rd.append(ev)
            if len(t.rd) > 64:
                best = {}
                for s, v in t.rd:
                    if best.get(s, 0) < v:
                        best[s] = v
                t.rd = list(best.items())
        for t in writes:
            t.lw = ev
            t.rd = []

    def op(self, eng, fn, reads=(), writes=(), inc=True):
        self._wait(eng, self._deps(reads, writes, eng))
        ins = fn(eng.e)
        if not inc:
            assert eng is self.pe
            eng.pending.append((reads, writes))
            for t in list(reads) + list(writes):
                t.pend = eng
            return None
        eng.count += 1
        ins.then_inc(self.sems[eng.sem], 1)
        ev = (eng.sem, eng.count)
        for r_, w_ in eng.pending:
            for t in list(r_) + list(w_):
                t.pend = None
            self._commit(ev, r_, w_)
        eng.pending = []
        self._commit(ev, reads, writes)
        return ev

    def dma(self, out_ap, in_ap, reads=(), writes=(), q=None, **kw):
        q = q or self.sp
        self._wait(q, self._deps(reads, writes, q))
        s = self.dma_pool[self.dma_rr % len(self.dma_pool)]
        self.dma_rr += 1
        ins = q.e.dma_start(out=out_ap, in_=in_ap, **kw)
        ins.then_inc(self.sems[s], 16)
        self.dma_tot[s] += 16
        ev = (s, self.dma_tot[s])
        self._commit(ev, reads, writes)
        return ev

    def finish(self, tiles):
        deps = []
        for t in tiles:
            if t.lw is not None:
                deps.append(t.lw)
            deps.extend(t.rd)
        self._wait(self.sp, deps)

    def close(self):
        self.ctx.close()


D = 2048
DFF = 8192
KC = 16
TT = 512
NTOK = 2048
SEQ = 8192
MEMW = 512
MIXW = 1536
EPS = 1e-6
SCALE = 128 ** -0.5


class Ring:
    def __init__(self, tiles):
        self.t = tiles
        self.i = 0

    def next(self):
        t = self.t[self.i % len(self.t)]
        self.i += 1
        return t


def make_consts(k):
    c = {}
    c["ones_f"] = k.sb("ones_f", [128, 128], F32)
    c["ones_b"] = k.sb("ones_b", [128, 128], BF16)
    c["ident"] = k.sb("ident", [128, 128], F32)
    c["eps"] = k.sb("eps_c", [128, 1], F32)
    k.op(k.dve, lambda e: e.memset(c["eps"].ap, EPS), [], [c["eps"]])
    k.op(k.dve, lambda e: e.memset(c["ones_f"].ap, 1.0), [], [c["ones_f"]])
    k.op(k.dve, lambda e: e.memset(c["ones_b"].ap, 1.0), [], [c["ones_b"]])
    k.op(k.pool, lambda e: e.memset(c["ident"].ap, 1.0), [], [c["ident"]])
    k.op(k.pool, lambda e: e.affine_select(out=c["ident"].ap, in_=c["ident"].ap, pattern=[[-1, 128]],
                                           compare_op=ALU.is_equal, fill=0.0, base=0, channel_multiplier=1),
         [c["ident"]], [c["ident"]])
    c["ident_b"] = k.sb("ident_b", [128, 128], BF16)
    k.op(k.dve, lambda e: e.tensor_copy(c["ident_b"].ap, c["ident"].ap), [c["ident"]], [c["ident_b"]])
    return c


def linear(k, psr, wring, xT, kcs, w_ap, chunks, cb, ntok=TT, xsl=None):
    groups = []
    cur = []
    for i, (c0, wd) in enumerate(chunks):
        if cur and (len(cur) == 4 or chunks[cur[-1]][0] + chunks[cur[-1]][1] != c0):
            groups.append(cur)
            cur = []
        cur.append(i)
    if cur:
        groups.append(cur)
    KB = 4
    for g in groups:
        g0 = chunks[g[0]][0]
        g1 = chunks[g[-1]][0] + chunks[g[-1]][1]
        ncol = g1 - g0
        pst = [psr.next() for _ in g]
        for kb in range(0, kcs, KB):
            nk = min(KB, kcs - kb)
            slab = wring.next()
            k.dma(slab[:, 0:nk, 0:ncol],
                  w_ap[kb * 128:(kb + nk) * 128, g0:g1].rearrange("(kc p) n -> p kc n", p=128),
                  writes=[slab], q=k.pool)
            for kk in range(nk):
                kc = kb + kk
                for j, ci in enumerate(g):
                    c0, wd = chunks[ci]
                    rhs = xT[:, kc, 0:ntok] if xsl is None else xsl(kc)
                    last = (kk == nk - 1 and j == len(g) - 1)
                    k.op(k.pe, lambda e, j=j, kk=kk, c0=c0, wd=wd, rhs=rhs, kc=kc: e.matmul(
                        pst[j][0:wd, 0:ntok], lhsT=slab[:, kk, c0 - g0:c0 - g0 + wd], rhs=rhs,
                        start=(kc == 0), stop=(kc == kcs - 1)), [slab, xT], [pst[j]], inc=last)
        for j, ci in enumerate(g):
            cb(ci, pst[j], chunks[ci][1])


def rmsnorm_T(k, c, psr, hT, gainT, outT, sqring, rstd, ntok=TT, kcs=KC, dmodel=D):
    ps = psr.next()
    for kc in range(kcs):
        sq = sqring.next()
        k.op(k.act, lambda e, kc=kc, sq=sq: e.activation(out=sq[:, 0:ntok], in_=hT[:, kc, 0:ntok], func=AF.Square),
             [hT], [sq])
        k.op(k.pe, lambda e, kc=kc, sq=sq: e.matmul(ps[:, 0:ntok], lhsT=c["ones_f"].ap, rhs=sq[:, 0:ntok],
                                                    start=(kc == 0), stop=(kc == kcs - 1)), [sq, c["ones_f"]], [ps])
    k.op(k.act, lambda e: e.activation(out=rstd[:, 0:ntok], in_=ps[:, 0:ntok], func=AF.Sqrt,
                                       bias=c["eps"].ap, scale=1.0 / dmodel), [ps, c["eps"]], [rstd])
    k.op(k.dve, lambda e: e.reciprocal(rstd[:, 0:ntok], rstd[:, 0:ntok]), [rstd], [rstd])
    for kc in range(kcs):
        k.op(k.dve, lambda e, kc=kc: e.scalar_tensor_tensor(out=outT[:, kc, 0:ntok], in0=hT[:, kc, 0:ntok],
                                                            scalar=gainT[:, kc:kc + 1], in1=rstd[:, 0:ntok],
                                                            op0=ALU.mult, op1=ALU.mult), [hT, gainT, rstd], [outT])


def load_vec_T(k, name, v_ap, n):
    t = k.sb(name + "_sb", [128, n // 128], F32)
    with k.nc.allow_non_contiguous_dma(reason="small param vector"):
        k.dma(t.ap, v_ap.rearrange("(kc p) -> p kc", p=128), writes=[t])
    return t


def in_chunks(kind):
    if kind == "s5":
        nm, small = 12, 0
    elif kind == "gdn":
        nm, small = 48, 24
    else:
        nm, small = 36, 12
    ch = [(i * 128, 128) for i in range(nm)]
    if small:
        ch.append((nm * 128, small))
    base = nm * 128 + small
    ch += [(base + i * 128, 128) for i in range(4)]
    return ch, nm, small


def build_tok(prev, nxt, final):
    nc = bass.Bass("TRN2", target_bir_lowering=False)
    k = K(nc)
    dt = nc.dram_tensor
    hT_d = dt("hT", [D, NTOK], F32, kind="ExternalInput").ap()
    if prev:
        mixT_d = dt("mixT", [D, NTOK], F32, kind="ExternalInput").ap()
        if prev == "s5":
            w_glu_d = dt("w_glu", [MIXW, MIXW], F32, kind="ExternalInput").ap()
            b_glu_d = dt("b_glu", [MIXW], F32, kind="ExternalInput").ap()
        w_out_d = dt("w_out", [D, D], F32, kind="ExternalInput").ap()
        norm2_d = dt("norm2", [D], F32, kind="ExternalInput").ap()
        w_up_d = dt("w_up", [D, DFF], F32, kind="ExternalInput").ap()
        w_down_d = dt("w_down", [DFF, D], F32, kind="ExternalInput").ap()
    if nxt:
        chunks, nm, small = in_chunks(nxt)
        win = chunks[-1][0] + 128
        norm1_d = dt("norm1", [D], F32, kind="ExternalInput").ap()
        w_in_d = dt("w_in", [D, win], F32, kind="ExternalInput").ap()
        memT_d = dt("memT", [D, 256], F32, kind="ExternalInput").ap()
        mem_norm_d = dt("mem_norm", [D], F32, kind="ExternalInput").ap()
        w_kv_d = dt("w_mem_kv", [D, 1024], F32, kind="ExternalInput").ap()
        hT_o = dt("hT_out", [D, NTOK], F32, kind="ExternalOutput").ap()
        projT_o = dt("projT", [nm * 128, NTOK], F32, kind="ExternalOutput").ap()
        readT_o = dt("readT", [MEMW, NTOK], F32, kind="ExternalOutput").ap()
        if small:
            smallT_o = dt("smallT", [small, NTOK], F32, kind="ExternalOutput").ap()
    if final:
        norm_f_d = dt("norm_f", [D], F32, kind="ExternalInput").ap()
        outT_o = dt("outT", [D, NTOK], F32, kind="ExternalOutput").ap()

    c = make_consts(k)
    hT = k.sb("hTt", [128, KC, TT], F32)
    aT = k.sb("aT", [128, KC, TT], BF16)
    gT = k.sb("gT", [128, 64, TT], BF16)
    mixT = k.sb("mixTt", [128, KC, TT], BF16)
    rstd = k.sb("rstd", [128, TT], F32)
    sqring = Ring([k.sb("sq%d" % i, [128, TT], F32) for i in range(2)])
    string = Ring([k.sb("stg%d" % i, [128, TT], F32) for i in range(4)])
    wring = Ring([k.sb("wsl%d" % i, [128, 4, 512], BF16) for i in range(6)])
    psr = Ring([k.ps("ps%d" % i, [128, 512]) for i in range(8)])
    outs = []

    if prev:
        norm2 = load_vec_T(k, "norm2", norm2_d, D)
        if prev == "s5":
            b_glu = load_vec_T(k, "b_glu", b_glu_d, MIXW)
            y32 = gT.ap.bitcast(F32)
    if final:
        norm_f = load_vec_T(k, "norm_f", norm_f_d, D)
    if nxt:
        norm1 = load_vec_T(k, "norm1", norm1_d, D)
        mem_norm = load_vec_T(k, "mem_norm", mem_norm_d, D)
        memkT = k.sb("memkT", [128, 4, 256], BF16)
        memv = k.sb("memv", [128, 2, 512], BF16)
        qmT = k.sb("qmT", [128, 4, TT], BF16)
        pT = [k.sb("pT%d" % i, [128, TT], BF16) for i in range(2)]
        rdT = k.sb("rdT", [128, 4, TT], F32)
        k.dma(hT[:, :, 0:256], memT_d.rearrange("(kc p) n -> p kc n", p=128), writes=[hT])
        rmsnorm_T(k, c, psr, hT, mem_norm, aT, sqring, rstd, ntok=256)

        def cb_k(i, ps, wd):
            k.op(k.act, lambda e: e.copy(memkT[:, i, :], ps[:, 0:256]), [ps], [memkT])
        linear(k, psr, wring, aT, KC, w_kv_d, [(i * 128, 128) for i in range(4)], cb_k, ntok=256)
        for mb in range(2):
            ps = psr.next()
            for kb in range(0, KC, 4):
                slab = wring.next()
                k.dma(slab[:, :, :], w_kv_d[kb * 128:(kb + 4) * 128, 512:1024].rearrange("(kc p) n -> p kc n", p=128),
                      writes=[slab], q=k.pool)
                for kk in range(4):
                    kc = kb + kk
                    k.op(k.pe, lambda e, kc=kc, kk=kk, slab=slab: e.matmul(
                        ps.ap, lhsT=aT[:, kc, mb * 128:(mb + 1) * 128], rhs=slab[:, kk, :],
                        start=(kc == 0), stop=(kc == KC - 1)), [aT, slab], [ps])
            k.op(k.act, lambda e, mb=mb, ps=ps: e.copy(memv[:, mb, :], ps.ap), [ps], [memv])

    for tt in range(NTOK // TT):
        t0 = tt * TT
        k.dma(hT.ap, hT_d[:, t0:t0 + TT].rearrange("(kc p) n -> p kc n", p=128), writes=[hT])
        if prev:
            if prev == "s5":
                def yv(oc):
                    return y32[:, 2 * oc:2 * oc + 2, :].rearrange("p a b -> p (a b)")
                for oc in range(12):
                    k.dma(yv(oc), mixT_d[oc * 128:(oc + 1) * 128, t0:t0 + TT], writes=[gT])
                k.dma(aT[:, 0:12, :], mixT_d[0:MIXW, t0:t0 + TT].rearrange("(kc p) n -> p kc n", p=128), writes=[aT], q=k.pool)
                k.dma(mixT[:, 12:16, :], mixT_d[MIXW:D, t0:t0 + TT].rearrange("(kc p) n -> p kc n", p=128), writes=[mixT], q=k.pool)

                def cb_glu(i, ps, wd):
                    st = string.next()
                    k.op(k.act, lambda e: e.activation(out=st.ap, in_=ps.ap, func=AF.Sigmoid, bias=b_glu[:, i:i + 1], scale=1.0),
                         [ps, b_glu], [st])
                    k.op(k.dve, lambda e: e.tensor_tensor(mixT[:, i, :], yv(i), st.ap, ALU.mult), [gT, st], [mixT])
                linear(k, psr, wring, aT, 12, w_glu_d, [(i * 128, 128) for i in range(12)], cb_glu)
            else:
                k.dma(mixT.ap, mixT_d[:, t0:t0 + TT].rearrange("(kc p) n -> p kc n", p=128), writes=[mixT], q=k.pool)

            def cb_res(i, ps, wd):
                k.op(k.dve, lambda e: e.tensor_tensor(hT[:, i, :], hT[:, i, :], ps.ap, ALU.add), [hT, ps], [hT])
            linear(k, psr, wring, mixT, KC, w_out_d, [(i * 128, 128) for i in range(KC)], cb_res)
            rmsnorm_T(k, c, psr, hT, norm2, aT, sqring, rstd)

            def cb_up(i, ps, wd):
                st = string.next()
                k.op(k.act, lambda e: e.activation(out=st.ap, in_=ps.ap, func=AF.Relu), [ps], [st])
                k.op(k.dve, lambda e: e.tensor_tensor(gT[:, i, :], st.ap, st.ap, ALU.mult), [st], [gT])
            linear(k, psr, wring, aT, KC, w_up_d, [(i * 128, 128) for i in range(64)], cb_up)
            linear(k, psr, wring, gT, 64, w_down_d, [(i * 128, 128) for i in range(KC)], cb_res)
        if nxt:
            ho = Tl(hT_o, "hT_o")
            k.dma(hT_o[:, t0:t0 + TT].rearrange("(kc p) n -> p kc n", p=128), hT.ap, reads=[hT], writes=[ho])
            outs.append(ho)
            rmsnorm_T(k, c, psr, hT, norm1, aT, sqring, rstd)

            def cb_in(i, ps, wd):
                if i < nm:
                    st = string.next()
                    k.op(k.act, lambda e: e.copy(st.ap, ps.ap), [ps], [st])
                    o = Tl(projT_o, "projT_o")
                    k.dma(projT_o[i * 128:(i + 1) * 128, t0:t0 + TT], st.ap, reads=[st], writes=[o])
                    outs.append(o)
                elif small and i == nm:
                    st = string.next()
                    k.op(k.act, lambda e: e.copy(st[0:wd, :], ps[0:wd, :]), [ps], [st])
                    o = Tl(smallT_o, "smallT_o")
                    k.dma(smallT_o[:, t0:t0 + TT], st[0:wd, :], reads=[st], writes=[o])
                    outs.append(o)
                else:
                    j = i - nm - (1 if small else 0)
                    k.op(k.act, lambda e: e.copy(qmT[:, j, :], ps.ap), [ps], [qmT])
            linear(k, psr, wring, aT, KC, w_in_d, chunks, cb_in)
            for hm in range(4):
                for mb in range(2):
                    ps = psr.next()
                    k.op(k.pe, lambda e, ps=ps, mb=mb: e.matmul(ps.ap, lhsT=memkT[:, hm, mb * 128:(mb + 1) * 128],
                                                                rhs=qmT[:, hm, :], start=True, stop=True),
                         [memkT, qmT], [ps])
                    k.op(k.act, lambda e, ps=ps, mb=mb: e.activation(out=pT[mb].ap, in_=ps.ap, func=AF.Exp, scale=SCALE),
                         [ps], [pT[mb]])
                pso = psr.next()
                psz = psr.next()
                for mb in range(2):
                    k.op(k.pe, lambda e, mb=mb: e.matmul(pso.ap, lhsT=memv[:, mb, hm * 128:(hm + 1) * 128], rhs=pT[mb].ap,
                                                         start=(mb == 0), stop=(mb == 1)), [memv, pT[mb]], [pso])
                for mb in range(2):
                    k.op(k.pe, lambda e, mb=mb: e.matmul(psz.ap, lhsT=c["ones_b"].ap, rhs=pT[mb].ap,
                                                         start=(mb == 0), stop=(mb == 1)), [c["ones_b"], pT[mb]], [psz])
                k.op(k.dve, lambda e: e.reciprocal(rstd.ap, psz.ap), [psz], [rstd])
                k.op(k.dve, lambda e: e.tensor_tensor(rdT[:, hm, :], pso.ap, rstd.ap, ALU.mult), [pso, rstd], [rdT])
            o = Tl(readT_o, "readT_o")
            k.dma(readT_o[:, t0:t0 + TT].rearrange("(kc p) n -> p kc n", p=128), rdT.ap, reads=[rdT], writes=[o])
            outs.append(o)
        if final:
            ps = psr.next()
            for kc in range(KC):
                sq = sqring.next()
                k.op(k.act, lambda e, kc=kc, sq=sq: e.activation(out=sq.ap, in_=hT[:, kc, :], func=AF.Square), [hT], [sq])
                k.op(k.pe, lambda e, kc=kc, sq=sq: e.matmul(ps.ap, lhsT=c["ones_f"].ap, rhs=sq.ap, start=(kc == 0),
                                                            stop=(kc == KC - 1)), [sq, c["ones_f"]], [ps])
            k.op(k.act, lambda e: e.activation(out=rstd.ap, in_=ps.ap, func=AF.Sqrt, bias=c["eps"].ap, scale=1.0 / D),
                 [ps, c["eps"]], [rstd])
            k.op(k.dve, lambda e: e.reciprocal(rstd.ap, rstd.ap), [rstd], [rstd])
            for kc in range(KC):
                st = string.next()
                k.op(k.dve, lambda e, kc=kc, st=st: e.scalar_tensor_tensor(out=st.ap, in0=hT[:, kc, :],
                                                                          scalar=norm_f[:, kc:kc + 1], in1=rstd.ap,
                                                                          op0=ALU.mult, op1=ALU.mult),
                     [hT, norm_f, rstd], [st])
                o = Tl(outT_o, "outT_o")
                k.dma(outT_o[kc * 128:(kc + 1) * 128, t0:t0 + TT], st.ap, reads=[st], writes=[o])
                outs.append(o)
    k.finish(outs)
    k.close()
    return nc


NEGBIG = -30000.0


def build_fox(nh=3, seq=SEQ):
    nc = bass.Bass("TRN2", target_bir_lowering=False)
    k = K(nc)
    dt = nc.dram_tensor
    qT_d = dt("qT", [nh, 128, seq], F32, kind="ExternalInput").ap()
    kT_d = dt("kT", [nh, 128, seq], F32, kind="ExternalInput").ap()
    v_d = dt("v", [nh, seq, 128], F32, kind="ExternalInput").ap()
    fl_d = dt("fl", [nh, seq], F32, kind="ExternalInput").ap()
    bf_d = dt("b_f", [nh], F32, kind="ExternalInput").ap()
    oT_o = dt("oT", [nh, 128, seq], F32, kind="ExternalOutput").ap()
    nb = seq // 128
    nqt = seq // TT
    c = make_consts(k)
    qTb = k.sb("qTb", [128, seq], BF16)
    kTb = k.sb("kTb", [128, seq], BF16)
    vb = k.sb("vb", [128, nb, 128], BF16)
    rowA = k.sb("rowA", [1, seq], F32)
    rowB = k.sb("rowB", [1, seq], F32)
    nbf = k.sb("nbf", [1, 4], F32)
    ncfT = k.sb("ncfT", [128, nb], F32)
    cfq = k.sb("cfq", [128, TT], F32)
    cfqm = [k.sb("cfqm%d" % i, [128, TT], F32) for i in range(4)]
    neg = [k.sb("neg%d" % i, [128, TT], F32) for i in range(4)]
    ering = Ring([k.sb("ein%d" % i, [128, TT], F32) for i in range(5)])
    pring = Ring([k.sb("pT%d" % i, [128, TT], BF16) for i in range(5)])
    oring = Ring([k.sb("ost%d" % i, [128, TT], F32) for i in range(2)])
    rz = k.sb("rz", [128, TT], F32)
    psS = Ring([k.ps("psS%d" % i, [128, TT]) for i in range(4)])
    psO = Ring([k.ps("psO%d" % i, [128, TT]) for i in range(2)])
    psZ = Ring([k.ps("psZ%d" % i, [128, TT]) for i in range(2)])
    outs = []
    for d in range(4):
        k.op(k.pool, lambda e, d=d: e.memset(neg[d].ap, 0.0), [], [neg[d]])
        k.op(k.pool, lambda e, d=d: e.affine_select(out=neg[d].ap, in_=neg[d].ap, pattern=[[1, TT]],
                                                    compare_op=ALU.is_ge, fill=NEGBIG, base=-d * 128,
                                                    channel_multiplier=-1), [neg[d]], [neg[d]])
    for h in range(nh):
        k.dma(qTb.ap, qT_d[h], writes=[qTb], q=k.pool)
        k.dma(kTb.ap, kT_d[h], writes=[kTb], q=k.pool)
        k.dma(vb.ap, v_d[h].rearrange("(j p) d -> p j d", p=128), writes=[vb], q=k.pool)
        k.dma(rowA.ap, fl_d[h:h + 1, :], writes=[rowA])
        k.dma(nbf[0:1, 0:1], bf_d[h:h + 1].rearrange("(a b) -> a b", a=1), writes=[nbf])
        k.op(k.dve, lambda e: e.tensor_scalar(nbf[0:1, 1:2], nbf[0:1, 0:1], -1.0, None, ALU.mult), [nbf], [nbf])
        k.op(k.act, lambda e: e.activation(out=rowB.ap, in_=rowA.ap, func=AF.Exp, bias=nbf[0:1, 1:2], scale=-1.0),
             [rowA, nbf], [rowB])
        k.op(k.act, lambda e: e.activation(out=rowB.ap, in_=rowB.ap, func=AF.Ln, bias=c["ones_f"][0:1, 0:1], scale=1.0),
             [rowB, c["ones_f"]], [rowB])
        k.op(k.dve, lambda e: e.tensor_tensor_scan(rowA.ap, c["ones_f"][0:1, 0:1].broadcast_to([1, seq]), rowB.ap, 0.0,
                                                   ALU.mult, ALU.subtract), [rowB, c["ones_f"]], [rowA])
        ps = psS.next()
        for j in range(nb):
            k.op(k.pe, lambda e, j=j: e.matmul(ps[:, j:j + 1], lhsT=rowA[0:1, j * 128:(j + 1) * 128],
                                               rhs=c["ones_f"][0:1, 0:1], start=True, stop=True),
                 [rowA, c["ones_f"]], [ps])
        k.op(k.dve, lambda e: e.tensor_scalar(ncfT.ap, ps[:, 0:nb], -1.0, None, ALU.mult), [ps], [ncfT])
        for qt in range(nqt):
            q0 = qt * TT
            ps = psS.next()
            k.op(k.pe, lambda e: e.matmul(ps.ap, lhsT=c["ones_f"][0:1, :], rhs=rowA[0:1, q0:q0 + TT], start=True, stop=True),
                 [rowA, c["ones_f"]], [ps])
            k.op(k.act, lambda e: e.copy(cfq.ap, ps.ap), [ps], [cfq])
            for d in range(4):
                k.op(k.pool, lambda e, d=d: e.tensor_tensor(cfqm[d].ap, cfq.ap, neg[d].ap, ALU.add), [cfq, neg[d]], [cfqm[d]])
            po = psO.next()
            pz = psZ.next()
            njb = 4 * (qt + 1)
            LA = 3
            pend = []
            for j in range(njb + LA):
                if j < njb:
                    ps = psS.next()
                    k.op(k.pe, lambda e, j=j, ps=ps: e.matmul(ps.ap, lhsT=kTb[:, j * 128:(j + 1) * 128], rhs=qTb[:, q0:q0 + TT],
                                                              start=True, stop=True), [kTb, qTb], [ps])
                    add = cfq if j < 4 * qt else cfqm[j - 4 * qt]
                    ein = ering.next()
                    k.op(k.dve, lambda e, ps=ps, add=add, ein=ein: e.scalar_tensor_tensor(
                        out=ein.ap, in0=ps.ap, scalar=SCALE, in1=add.ap, op0=ALU.mult, op1=ALU.add), [ps, add], [ein])
                    pT = pring.next()
                    k.op(k.act, lambda e, j=j, ein=ein, pT=pT: e.activation(out=pT.ap, in_=ein.ap, func=AF.Exp,
                                                                           bias=ncfT[:, j:j + 1], scale=1.0),
                         [ein, ncfT], [pT])
                    pend.append(pT)
                if j >= LA:
                    jj = j - LA
                    pT = pend[jj]
                    k.op(k.pe, lambda e, jj=jj, pT=pT: e.matmul(po.ap, lhsT=vb[:, jj, :], rhs=pT.ap, start=(jj == 0),
                                                                stop=(jj == njb - 1)), [vb, pT], [po])
                    k.op(k.pe, lambda e, jj=jj, pT=pT: e.matmul(pz.ap, lhsT=c["ones_b"].ap, rhs=pT.ap, start=(jj == 0),
                                                                stop=(jj == njb - 1)), [c["ones_b"], pT], [pz])
            k.op(k.dve, lambda e: e.reciprocal(rz.ap, pz.ap), [pz], [rz])
            ost = oring.next()
            k.op(k.dve, lambda e, ost=ost: e.tensor_tensor(ost.ap, po.ap, rz.ap, ALU.mult), [po, rz], [ost])
            o = Tl(oT_o, "oT_o")
            k.dma(oT_o[h, :, q0:q0 + TT], ost.ap, reads=[ost], writes=[o])
            outs.append(o)
    k.finish(outs)
    k.close()
    return nc


CL = 64


def build_gdn(nh=3, seq=SEQ):
    nc = bass.Bass("TRN2", target_bir_lowering=False)
    k = K(nc)
    dt = nc.dram_tensor
    nch = seq // CL
    ngrp = seq // TT
    qkvT_d = dt("qkvT", [nh, 3, 128, seq], F32, kind="ExternalInput").ap()
    gateT_d = dt("gateT", [nh, 128, seq], F32, kind="ExternalInput").ap()
    cw_d = dt("cw", [nh, 3, 128, 4], F32, kind="ExternalInput").ap()
    abrow_d = dt("abrow", [nh, 2, seq], F32, kind="ExternalInput").ap()
    abcol_d = dt("abcol", [nh, 2, CL, nch], F32, kind="ExternalInput").ap()
    hp_d = dt("hp", [nh, 2], F32, kind="ExternalInput").ap()
    onorm_d = dt("o_norm", [128], F32, kind="ExternalInput").ap()
    oT_o = dt("oT", [nh, 128, seq], F32, kind="ExternalOutput").ap()

    c = make_consts(k)
    ident = c["ident"]
    raw = k.sb("raw", [128, seq], F32)
    acc = k.sb("acc", [128, seq], F32)
    rows = k.sb("rows", [65, seq], F32)
    qT = k.sb("qT", [128, seq], BF16)
    kT = k.sb("kT", [128, seq], BF16)
    vT = k.sb("vT", [128, seq], BF16)
    cw = k.sb("cw_sb", [128, 3, 4], F32)
    hp = k.sb("hp_sb", [128, 4], F32)
    onorm = k.sb("onorm_sb", [128, 1], F32)
    acol = k.sb("acol", [CL, nch], F32)
    bcol = k.sb("bcol", [CL, nch], F32)
    gccol = k.sb("gccol", [CL, nch], F32)
    egcol = k.sb("egcol", [CL, nch], F32)
    edcol = k.sb("edcol", [CL, nch], F32)
    glast = k.sb("glast", [128, nch], F32)
    triU = k.sb("triU", [CL, CL], F32)
    maskU = k.sb("maskU", [CL, CL], F32)
    maskL = k.sb("maskL", [CL, CL], F32)
    sU01 = k.sb("sU01", [CL, CL], F32)
    S = k.sb("S", [128, 128], F32)
    Sb = k.sb("Sb", [128, 128], BF16)
    sq = Ring([k.sb("sq%d" % i, [128, TT], F32) for i in range(2)])
    rstd = k.sb("rstd", [128, TT], F32)
    qdG = k.sb("qdG", [128, TT], BF16)
    kbG = k.sb("kbG", [128, TT], BF16)
    GBs = k.sb("GBs", [CL, TT], F32)
    gst = Ring([k.sb("gst%d" % i, [128, TT], F32) for i in range(2)])
    ost = Ring([k.sb("ost%d" % i, [128, TT], F32) for i in range(2)])

    def small(name, shape, dtp, n=5):
        return Ring([k.sb("%s%d" % (name, i), shape, dtp) for i in range(n)])
    vb_r = small("vb", [CL, 128], BF16)
    kbd_r = small("kbd", [CL, 128], BF16)
    kdec_r = small("kdec", [CL, 128], BF16)
    tmp_r = small("tmp", [CL, CL], F32, 4)
    dec_r = small("dec", [CL, CL], F32)
    decT_r = small("decT", [CL, CL], F32)
    decTs_r = small("decTs", [CL, CL], F32)
    P_r = small("P", [CL, CL], F32, 9)
    Pt_r = small("Pt", [CL, CL], F32, 9)
    Tt_r = small("Tt", [CL, CL], F32, 9)
    Ttb_r = small("Ttb", [CL, CL], BF16)
    attT_r = small("attT", [CL, CL], BF16)
    nwT_r = small("nwT", [128, CL], BF16)
    vnew_r = small("vnew", [CL, 128], BF16, 2)
    psA = Ring([k.ps("psA%d" % i, [128, TT]) for i in range(2)])
    psB = Ring([k.ps("psB%d" % i, [128, 128]) for i in range(4)])
    psT = Ring([k.ps("psT%d" % i, [CL, 128], BF16) for i in range(2)])
    outs = []

    k.op(k.pool, lambda e: e.memset(triU.ap, 1.0), [], [triU])
    k.op(k.pool, lambda e: e.affine_select(out=triU.ap, in_=triU.ap, pattern=[[1, CL]], compare_op=ALU.is_ge, fill=0.0,
                                           base=0, channel_multiplier=-1), [triU], [triU])
    k.op(k.pool, lambda e: e.memset(maskU.ap, 0.0), [], [maskU])
    k.op(k.pool, lambda e: e.affine_select(out=maskU.ap, in_=maskU.ap, pattern=[[1, CL]], compare_op=ALU.is_ge,
                                           fill=NEGBIG, base=0, channel_multiplier=-1), [maskU], [maskU])
    k.op(k.pool, lambda e: e.memset(maskL.ap, 0.0), [], [maskL])
    k.op(k.pool, lambda e: e.affine_select(out=maskL.ap, in_=maskL.ap, pattern=[[-1, CL]], compare_op=ALU.is_ge,
                                           fill=-NEGBIG, base=-1, channel_multiplier=1), [maskL], [maskL])
    k.op(k.pool, lambda e: e.memset(sU01.ap, 1.0), [], [sU01])
    k.op(k.pool, lambda e: e.affine_select(out=sU01.ap, in_=sU01.ap, pattern=[[1, CL]], compare_op=ALU.is_ge, fill=0.0,
                                           base=-1, channel_multiplier=-1), [sU01], [sU01])
    with nc.allow_non_contiguous_dma(reason="small params"):
        k.dma(onorm.ap, onorm_d.rearrange("(p a) -> p a", a=1), writes=[onorm])

    for h in range(nh):
        with nc.allow_non_contiguous_dma(reason="small params"):
            k.dma(cw.ap, cw_d[h].rearrange("t p j -> p t j"), writes=[cw])
            k.dma(hp[:, 0:2], hp_d[h:h + 1, :].partition_broadcast(128), writes=[hp])
        k.op(k.act, lambda e: e.activation(out=hp[:, 2:3], in_=hp[:, 0:1], func=AF.Exp), [hp], [hp])
        k.op(k.dve, lambda e: e.tensor_scalar(hp[:, 2:3], hp[:, 2:3], -1.0, None, ALU.mult), [hp], [hp])
        k.dma(acol.ap, abcol_d[h, 0], writes=[acol])
        k.dma(bcol.ap, abcol_d[h, 1], writes=[bcol])
        k.op(k.act, lambda e: e.activation(out=bcol.ap, in_=bcol.ap, func=AF.Sigmoid), [bcol], [bcol])
        k.op(k.act, lambda e: e.activation(out=acol.ap, in_=acol.ap, func=AF.Exp, bias=hp[0:CL, 1:2], scale=1.0), [acol, hp], [acol])
        k.op(k.act, lambda e: e.activation(out=acol.ap, in_=acol.ap, func=AF.Ln, bias=c["ones_f"][0:CL, 0:1], scale=1.0),
             [acol, c["ones_f"]], [acol])
        k.op(k.dve, lambda e: e.tensor_scalar(acol.ap, acol.ap, hp[0:CL, 2:3], None, ALU.mult), [acol, hp], [acol])
        ps = psB.next()
        k.op(k.pe, lambda e: e.matmul(ps[0:CL, 0:nch], lhsT=triU.ap, rhs=acol.ap, start=True, stop=True), [triU, acol], [ps])
        k.op(k.dve, lambda e: e.tensor_copy(gccol.ap, ps[0:CL, 0:nch]), [ps], [gccol])
        ps2 = psB.next()
        k.op(k.pe, lambda e: e.matmul(ps2[:, 0:nch], lhsT=c["ones_f"][0:CL, :], rhs=acol.ap, start=True, stop=True),
             [c["ones_f"], acol], [ps2])
        k.op(k.act, lambda e: e.activation(out=glast.ap, in_=ps2[:, 0:nch], func=AF.Exp), [ps2], [glast])
        k.op(k.dve, lambda e: e.tensor_tensor(edcol.ap, ps2[0:CL, 0:nch], gccol.ap, ALU.subtract), [ps2, gccol], [edcol])
        k.op(k.act, lambda e: e.activation(out=edcol.ap, in_=edcol.ap, func=AF.Exp), [edcol], [edcol])
        k.op(k.act, lambda e: e.activation(out=egcol.ap, in_=gccol.ap, func=AF.Exp), [gccol], [egcol])
        k.op(k.dve, lambda e: e.tensor_tensor(egcol.ap, egcol.ap, bcol.ap, ALU.mult), [egcol, bcol], [egcol])
        k.dma(rows[32:33, :], abrow_d[h, 0:1, :], writes=[rows])
        k.dma(rows[0:1, :], abrow_d[h, 1:2, :], writes=[rows])
        k.op(k.act, lambda e: e.activation(out=rows[0:1, :], in_=rows[0:1, :], func=AF.Sigmoid), [rows], [rows])
        k.op(k.act, lambda e: e.activation(out=rows[32:33, :], in_=rows[32:33, :], func=AF.Exp, bias=hp[32:33, 1:2], scale=1.0),
             [rows, hp], [rows])
        k.op(k.act, lambda e: e.activation(out=rows[32:33, :], in_=rows[32:33, :], func=AF.Ln, bias=c["ones_f"][32:33, 0:1],
                                           scale=1.0), [rows, c["ones_f"]], [rows])
        k.op(k.dve, lambda e: e.tensor_scalar(rows[32:33, :], rows[32:33, :], hp[32:33, 2:3], None, ALU.mult), [rows, hp], [rows])
        k.op(k.pool, lambda e: e.memset(raw[32:33, :], 1.0), [raw], [raw])
        k.op(k.pool, lambda e: e.memset(raw[32:33, :].rearrange("p (n c) -> p n c", c=CL)[:, :, 0:1], 0.0), [raw], [raw])
        k.op(k.dve, lambda e: e.tensor_tensor_scan(rows[32:33, :], raw[32:33, :], rows[32:33, :], 0.0, ALU.mult, ALU.add),
             [rows, raw], [rows])
        k.op(k.act, lambda e: e.activation(out=rows[64:65, :], in_=rows[32:33, :], func=AF.Exp), [rows], [rows])
        for ti, dst in enumerate((qT, kT, vT)):
            k.dma(raw.ap, qkvT_d[h, ti], writes=[raw])
            k.op(k.dve, lambda e: e.tensor_scalar(acc.ap, raw.ap, cw[:, ti, 3:4], None, ALU.mult), [raw, cw], [acc])
            for s in (1, 2, 3):
                k.op(k.dve, lambda e, s=s: e.scalar_tensor_tensor(out=acc[:, s:seq], in0=raw[:, 0:seq - s],
                                                                 scalar=cw[:, ti, 3 - s:4 - s], in1=acc[:, s:seq],
                                                                 op0=ALU.mult, op1=ALU.add), [raw, cw, acc], [acc])
            if ti == 2:
                k.op(k.act, lambda e: e.activation(out=vT.ap, in_=acc.ap, func=AF.Silu), [acc], [vT])
                continue
            k.op(k.act, lambda e: e.activation(out=acc.ap, in_=acc.ap, func=AF.Silu), [acc], [acc])
            for g in range(ngrp):
                sl = slice(g * TT, (g + 1) * TT)
                s_ = sq.next()
                k.op(k.act, lambda e, s_=s_, sl=sl: e.activation(out=s_.ap, in_=acc[:, sl], func=AF.Square), [acc], [s_])
                ps = psA.next()
                k.op(k.pe, lambda e, s_=s_, ps=ps: e.matmul(ps.ap, lhsT=c["ones_f"].ap, rhs=s_.ap, start=True, stop=True),
                     [s_, c["ones_f"]], [ps])
                k.op(k.act, lambda e, ps=ps: e.activation(out=rstd.ap, in_=ps.ap, func=AF.Sqrt, bias=c["eps"].ap, scale=1.0),
                     [ps, c["eps"]], [rstd])
                k.op(k.dve, lambda e: e.reciprocal(rstd.ap, rstd.ap), [rstd], [rstd])
                if ti == 0:
                    k.op(k.dve, lambda e, sl=sl: e.scalar_tensor_tensor(out=qT[:, sl], in0=acc[:, sl], scalar=SCALE, in1=rstd.ap,
                                                                       op0=ALU.mult, op1=ALU.mult), [acc, rstd], [qT])
                else:
                    k.op(k.dve, lambda e, sl=sl: e.tensor_tensor(kT[:, sl], acc[:, sl], rstd.ap, ALU.mult), [acc, rstd], [kT])
        k.op(k.dve, lambda e: e.memset(S.ap, 0.0), [], [S])
        k.op(k.act, lambda e: e.copy(Sb.ap, S.ap), [S], [Sb])
        for g in range(ngrp):
            sl = slice(g * TT, (g + 1) * TT)
            ps = psA.next()
            k.op(k.pe, lambda e, ps=ps: e.matmul(ps.ap, lhsT=c["ones_f"][64:65, :], rhs=rows[64:65, sl], start=True, stop=True),
                 [rows, c["ones_f"]], [ps])
            k.op(k.dve, lambda e, ps=ps: e.tensor_tensor(qdG.ap, qT[:, sl], ps.ap, ALU.mult), [qT, ps], [qdG])
            ps = psA.next()
            k.op(k.pe, lambda e, ps=ps: e.matmul(ps.ap, lhsT=c["ones_f"][0:1, :], rhs=rows[0:1, sl], start=True, stop=True),
                 [rows, c["ones_f"]], [ps])
            k.op(k.dve, lambda e, ps=ps: e.tensor_tensor(kbG.ap, kT[:, sl], ps.ap, ALU.mult), [kT, ps], [kbG])
            ps = psA.next()
            k.op(k.pe, lambda e, ps=ps: e.matmul(ps[0:CL, :], lhsT=c["ones_f"][32:33, 0:CL], rhs=rows[32:33, sl], start=True, stop=True),
                 [rows, c["ones_f"]], [ps])
            k.op(k.act, lambda e, ps=ps: e.copy(GBs.ap, ps[0:CL, :]), [ps], [GBs])
            NI = 4
            for sg in range(TT // CL // NI):
                cis = [sg * NI + u for u in range(NI)]
                ns = [g * (TT // CL) + ci for ci in cis]
                css = [slice(g * TT + ci * CL, g * TT + (ci + 1) * CL) for ci in cis]
                gss = [slice(ci * CL, (ci + 1) * CL) for ci in cis]
                X = [dict() for _ in cis]
                for u in range(NI):
                    n, cs, T = ns[u], css[u], X[u]
                    pk = psT.next()
                    k.op(k.pe, lambda e: e.transpose(pk.ap, kT[:, cs], c["ident_b"].ap), [kT, c["ident_b"]], [pk])
                    T["kbd"], T["kdec"], T["vb"] = kbd_r.next(), kdec_r.next(), vb_r.next()
                    k.op(k.dve, lambda e: e.tensor_scalar(T["kbd"].ap, pk.ap, egcol[:, n:n + 1], None, ALU.mult), [pk, egcol], [T["kbd"]])
                    k.op(k.act, lambda e: e.activation(out=T["kdec"].ap, in_=pk.ap, func=AF.Copy, scale=edcol[:, n:n + 1]),
                         [pk, edcol], [T["kdec"]])
                    pv = psT.next()
                    k.op(k.pe, lambda e: e.transpose(pv.ap, vT[:, cs], c["ident_b"].ap), [vT, c["ident_b"]], [pv])
                    k.op(k.act, lambda e: e.activation(out=T["vb"].ap, in_=pv.ap, func=AF.Copy, scale=bcol[:, n:n + 1]),
                         [pv, bcol], [T["vb"]])
                for u in range(NI):
                    n, gs, T = ns[u], gss[u], X[u]
                    t1 = tmp_r.next()
                    k.op(k.dve, lambda e: e.scalar_tensor_tensor(out=t1.ap, in0=GBs[:, gs], scalar=gccol[:, n:n + 1], in1=maskU.ap,
                                                                 op0=ALU.subtract, op1=ALU.add), [GBs, gccol, maskU], [t1])
                    T["decT"] = decT_r.next()
                    k.op(k.act, lambda e: e.activation(out=T["decT"].ap, in_=t1.ap, func=AF.Exp), [t1], [T["decT"]])
                    t2 = tmp_r.next()
                    k.op(k.dve, lambda e: e.scalar_tensor_tensor(out=t2.ap, in0=GBs[:, gs], scalar=gccol[:, n:n + 1], in1=maskL.ap,
                                                                 op0=ALU.subtract, op1=ALU.add), [GBs, gccol, maskL], [t2])
                    T["dec"] = dec_r.next()
                    k.op(k.act, lambda e: e.activation(out=T["dec"].ap, in_=t2.ap, func=AF.Exp, scale=-1.0), [t2], [T["dec"]])
                    T["decTs"] = decTs_r.next()
                    k.op(k.pool, lambda e: e.tensor_tensor(T["decTs"].ap, T["decT"].ap, sU01.ap, ALU.mult), [T["decT"], sU01], [T["decTs"]])
                for u in range(NI):
                    cs, gs, T = css[u], gss[u], X[u]
                    pL = psB.next()
                    k.op(k.pe, lambda e: e.matmul(pL[0:CL, 0:CL], lhsT=kbG[:, gs], rhs=kT[:, cs], start=True, stop=True), [kbG, kT], [pL])
                    T["P"] = P_r.next()
                    k.op(k.dve, lambda e: e.scalar_tensor_tensor(out=T["P"].ap, in0=pL[0:CL, 0:CL], scalar=-1.0, in1=T["dec"].ap,
                                                                 op0=ALU.mult, op1=ALU.mult), [pL, T["dec"]], [T["P"]])
                    pLt = psB.next()
                    k.op(k.pe, lambda e: e.matmul(pLt[0:CL, 0:CL], lhsT=kT[:, cs], rhs=kbG[:, gs], start=True, stop=True), [kbG, kT], [pLt])
                    T["Pt"] = Pt_r.next()
                    k.op(k.dve, lambda e: e.scalar_tensor_tensor(out=T["Pt"].ap, in0=pLt[0:CL, 0:CL], scalar=-1.0, in1=T["decTs"].ap,
                                                                 op0=ALU.mult, op1=ALU.mult), [pLt, T["decTs"]], [T["Pt"]])
                    pA = psB.next()
                    k.op(k.pe, lambda e: e.matmul(pA[0:CL, 0:CL], lhsT=kT[:, cs], rhs=qT[:, cs], start=True, stop=True), [qT, kT], [pA])
                    T["attT"] = attT_r.next()
                    k.op(k.dve, lambda e: e.tensor_tensor(T["attT"].ap, pA[0:CL, 0:CL], T["decT"].ap, ALU.mult), [pA, T["decT"]], [T["attT"]])
                    T["Tt"] = Tt_r.next()
                    k.op(k.pool, lambda e: e.tensor_tensor(T["Tt"].ap, T["Pt"].ap, ident[0:CL, 0:CL], ALU.add), [T["Pt"], ident], [T["Tt"]])
                for lev in range(1, 6):
                    for u in range(NI):
                        T = X[u]
                        p1 = psB.next()
                        k.op(k.pe, lambda e: e.matmul(p1[0:CL, 0:CL], lhsT=T["Pt"].ap, rhs=T["P"].ap, start=True, stop=True),
                             [T["Pt"], T["P"]], [p1])
                        T["Pn"] = P_r.next()
                        k.op(k.act, lambda e: e.copy(T["Pn"].ap, p1[0:CL, 0:CL]), [p1], [T["Pn"]])
                    if lev < 5:
                        for u in range(NI):
                            T = X[u]
                            p2 = psB.next()
                            k.op(k.pe, lambda e: e.matmul(p2[0:CL, 0:CL], lhsT=T["P"].ap, rhs=T["Pt"].ap, start=True, stop=True),
                                 [T["Pt"], T["P"]], [p2])
                            T["Ptn"] = Pt_r.next()
                            k.op(k.act, lambda e: e.copy(T["Ptn"].ap, p2[0:CL, 0:CL]), [p2], [T["Ptn"]])
                    for u in range(NI):
                        T = X[u]
                        p3 = psB.next()
                        k.op(k.pe, lambda e: e.matmul(p3[0:CL, 0:CL], lhsT=T["Pn"].ap, rhs=T["Tt"].ap, start=True, stop=True),
                             [T["Pn"], T["Tt"]], [p3])
                        Ttn = Tt_r.next()
                        k.op(k.dve, lambda e: e.tensor_tensor(Ttn.ap, T["Tt"].ap, p3[0:CL, 0:CL], ALU.add), [T["Tt"], p3], [Ttn])
                        T["P"], T["Pt"], T["Tt"] = T["Pn"], T.get("Ptn"), Ttn
                for u in range(NI):
                    T = X[u]
                    T["Ttb"] = Ttb_r.next()
                    k.op(k.act, lambda e: e.copy(T["Ttb"].ap, T["Tt"].ap), [T["Tt"]], [T["Ttb"]])
                    pw = psB.next()
                    k.op(k.pe, lambda e: e.matmul(pw[:, 0:CL], lhsT=T["kbd"].ap, rhs=T["Ttb"].ap, start=True, stop=True),
                         [T["kbd"], T["Ttb"]], [pw])
                    T["nwT"] = nwT_r.next()
                    k.op(k.act, lambda e: e.activation(out=T["nwT"].ap, in_=pw[:, 0:CL], func=AF.Copy, scale=-1.0), [pw], [T["nwT"]])
                for u in range(NI):
                    n, cs, gs, T = ns[u], css[u], gss[u], X[u]
                    pvn = psB.next()
                    k.op(k.pe, lambda e: e.matmul(pvn[0:CL, :], lhsT=T["Ttb"].ap, rhs=T["vb"].ap, start=True, stop=False),
                         [T["Ttb"], T["vb"]], [pvn])
                    k.op(k.pe, lambda e: e.matmul(pvn[0:CL, :], lhsT=T["nwT"].ap, rhs=Sb.ap, start=False, stop=True), [T["nwT"], Sb], [pvn])
                    vnew = vnew_r.next()
                    k.op(k.act, lambda e: e.copy(vnew.ap, pvn[0:CL, :]), [pvn], [vnew])
                    po = psB.next()
                    k.op(k.pe, lambda e: e.matmul(po[:, 0:CL], lhsT=Sb.ap, rhs=qdG[:, gs], start=True, stop=False), [Sb, qdG], [po])
                    k.op(k.pe, lambda e: e.matmul(po[:, 0:CL], lhsT=vnew.ap, rhs=T["attT"].ap, start=False, stop=True), [vnew, T["attT"]], [po])
                    k.op(k.dve, lambda e: e.tensor_copy(acc[:, cs], po[:, 0:CL]), [po], [acc])
                    pd = psB.next()
                    k.op(k.pe, lambda e: e.matmul(pd.ap, lhsT=T["kdec"].ap, rhs=vnew.ap, start=True, stop=True), [T["kdec"], vnew], [pd])
                    k.op(k.dve, lambda e: e.scalar_tensor_tensor(out=S.ap, in0=S.ap, scalar=glast[:, n:n + 1], in1=pd.ap,
                                                                 op0=ALU.mult, op1=ALU.add), [S, glast, pd], [S])
                    k.op(k.act, lambda e: e.copy(Sb.ap, S.ap), [S], [Sb])
        for g in range(ngrp):
            sl = slice(g * TT, (g + 1) * TT)
            gt = gst.next()
            k.dma(gt.ap, gateT_d[h, :, sl], writes=[gt])
            k.op(k.act, lambda e, gt=gt: e.activation(out=gt.ap, in_=gt.ap, func=AF.Silu), [gt], [gt])
            s_ = sq.next()
            k.op(k.act, lambda e, s_=s_: e.activation(out=s_.ap, in_=acc[:, sl], func=AF.Square), [acc], [s_])
            ps = psA.next()
            k.op(k.pe, lambda e, s_=s_, ps=ps: e.matmul(ps.ap, lhsT=c["ones_f"].ap, rhs=s_.ap, start=True, stop=True),
                 [s_, c["ones_f"]], [ps])
            k.op(k.act, lambda e, ps=ps: e.activation(out=rstd.ap, in_=ps.ap, func=AF.Sqrt, bias=c["eps"].ap, scale=1.0 / 128),
                 [ps, c["eps"]], [rstd])
            k.op(k.dve, lambda e: e.reciprocal(rstd.ap, rstd.ap), [rstd], [rstd])
            o_ = ost.next()
            k.op(k.dve, lambda e, o_=o_: e.scalar_tensor_tensor(out=o_.ap, in0=acc[:, sl], scalar=onorm[:, 0:1], in1=rstd.ap,
                                                               op0=ALU.mult, op1=ALU.mult), [acc, onorm, rstd], [o_])
            k.op(k.dve, lambda e, o_=o_, gt=gt: e.tensor_tensor(o_.ap, o_.ap, gt.ap, ALU.mult), [o_, gt], [o_])
            o = Tl(oT_o, "oT_o")
            k.dma(oT_o[h, :, sl], o_.ap, reads=[o_], writes=[o])
            outs.append(o)
    k.finish(outs)
    k.close()
    return nc


PCH = 128
TWO_PI = 2.0 * math.pi
C1_2PI = 6.28125
C2_2PI = TWO_PI - 6.28125


def build_s5(ng=24, seq=SEQ):
    nc = bass.Bass("TRN2", target_bir_lowering=False)
    k = K(nc)
    dt = nc.dram_tensor
    nfc = ng // 8
    ntile = seq // TT
    NJ = PCH + 1
    uT_d = dt("uT", [nfc, 128, seq], F32, kind="ExternalInput").ap()
    lre_d = dt("lam_re", [ng, 64], F32, kind="ExternalInput").ap()
    lim_d = dt("lam_im", [ng, 64], F32, kind="ExternalInput").ap()
    ldt_d = dt("log_dt", [1, ng], F32, kind="ExternalInput").ap()
    bre_d = dt("b_re", [ng, 64, 16], F32, kind="ExternalInput").ap()
    bim_d = dt("b_im", [ng, 64, 16], F32, kind="ExternalInput").ap()
    cre_d = dt("c_re", [ng * 16, 64], F32, kind="ExternalInput").ap()
    cim_d = dt("c_im", [ng * 16, 64], F32, kind="ExternalInput").ap()
    dsk_d = dt("d_skip", [nfc * 128], F32, kind="ExternalInput").ap()
    yT_o = dt("yT", [nfc, 128, seq], F32, kind="ExternalOutput").ap()

    c = make_consts(k)
    ident = c["ident"]
    uTb = [k.sb("uTb%d" % i, [128, seq], BF16) for i in range(nfc)]
    GRall = k.sb("GRall", [128, ng, TT], F32)
    GR = [Tl(GRall[:, g, :], "GR%d" % g) for g in range(ng)]
    cosT = k.sb("cosT", [128, ng, NJ], F32)
    sinT = k.sb("sinT", [128, ng, NJ], F32)
    W1 = [k.sb("W1_%d" % g, [128, 128], BF16) for g in range(ng)]
    W2 = [k.sb("W2_%d" % g, [128, 128], BF16) for g in range(ng)]
    CA = [k.sb("CA_%d" % g, [128, 128], BF16) for g in range(ng)]
    CB = [k.sb("CB_%d" % g, [128, 128], BF16) for g in range(ng)]
    ROT = [k.sb("ROT_%d" % g, [128, 128], F32) for g in range(ng)]
    init = [k.sb("init_%d" % g, [128, 1], F32) for g in range(ng)]
    LRe = k.sb("LRe", [128, ng], F32)
    LIm = k.sb("LIm", [128, ng], F32)
    DT = k.sb("DT", [128, ng], F32)
    RHO = k.sb("RHO", [128, ng], F32)
    TH = k.sb("TH", [128, ng], F32)
    pp = [k.sb("pp%d" % i, [128, ng], F32) for i in range(8)]
    sgnA = k.sb("sgnA", [128, 1], F32)
    sgnB = k.sb("sgnB", [128, 1], F32)
    gmask = k.sb("gmask", [128, 8], F32)
    SW = k.sb("SW", [128, 128], F32)
    E1 = k.sb("E1", [128, 128], F32)
    dsk = load_vec_T(k, "dsk", dsk_d, nfc * 128)
    BA = k.sb("BA", [128, ng, 16], F32)
    BB = k.sb("BB", [128, ng, 16], F32)
    BP1 = k.sb("BP1", [128, ng, 16], F32)
    BP2 = k.sb("BP2", [128, ng, 16], F32)
    u32 = Ring([k.sb("u32_%d" % i, [128, TT], F32) for i in range(2 * nfc)])
    xs = Ring([k.sb("xs%d" % i, [128, TT], F32) for i in range(4)])
    tt_r = Ring([k.sb("tt%d" % i, [128, TT], F32) for i in range(2)])
    z_r = Ring([k.sb("z%d" % i, [128, TT], BF16) for i in range(4)])
    ost = Ring([k.sb("ost%d" % i, [128, TT], F32) for i in range(2)])
    psX = Ring([k.ps("psX%d" % i, [128, TT]) for i in range(4)])
    psY = Ring([k.ps("psY%d" % i, [128, TT]) for i in range(2)])
    psI = Ring([k.ps("psI%d" % i, [128, 128]) for i in range(2)])
    outs = []

    with nc.allow_non_contiguous_dma(reason="small params"):
        for half in range(2):
            k.dma(LRe[half * 64:(half + 1) * 64, :], lre_d.rearrange("g p -> p g"), writes=[LRe])
            k.dma(LIm[half * 64:(half + 1) * 64, :], lim_d.rearrange("g p -> p g"), writes=[LIm])
        k.dma(DT.ap, ldt_d.partition_broadcast(128), writes=[DT])
        k.dma(BA[0:64], bre_d.rearrange("g p c -> p g c"), writes=[BA])
        k.dma(BA[64:128], bim_d.rearrange("g p c -> p g c"), writes=[BA])
        k.dma(BB[0:64], bim_d.rearrange("g p c -> p g c"), writes=[BB])
        k.dma(BB[64:128], bre_d.rearrange("g p c -> p g c"), writes=[BB])
    for i in range(nfc):
        k.dma(uTb[i].ap, uT_d[i], writes=[uTb[i]], q=k.pool)
    k.op(k.act, lambda e: e.activation(out=DT.ap, in_=DT.ap, func=AF.Exp), [DT], [DT])
    k.op(k.dve, lambda e: e.tensor_tensor(RHO.ap, LRe.ap, DT.ap, ALU.mult), [LRe, DT], [RHO])
    k.op(k.act, lambda e: e.activation(out=RHO.ap, in_=RHO.ap, func=AF.Exp), [RHO], [RHO])
    k.op(k.dve, lambda e: e.tensor_tensor(TH.ap, LIm.ap, DT.ap, ALU.mult), [LIm, DT], [TH])
    k.op(k.dve, lambda e: e.memset(sgnA.ap, 1.0), [], [sgnA])
    k.op(k.dve, lambda e: e.memset(sgnA[0:64, :], -1.0), [sgnA], [sgnA])
    k.op(k.dve, lambda e: e.tensor_scalar(sgnB.ap, sgnA.ap, -1.0, None, ALU.mult), [sgnA], [sgnB])
    nel = ng * NJ
    flat = GRall.ap.rearrange("p g t -> p (g t)")
    ANG = flat[:, 0:nel].rearrange("p (g j) -> p g j", g=ng)
    TF = flat[:, nel:2 * nel].rearrange("p (g j) -> p g j", g=ng)
    NI = flat[:, 2 * nel:3 * nel].bitcast(mybir.dt.int32).rearrange("p (g j) -> p g j", g=ng)
    JJ = flat[:, 3 * nel:3 * nel + NJ]
    k.op(k.pool, lambda e: e.iota(JJ, [[1, NJ]], base=0, channel_multiplier=0, allow_small_or_imprecise_dtypes=True),
         [], [GRall])
    k.op(k.dve, lambda e: e.tensor_tensor(ANG, TH.ap.unsqueeze(2).broadcast_to([128, ng, NJ]),
                                          JJ.unsqueeze(1).broadcast_to([128, ng, NJ]), ALU.mult), [TH, GRall], [GRall])
    for shift, outT in ((0.0, sinT), (0.5 * math.pi, cosT)):
        k.op(k.dve, lambda e: e.tensor_scalar(TF, ANG, shift, 1.0 / TWO_PI, ALU.add, ALU.mult), [GRall], [GRall])
        k.op(k.dve, lambda e: e.tensor_copy(NI, TF), [GRall], [GRall])
        k.op(k.dve, lambda e: e.tensor_copy(TF, NI), [GRall], [GRall])
        k.op(k.dve, lambda e: e.scalar_tensor_tensor(out=outT.ap, in0=TF, scalar=-C1_2PI, in1=ANG, op0=ALU.mult, op1=ALU.add),
             [GRall], [outT])
        k.op(k.dve, lambda e: e.scalar_tensor_tensor(out=outT.ap, in0=TF, scalar=-C2_2PI, in1=outT.ap, op0=ALU.mult, op1=ALU.add),
             [GRall, outT], [outT])
        k.op(k.dve, lambda e: e.tensor_scalar(outT.ap, outT.ap, shift, math.pi, ALU.add, ALU.min), [outT], [outT])
        k.op(k.dve, lambda e: e.tensor_scalar(outT.ap, outT.ap, -math.pi, None, ALU.max), [outT], [outT])
        k.op(k.act, lambda e: e.activation(out=outT.ap, in_=outT.ap, func=AF.Sin), [outT], [outT])
    cth, sth = cosT[:, :, 1], sinT[:, :, 1]
    are, aim, am1, den, zre, zim, t0_, t1_ = pp

    def tt(out, a, b, op, rd, wr):
        k.op(k.dve, lambda e: e.tensor_tensor(out, a, b, op), rd, wr)
    tt(are.ap, RHO.ap, cth, ALU.mult, [RHO, cosT], [are])
    tt(aim.ap, RHO.ap, sth, ALU.mult, [RHO, sinT], [aim])
    k.op(k.dve, lambda e: e.tensor_scalar(am1.ap, are.ap, -1.0, None, ALU.add), [are], [am1])
    tt(den.ap, LRe.ap, LRe.ap, ALU.mult, [LRe], [den])
    tt(t0_.ap, LIm.ap, LIm.ap, ALU.mult, [LIm], [t0_])
    tt(den.ap, den.ap, t0_.ap, ALU.add, [den, t0_], [den])
    k.op(k.dve, lambda e: e.reciprocal(den.ap, den.ap), [den], [den])
    tt(zre.ap, am1.ap, LRe.ap, ALU.mult, [am1, LRe], [zre])
    tt(t0_.ap, aim.ap, LIm.ap, ALU.mult, [aim, LIm], [t0_])
    tt(zre.ap, zre.ap, t0_.ap, ALU.add, [zre, t0_], [zre])
    tt(zre.ap, zre.ap, den.ap, ALU.mult, [zre, den], [zre])
    tt(zim.ap, aim.ap, LRe.ap, ALU.mult, [aim, LRe], [zim])
    tt(t0_.ap, am1.ap, LIm.ap, ALU.mult, [am1, LIm], [t0_])
    tt(zim.ap, zim.ap, t0_.ap, ALU.subtract, [zim, t0_], [zim])
    tt(zim.ap, zim.ap, den.ap, ALU.mult, [zim, den], [zim])
    k.op(k.dve, lambda e: e.tensor_scalar(t0_.ap, zim.ap, sgnA[:, 0:1], None, ALU.mult), [zim, sgnA], [t0_])
    k.op(k.dve, lambda e: e.tensor_scalar(t1_.ap, zre.ap, sgnB[:, 0:1], None, ALU.mult), [zre, sgnB], [t1_])

    def bc(t):
        return t.ap.unsqueeze(2).broadcast_to([128, ng, 16])
    tmpB = k.sb("tmpB", [128, ng, 16], F32)
    tt(BP1.ap, BA.ap, bc(zre), ALU.mult, [BA, zre], [BP1])
    tt(tmpB.ap, BB.ap, bc(t0_), ALU.mult, [BB, t0_], [tmpB])
    tt(BP1.ap, BP1.ap, tmpB.ap, ALU.add, [BP1, tmpB], [BP1])
    tt(BP2.ap, BB.ap, bc(t1_), ALU.mult, [BB, t1_], [BP2])
    tt(tmpB.ap, BA.ap, bc(zim), ALU.mult, [BA, zim], [tmpB])
    tt(BP2.ap, BP2.ap, tmpB.ap, ALU.add, [BP2, tmpB], [BP2])
    k.op(k.pool, lambda e: e.memset(gmask.ap, 1.0), [], [gmask])
    k.op(k.pool, lambda e: e.affine_select(out=gmask.ap, in_=gmask.ap, pattern=[[-16, 8]], compare_op=ALU.is_ge, fill=0.0,
                                           base=0, channel_multiplier=1), [gmask], [gmask])
    k.op(k.pool, lambda e: e.affine_select(out=gmask.ap, in_=gmask.ap, pattern=[[16, 8]], compare_op=ALU.is_ge, fill=0.0,
                                           base=15, channel_multiplier=-1), [gmask], [gmask])
    k.op(k.pool, lambda e: e.memset(SW.ap, 1.0), [], [SW])
    k.op(k.pool, lambda e: e.affine_select(out=SW.ap, in_=SW.ap, pattern=[[-1, 128]], compare_op=ALU.is_equal, fill=0.0,
                                           base=64, channel_multiplier=1), [SW], [SW])
    k.op(k.pool, lambda e: e.memset(E1.ap, 1.0), [], [E1])
    k.op(k.pool, lambda e: e.affine_select(out=E1.ap, in_=E1.ap, pattern=[[-1, 128]], compare_op=ALU.is_equal, fill=0.0,
                                           base=-64, channel_multiplier=1), [E1], [E1])
    k.op(k.pool, lambda e: e.tensor_tensor(SW.ap, SW.ap, E1.ap, ALU.subtract), [SW, E1], [SW])
    cin = k.sb("cin", [128, 128], F32)
    for fc in range(nfc):
        for src, Wl in ((BP1, W1), (BP2, W2)):
            ps = psI.next()
            k.op(k.pe, lambda e: e.transpose(ps.ap, src[:, fc * 8:(fc + 1) * 8, :].rearrange("p g c -> p (g c)"), ident.ap),
                 [src, ident], [ps])
            for gg in range(8):
                g = fc * 8 + gg
                k.op(k.dve, lambda e: e.tensor_scalar(Wl[g].ap, ps.ap, gmask[:, gg:gg + 1], None, ALU.mult), [ps, gmask], [Wl[g]])
        for which, Cl in ((0, CA), (1, CB)):
            first, second = (cre_d, cim_d) if which == 0 else (cim_d, cre_d)
            k.dma(cin[:, 0:64], first[fc * 128:(fc + 1) * 128, :], writes=[cin])
            k.dma(cin[:, 64:128], second[fc * 128:(fc + 1) * 128, :], writes=[cin])
            if which == 0:
                k.op(k.dve, lambda e: e.tensor_scalar(cin[:, 64:128], cin[:, 64:128], -1.0, None, ALU.mult), [cin], [cin])
            else:
                k.op(k.dve, lambda e: e.tensor_scalar(cin.ap, cin.ap, -1.0, None, ALU.mult), [cin], [cin])
            ps = psI.next()
            k.op(k.pe, lambda e: e.transpose(ps.ap, cin.ap, ident.ap), [cin, ident], [ps])
            for gg in range(8):
                g = fc * 8 + gg
                k.op(k.pool, lambda e: e.memset(Cl[g].ap, 0.0), [], [Cl[g]])
                k.op(k.act, lambda e: e.copy(Cl[g][:, gg * 16:(gg + 1) * 16], ps[:, gg * 16:(gg + 1) * 16]), [ps, Cl[g]], [Cl[g]])
    for g in range(ng):
        k.op(k.dve, lambda e: e.tensor_scalar(E1.ap, ident.ap, cosT[:, g, PCH:PCH + 1], None, ALU.mult), [ident, cosT, E1], [E1])
        k.op(k.dve, lambda e: e.scalar_tensor_tensor(out=ROT[g].ap, in0=SW.ap, scalar=sinT[:, g, PCH:PCH + 1], in1=E1.ap,
                                                     op0=ALU.mult, op1=ALU.add), [SW, sinT, E1], [ROT[g]])
        k.op(k.pool, lambda e: e.memset(init[g].ap, 0.0), [], [init[g]])
    for g in range(ng):
        GR[g].lw = GRall.lw
        GR[g].rd = list(GRall.rd)

    npc = TT // PCH
    for ti in range(ntile):
        t0 = ti * TT
        uts = []
        for fc in range(nfc):
            ut = u32.next()
            k.dma(ut.ap, uT_d[fc, :, t0:t0 + TT], writes=[ut])
            uts.append(ut)
        for g in range(ng):
            fc = g // 8
            xss = []
            for Wl in (W1, W2):
                ps = psX.next()
                k.op(k.pe, lambda e: e.matmul(ps.ap, lhsT=Wl[g].ap, rhs=uTb[fc][:, t0:t0 + TT], start=True, stop=True),
                     [Wl[g], uTb[fc]], [ps])
                x = xs.next()
                k.op(k.act, lambda e: e.copy(x.ap, ps.ap), [ps], [x])
                xss.append(x)
            cb_ = cosT[:, g, 0:PCH].unsqueeze(1).broadcast_to([128, npc, PCH])
            sb_ = sinT[:, g, 0:PCH].unsqueeze(1).broadcast_to([128, npc, PCH])
            ta = tt_r.next()
            tb = tt_r.next()

            def v3(ap):
                return ap.rearrange("p (a b) -> p a b", a=npc)
            k.op(k.pool, lambda e: e.tensor_tensor(v3(ta.ap), v3(xss[0].ap), cb_, ALU.mult), [xss[0], cosT], [ta])
            k.op(k.pool, lambda e: e.tensor_tensor(v3(tb.ap), v3(xss[1].ap), sb_, ALU.mult), [xss[1], sinT], [tb])
            k.op(k.pool, lambda e: e.tensor_tensor(GR[g].ap, ta.ap, tb.ap, ALU.add), [ta, tb], [GR[g]])
        for pc in range(npc):
            cs = slice(pc * PCH, (pc + 1) * PCH)
            for g in range(ng):
                k.op(k.dve, lambda e: e.tensor_tensor_scan(GR[g][:, cs], RHO[:, g:g + 1].broadcast_to([128, PCH]), GR[g][:, cs],
                                                           init[g][:, 0:1], ALU.mult, ALU.add), [GR[g], RHO, init[g]], [GR[g]])
                pi = psI.next()
                k.op(k.pe, lambda e: e.matmul(pi[:, 0:1], lhsT=ROT[g].ap, rhs=GR[g][:, (pc + 1) * PCH - 1:(pc + 1) * PCH],
                                              start=True, stop=True), [ROT[g], GR[g]], [pi])
                k.op(k.act, lambda e: e.copy(init[g].ap, pi[:, 0:1]), [pi], [init[g]])
        for fc in range(nfc):
            py = psY.next()
            for gg in range(8):
                g = fc * 8 + gg
                cb_ = cosT[:, g, 0:PCH].unsqueeze(1).broadcast_to([128, npc, PCH])
                sb_ = sinT[:, g, 0:PCH].unsqueeze(1).broadcast_to([128, npc, PCH])
                z1 = z_r.next()
                z2 = z_r.next()
                k.op(k.dve, lambda e: e.tensor_tensor(v3(z1.ap), v3(GR[g].ap), cb_, ALU.mult), [GR[g], cosT], [z1])
                k.op(k.dve, lambda e: e.tensor_tensor(v3(z2.ap), v3(GR[g].ap), sb_, ALU.mult), [GR[g], sinT], [z2])
                k.op(k.pe, lambda e: e.matmul(py.ap, lhsT=CA[g].ap, rhs=z1.ap, start=(gg == 0), stop=False), [CA[g], z1], [py])
                k.op(k.pe, lambda e: e.matmul(py.ap, lhsT=CB[g].ap, rhs=z2.ap, start=False, stop=(gg == 7)), [CB[g], z2], [py])
            o_ = ost.next()
            k.op(k.dve, lambda e: e.scalar_tensor_tensor(out=o_.ap, in0=uts[fc].ap, scalar=dsk[:, fc:fc + 1], in1=py.ap,
                                                         op0=ALU.mult, op1=ALU.add), [uts[fc], dsk, py], [o_])
            k.op(k.act, lambda e: e.activation(out=o_.ap, in_=o_.ap, func=AF.Gelu), [o_], [o_])
            o = Tl(yT_o, "yT_o")
            k.dma(yT_o[fc, :, t0:t0 + TT], o_.ap, reads=[o_], writes=[o])
            outs.append(o)
    k.finish(outs)
    k.close()
    return nc


_PROGS = {}


def _prog(key, fn):
    if key not in _PROGS:
        _PROGS[key] = fn()
    return _PROGS[key]


def _run(nc, in_maps):
    res = run_bass_kernel_spmd(nc, in_maps, core_ids=list(range(8)))
    return res.results


KINDS = ["s5", "gdn", "fox"]
_C = np.ascontiguousarray


def kernel(x, mem, mem_norm, w_mem_kv, norm1, w_out, norm2, w_up, w_down, norm_f,
           s5_w_in, s5_lam_re, s5_lam_im, s5_log_dt, s5_b_re, s5_b_im, s5_c_re, s5_c_im,
           s5_d_skip, s5_w_glu, s5_b_glu,
           gdn_w_in, gdn_conv_w, gdn_a_log, gdn_dt_bias, gdn_o_norm,
           fox_w_in, fox_b_f):
    f = lambda a: np.asarray(a, dtype=np.float32)
    x, mem = f(x), f(mem)
    depth = norm1.shape[0]
    W = MIXW
    hT = [_C(x[c // 4, (c % 4) * NTOK:(c % 4 + 1) * NTOK, :].T) for c in range(8)]
    memT = [_C(mem[b].T) for b in range(2)]
    mixT = None
    prev = None
    for i in range(depth + 1):
        nxt = KINDS[i % 3] if i < depth else None
        j = i // 3
        final = (i == depth)
        nc = _prog(("tok", prev, nxt, final), lambda: build_tok(prev, nxt, final))
        in_maps = []
        for c in range(8):
            m = {"hT": hT[c]}
            if prev:
                m.update(mixT=mixT[c], w_out=f(w_out[i - 1]), norm2=f(norm2[i - 1]), w_up=f(w_up[i - 1]), w_down=f(w_down[i - 1]))
                if prev == "s5":
                    jp = (i - 1) // 3
                    m.update(w_glu=f(s5_w_glu[jp]), b_glu=f(s5_b_glu[jp]))
            if nxt:
                w_in = {"s5": s5_w_in, "gdn": gdn_w_in, "fox": fox_w_in}[nxt][j]
                m.update(norm1=f(norm1[i]), w_in=f(w_in), memT=memT[c // 4], mem_norm=f(mem_norm), w_mem_kv=f(w_mem_kv))
            if final:
                m.update(norm_f=f(norm_f))
            in_maps.append(m)
        r = _run(nc, in_maps)
        if final:
            out = np.empty((2, SEQ, D), np.float32)
            for c in range(8):
                out[c // 4, (c % 4) * NTOK:(c % 4 + 1) * NTOK, :] = r[c]["outT"].T
            return out
        hT = [r[c]["hT_out"] for c in range(8)]
        readT = [r[c]["readT"] for c in range(8)]
        projB = [np.concatenate([r[b * 4 + q]["projT"] for q in range(4)], axis=1) for b in range(2)]
        if nxt != "s5":
            smallB = [np.concatenate([r[b * 4 + q]["smallT"] for q in range(4)], axis=1) for b in range(2)]
        mixB = [np.empty((W, SEQ), np.float32) for _ in range(2)]
        if nxt == "s5":
            ncm = _prog(("s5",), build_s5)
            in_maps = []
            for c in range(8):
                b, sub = c // 4, c % 4
                g0 = sub * 24
                in_maps.append({
                    "uT": _C(projB[b][sub * 384:(sub + 1) * 384].reshape(3, 128, SEQ)),
                    "lam_re": _C(f(s5_lam_re[j])[g0:g0 + 24]), "lam_im": _C(f(s5_lam_im[j])[g0:g0 + 24]),
                    "log_dt": _C(f(s5_log_dt[j])[g0:g0 + 24][None, :]),
                    "b_re": _C(f(s5_b_re[j])[g0:g0 + 24]), "b_im": _C(f(s5_b_im[j])[g0:g0 + 24]),
                    "c_re": _C(f(s5_c_re[j])[g0:g0 + 24].reshape(384, 64)), "c_im": _C(f(s5_c_im[j])[g0:g0 + 24].reshape(384, 64)),
                    "d_skip": _C(f(s5_d_skip[j])[sub * 384:(sub + 1) * 384])})
            rm = _run(ncm, in_maps)
            for c in range(8):
                mixB[c // 4][(c % 4) * 384:(c % 4 + 1) * 384] = rm[c]["yT"].reshape(384, SEQ)
        elif nxt == "fox":
            ncm = _prog(("fox",), build_fox)
            in_maps = []
            for c in range(8):
                b, sub = c // 4, c % 4
                hs = [sub * 3 + t for t in range(3)]
                P = projB[b]
                in_maps.append({
                    "qT": _C(np.stack([P[h * 128:(h + 1) * 128] for h in hs])),
                    "kT": _C(np.stack([P[W + h * 128:W + (h + 1) * 128] for h in hs])),
                    "v": _C(np.stack([P[2 * W + h * 128:2 * W + (h + 1) * 128].T for h in hs])),
                    "fl": _C(np.stack([smallB[b][h] for h in hs])),
                    "b_f": _C(f(fox_b_f[j])[hs[0]:hs[0] + 3])})
            rm = _run(ncm, in_maps)
            for c in range(8):
                mixB[c // 4][(c % 4) * 384:(c % 4 + 1) * 384] = rm[c]["oT"].reshape(384, SEQ)
        else:
            cwj, alj, dbj, onj = f(gdn_conv_w[j]), f(gdn_a_log[j]), f(gdn_dt_bias[j]), f(gdn_o_norm[j])
            for rnd, nh in ((0, 2), (1, 1)):
                ncm = _prog(("gdn", nh), lambda: build_gdn(nh=nh))
                in_maps = []
                hsl = []
                for c in range(8):
                    b, sub = c // 4, c % 4
                    hs = [sub * 3 + t for t in range(3)][(0 if rnd == 0 else 2):(2 if rnd == 0 else 3)]
                    hsl.append(hs)
                    P, Sm = projB[b], smallB[b]
                    in_maps.append({
                        "qkvT": _C(np.stack([np.stack([P[t * W + h * 128:t * W + (h + 1) * 128] for t in range(3)]) for h in hs])),
                        "gateT": _C(np.stack([P[3 * W + h * 128:3 * W + (h + 1) * 128] for h in hs])),
                        "cw": _C(np.stack([np.stack([cwj[:, t * W + h * 128:t * W + (h + 1) * 128].T for t in range(3)]) for h in hs])),
                        "abrow": _C(np.stack([np.stack([Sm[h], Sm[12 + h]]) for h in hs])),
                        "abcol": _C(np.stack([np.stack([Sm[h].reshape(-1, CL).T, Sm[12 + h].reshape(-1, CL).T]) for h in hs])),
                        "hp": _C(np.stack([np.stack([alj[h], dbj[h]]) for h in hs]).astype(np.float32)),
                        "o_norm": onj})
                rm = _run(ncm, in_maps)
                for c in range(8):
                    for t, h in enumerate(hsl[c]):
                        mixB[c // 4][h * 128:(h + 1) * 128] = rm[c]["oT"][t]
        mixT = [_C(np.concatenate([mixB[c // 4][:, (c % 4) * NTOK:(c % 4 + 1) * NTOK], readT[c]], axis=0)) for c in range(8)]
        prev = nxt
```

```python
import contextlib
import math
import numpy as np
import concourse.bass as bass
import concourse.mybir as mybir
from concourse.bass_utils import run_bass_kernel_spmd

F32 = mybir.dt.float32
BF16 = mybir.dt.bfloat16
AF = mybir.ActivationFunctionType
ALU = mybir.AluOpType
AX = mybir.AxisListType


class Tl:
    __slots__ = ("ap", "lw", "rd", "name", "dsem", "pend")

    def __init__(self, ap, name=""):
        self.ap = ap
        self.lw = None
        self.rd = []
        self.name = name
        self.dsem = None
        self.pend = None

    def __getitem__(self, idx):
        return self.ap[idx]


class Eng:
    def __init__(self, k, e, name):
        self.k = k
        self.e = e
        self.name = name
        self.sem = k.new_sem("e_" + name)
        self.count = 0
        self.known = {}
        self.pending = []


class K:
    def __init__(self, nc):
        self.nc = nc
        self.ctx = contextlib.ExitStack()
        self.sems = {}
        self.dma_tot = {}
        self.nsem = 0
        self.pe = Eng(self, nc.tensor, "pe")
        self.act = Eng(self, nc.scalar, "act")
        self.dve = Eng(self, nc.vector, "dve")
        self.pool = Eng(self, nc.gpsimd, "pool")
        self.sp = Eng(self, nc.sync, "sp")
        self.engs = [self.pe, self.act, self.dve, self.pool, self.sp]
        self.ndma = 0
        self.dma_rr = 0
        self.dma_pool = [self.new_sem("dma%d" % i) for i in range(24)]
        self.dma_last_ev = {}

    def new_sem(self, name):
        h = self.ctx.enter_context(self.nc.semaphore(name))
        key = self.nsem
        self.nsem += 1
        self.sems[key] = h
        self.dma_tot[key] = 0
        return key

    def sb(self, name, shape, dt):
        t = self.ctx.enter_context(self.nc.sbuf_tensor(name, list(shape), dt))
        return Tl(t.ap() if hasattr(t, "ap") and callable(getattr(t, "ap")) else t, name)

    def ps(self, name, shape, dt=F32):
        t = self.ctx.enter_context(self.nc.psum_tensor(name, list(shape), dt))
        return Tl(t.ap() if hasattr(t, "ap") and callable(getattr(t, "ap")) else t, name)

    def sub(self, tl, ap, name=""):
        return Tl(ap, name or tl.name)

    def _wait(self, eng, deps):
        need = {}
        for d in deps:
            if d is None:
                continue
            s, v = d
            if s in self.dma_tot and self.dma_tot[s] > 0:
                v = self.dma_tot[s]
            if need.get(s, 0) < v:
                need[s] = v
        for s, v in need.items():
            if s == eng.sem and eng is self.pe:
                continue
            if eng.known.get(s, 0) < v:
                eng.e.wait_ge(self.sems[s], v)
                eng.known[s] = v

    def _deps(self, reads, writes, eng=None):
        deps = []
        for t in list(reads) + list(writes):
            if t.pend is not None and t.pend is not eng:
                raise RuntimeError("tile %s has uncommitted accesses on %s" % (t.name, t.pend.name))
        for t in reads:
            if t.lw is not None:
                deps.append(t.lw)
        for t in writes:
            if t.lw is not None:
                deps.append(t.lw)
            deps.extend(t.rd)
        return deps

    def _commit(self, ev, reads, writes):
        for t in reads:
            t.rd.append(ev)
            if len(t.rd) > 64:
                best = {}
                for s, v in t.rd:
                    if best.get(s, 0) < v:
                        best[s] = v
                t.rd = list(best.items())
        for t in writes:
            t.lw = ev
            t.rd = []

    def op(self, eng, fn, reads=(), writes=(), inc=True):
        self._wait(eng, self._deps(reads, writes, eng))
        ins = fn(eng.e)
        if not inc:
            assert eng is self.pe
            eng.pending.append((reads, writes))
            for t in list(reads) + list(writes):
                t.pend = eng
            return None
        eng.count += 1
        ins.then_inc(self.sems[eng.sem], 1)
        ev = (eng.sem, eng.count)
        for r_, w_ in eng.pending:
            for t in list(r_) + list(w_):
                t.pend = None
            self._commit(ev, r_, w_)
        eng.pending = []
        self._commit(ev, reads, writes)
        return ev

    def dma(self, out_ap, in_ap, reads=(), writes=(), q=None, **kw):
        q = q or self.sp
        self._wait(q, self._deps(reads, writes, q))
        s = self.dma_pool[self.dma_rr % len(self.dma_pool)]
        self.dma_rr += 1
        ins = q.e.dma_start(out=out_ap, in_=in_ap, **kw)
        ins.then_inc(self.sems[s], 16)
        self.dma_tot[s] += 16
        ev = (s, self.dma_tot[s])
        self._commit(ev, reads, writes)
        return ev

    def finish(self, tiles):
        deps = []
        for t in tiles:
            if t.lw is not None:
                deps.append(t.lw)
            deps.extend(t.rd)
        self._wait(self.sp, deps)

    def close(self):
        self.ctx.close()


D = 2048
DFF = 8192
KC = 16
TT = 512
NTOK = 2048
SEQ = 8192
MEMW = 512
MIXW = 1536
EPS = 1e-6
SCALE = 128 ** -0.5


class Ring:
    def __init__(self, tiles):
        self.t = tiles
        self.i = 0

    def next(self):
        t = self.t[self.i % len(self.t)]
        self.i += 1
        return t


def make_consts(k):
    c = {}
    c["ones_f"] = k.sb("ones_f", [128, 128], F32)
    c["ones_b"] = k.sb("ones_b", [128, 128], BF16)
    c["ident"] = k.sb("ident", [128, 128], F32)
    c["eps"] = k.sb("eps_c", [128, 1], F32)
    k.op(k.dve, lambda e: e.memset(c["eps"].ap, EPS), [], [c["eps"]])
    k.op(k.dve, lambda e: e.memset(c["ones_f"].ap, 1.0), [], [c["ones_f"]])
    k.op(k.dve, lambda e: e.memset(c["ones_b"].ap, 1.0), [], [c["ones_b"]])
    k.op(k.pool, lambda e: e.memset(c["ident"].ap, 1.0), [], [c["ident"]])
    k.op(k.pool, lambda e: e.affine_select(out=c["ident"].ap, in_=c["ident"].ap, pattern=[[-1, 128]],
                                           compare_op=ALU.is_equal, fill=0.0, base=0, channel_multiplier=1),
         [c["ident"]], [c["ident"]])
    c["ident_b"] = k.sb("ident_b", [128, 128], BF16)
    k.op(k.dve, lambda e: e.tensor_copy(c["ident_b"].ap, c["ident"].ap), [c["ident"]], [c["ident_b"]])
    return c


def linear(k, psr, wring, xT, kcs, w_ap, chunks, cb, ntok=TT, xsl=None):
    groups = []
    cur = []
    for i, (c0, wd) in enumerate(chunks):
        if cur and (len(cur) == 4 or chunks[cur[-1]][0] + chunks[cur[-1]][1] != c0):
            groups.append(cur)
            cur = []
        cur.append(i)
    if cur:
        groups.append(cur)
    KB = 4
    for g in groups:
        g0 = chunks[g[0]][0]
        g1 = chunks[g[-1]][0] + chunks[g[-1]][1]
        ncol = g1 - g0
        pst = [psr.next() for _ in g]
        for kb in range(0, kcs, KB):
            nk = min(KB, kcs - kb)
            slab = wring.next()
            k.dma(slab[:, 0:nk, 0:ncol],
                  w_ap[kb * 128:(kb + nk) * 128, g0:g1].rearrange("(kc p) n -> p kc n", p=128),
                  writes=[slab], q=k.pool)
            for kk in range(nk):
                kc = kb + kk
                for j, ci in enumerate(g):
                    c0, wd = chunks[ci]
                    rhs = xT[:, kc, 0:ntok] if xsl is None else xsl(kc)
                    last = (kk == nk - 1 and j == len(g) - 1)
                    k.op(k.pe, lambda e, j=j, kk=kk, c0=c0, wd=wd, rhs=rhs, kc=kc: e.matmul(
                        pst[j][0:wd, 0:ntok], lhsT=slab[:, kk, c0 - g0:c0 - g0 + wd], rhs=rhs,
                        start=(kc == 0), stop=(kc == kcs - 1)), [slab, xT], [pst[j]], inc=last)
        for j, ci in enumerate(g):
            cb(ci, pst[j], chunks[ci][1])


def rmsnorm_T(k, c, psr, hT, gainT, outT, sqring, rstd, ntok=TT, kcs=KC, dmodel=D):
    ps = psr.next()
    for kc in range(kcs):
        sq = sqring.next()
        k.op(k.act, lambda e, kc=kc, sq=sq: e.activation(out=sq[:, 0:ntok], in_=hT[:, kc, 0:ntok], func=AF.Square),
             [hT], [sq])
        k.op(k.pe, lambda e, kc=kc, sq=sq: e.matmul(ps[:, 0:ntok], lhsT=c["ones_f"].ap, rhs=sq[:, 0:ntok],
                                                    start=(kc == 0), stop=(kc == kcs - 1)), [sq, c["ones_f"]], [ps])
    k.op(k.act, lambda e: e.activation(out=rstd[:, 0:ntok], in_=ps[:, 0:ntok], func=AF.Sqrt,
                                       bias=c["eps"].ap, scale=1.0 / dmodel), [ps, c["eps"]], [rstd])
    k.op(k.dve, lambda e: e.reciprocal(rstd[:, 0:ntok], rstd[:, 0:ntok]), [rstd], [rstd])
    for kc in range(kcs):
        k.op(k.dve, lambda e, kc=kc: e.scalar_tensor_tensor(out=outT[:, kc, 0:ntok], in0=hT[:, kc, 0:ntok],
                                                            scalar=gainT[:, kc:kc + 1], in1=rstd[:, 0:ntok],
                                                            op0=ALU.mult, op1=ALU.mult), [hT, gainT, rstd], [outT])


def load_vec_T(k, name, v_ap, n):
    t = k.sb(name + "_sb", [128, n // 128], F32)
    with k.nc.allow_non_contiguous_dma(reason="small param vector"):
        k.dma(t.ap, v_ap.rearrange("(kc p) -> p kc", p=128), writes=[t])
    return t


def in_chunks(kind):
    if kind == "s5":
        nm, small = 12, 0
    elif kind == "gdn":
        nm, small = 48, 24
    else:
        nm, small = 36, 12
    ch = [(i * 128, 128) for i in range(nm)]
    if small:
        ch.append((nm * 128, small))
    base = nm * 128 + small
    ch += [(base + i * 128, 128) for i in range(4)]
    return ch, nm, small


def build_tok(prev, nxt, final):
    nc = bass.Bass("TRN2", target_bir_lowering=False)
    k = K(nc)
    dt = nc.dram_tensor
    hT_d = dt("hT", [D, NTOK], F32, kind="ExternalInput").ap()
    if prev:
        mixT_d = dt("mixT", [D, NTOK], F32, kind="ExternalInput").ap()
        if prev == "s5":
            w_glu_d = dt("w_glu", [MIXW, MIXW], F32, kind="ExternalInput").ap()
            b_glu_d = dt("b_glu", [MIXW], F32, kind="ExternalInput").ap()
        w_out_d = dt("w_out", [D, D], F32, kind="ExternalInput").ap()
        norm2_d = dt("norm2", [D], F32, kind="ExternalInput").ap()
        w_up_d = dt("w_up", [D, DFF], F32, kind="ExternalInput").ap()
        w_down_d = dt("w_down", [DFF, D], F32, kind="ExternalInput").ap()
    if nxt:
        chunks, nm, small = in_chunks(nxt)
        win = chunks[-1][0] + 128
        norm1_d = dt("norm1", [D], F32, kind="ExternalInput").ap()
        w_in_d = dt("w_in", [D, win], F32, kind="ExternalInput").ap()
        memT_d = dt("memT", [D, 256], F32, kind="ExternalInput").ap()
        mem_norm_d = dt("mem_norm", [D], F32, kind="ExternalInput").ap()
        w_kv_d = dt("w_mem_kv", [D, 1024], F32, kind="ExternalInput").ap()
        hT_o = dt("hT_out", [D, NTOK], F32, kind="ExternalOutput").ap()
        projT_o = dt("projT", [nm * 128, NTOK], F32, kind="ExternalOutput").ap()
        readT_o = dt("readT", [MEMW, NTOK], F32, kind="ExternalOutput").ap()
        if small:
            smallT_o = dt("smallT", [small, NTOK], F32, kind="ExternalOutput").ap()
    if final:
        norm_f_d = dt("norm_f", [D], F32, kind="ExternalInput").ap()
        outT_o = dt("outT", [D, NTOK], F32, kind="ExternalOutput").ap()

    c = make_consts(k)
    hT = k.sb("hTt", [128, KC, TT], F32)
    aT = k.sb("aT", [128, KC, TT], BF16)
    gT = k.sb("gT", [128, 64, TT], BF16)
    mixT = k.sb("mixTt", [128, KC, TT], BF16)
    rstd = k.sb("rstd", [128, TT], F32)
    sqring = Ring([k.sb("sq%d" % i, [128, TT], F32) for i in range(2)])
    string = Ring([k.sb("stg%d" % i, [128, TT], F32) for i in range(4)])
    wring = Ring([k.sb("wsl%d" % i, [128, 4, 512], BF16) for i in range(6)])
    psr = Ring([k.ps("ps%d" % i, [128, 512]) for i in range(8)])
    outs = []

    if prev:
        norm2 = load_vec_T(k, "norm2", norm2_d, D)
        if prev == "s5":
            b_glu = load_vec_T(k, "b_glu", b_glu_d, MIXW)
            y32 = gT.ap.bitcast(F32)
    if final:
        norm_f = load_vec_T(k, "norm_f", norm_f_d, D)
    if nxt:
        norm1 = load_vec_T(k, "norm1", norm1_d, D)
        mem_norm = load_vec_T(k, "mem_norm", mem_norm_d, D)
        memkT = k.sb("memkT", [128, 4, 256], BF16)
        memv = k.sb("memv", [128, 2, 512], BF16)
        qmT = k.sb("qmT", [128, 4, TT], BF16)
        pT = [k.sb("pT%d" % i, [128, TT], BF16) for i in range(2)]
        rdT = k.sb("rdT", [128, 4, TT], F32)
        k.dma(hT[:, :, 0:256], memT_d.rearrange("(kc p) n -> p kc n", p=128), writes=[hT])
        rmsnorm_T(k, c, psr, hT, mem_norm, aT, sqring, rstd, ntok=256)

        def cb_k(i, ps, wd):
            k.op(k.act, lambda e: e.copy(memkT[:, i, :], ps[:, 0:256]), [ps], [memkT])
        linear(k, psr, wring, aT, KC, w_kv_d, [(i * 128, 128) for i in range(4)], cb_k, ntok=256)
        for mb in range(2):
            ps = psr.next()
            for kb in range(0, KC, 4):
                slab = wring.next()
                k.dma(slab[:, :, :], w_kv_d[kb * 128:(kb + 4) * 128, 512:1024].rearrange("(kc p) n -> p kc n", p=128),
                      writes=[slab], q=k.pool)
                for kk in range(4):
                    kc = kb + kk
                    k.op(k.pe, lambda e, kc=kc, kk=kk, slab=slab: e.matmul(
                        ps.ap, lhsT=aT[:, kc, mb * 128:(mb + 1) * 128], rhs=slab[:, kk, :],
                        start=(kc == 0), stop=(kc == KC - 1)), [aT, slab], [ps])
            k.op(k.act, lambda e, mb=mb, ps=ps: e.copy(memv[:, mb, :], ps.ap), [ps], [memv])

    for tt in range(NTOK // TT):
        t0 = tt * TT
        k.dma(hT.ap, hT_d[:, t0:t0 + TT].rearrange("(kc p) n -> p kc n", p=128), writes=[hT])
        if prev:
            if prev == "s5":
                def yv(oc):
                    return y32[:, 2 * oc:2 * oc + 2, :].rearrange("p a b -> p (a b)")
                for oc in range(12):
                    k.dma(yv(oc), mixT_d[oc * 128:(oc + 1) * 128, t0:t0 + TT], writes=[gT])
                k.dma(aT[:, 0:12, :], mixT_d[0:MIXW, t0:t0 + TT].rearrange("(kc p) n -> p kc n", p=128), writes=[aT], q=k.pool)
                k.dma(mixT[:, 12:16, :], mixT_d[MIXW:D, t0:t0 + TT].rearrange("(kc p) n -> p kc n", p=128), writes=[mixT], q=k.pool)

                def cb_glu(i, ps, wd):
                    st = string.next()
                    k.op(k.act, lambda e: e.activation(out=st.ap, in_=ps.ap, func=AF.Sigmoid, bias=b_glu[:, i:i + 1], scale=1.0),
                         [ps, b_glu], [st])
                    k.op(k.dve, lambda e: e.tensor_tensor(mixT[:, i, :], yv(i), st.ap, ALU.mult), [gT, st], [mixT])
                linear(k, psr, wring, aT, 12, w_glu_d, [(i * 128, 128) for i in range(12)], cb_glu)
            else:
                k.dma(mixT.ap, mixT_d[:, t0:t0 + TT].rearrange("(kc p) n -> p kc n", p=128), writes=[mixT], q=k.pool)

            def cb_res(i, ps, wd):
                k.op(k.dve, lambda e: e.tensor_tensor(hT[:, i, :], hT[:, i, :], ps.ap, ALU.add), [hT, ps], [hT])
            linear(k, psr, wring, mixT, KC, w_out_d, [(i * 128, 128) for i in range(KC)], cb_res)
            rmsnorm_T(k, c, psr, hT, norm2, aT, sqring, rstd)

            def cb_up(i, ps, wd):
                st = string.next()
                k.op(k.act, lambda e: e.activation(out=st.ap, in_=ps.ap, func=AF.Relu), [ps], [st])
                k.op(k.dve, lambda e: e.tensor_tensor(gT[:, i, :], st.ap, st.ap, ALU.mult), [st], [gT])
            linear(k, psr, wring, aT, KC, w_up_d, [(i * 128, 128) for i in range(64)], cb_up)
            linear(k, psr, wring, gT, 64, w_down_d, [(i * 128, 128) for i in range(KC)], cb_res)
        if nxt:
            ho = Tl(hT_o, "hT_o")
            k.dma(hT_o[:, t0:t0 + TT].rearrange("(kc p) n -> p kc n", p=128), hT.ap, reads=[hT], writes=[ho])
            outs.append(ho)
            rmsnorm_T(k, c, psr, hT, norm1, aT, sqring, rstd)

            def cb_in(i, ps, wd):
                if i < nm:
                    st = string.next()
                    k.op(k.act, lambda e: e.copy(st.ap, ps.ap), [ps], [st])
                    o = Tl(projT_o, "projT_o")
                    k.dma(projT_o[i * 128:(i + 1) * 128, t0:t0 + TT], st.ap, reads=[st], writes=[o])
                    outs.append(o)
                elif small and i == nm:
                    st = string.next()
                    k.op(k.act, lambda e: e.copy(st[0:wd, :], ps[0:wd, :]), [ps], [st])
                    o = Tl(smallT_o, "smallT_o")
                    k.dma(smallT_o[:, t0:t0 + TT], st[0:wd, :], reads=[st], writes=[o])
                    outs.append(o)
                else:
                    j = i - nm - (1 if small else 0)
                    k.op(k.act, lambda e: e.copy(qmT[:, j, :], ps.ap), [ps], [qmT])
            linear(k, psr, wring, aT, KC, w_in_d, chunks, cb_in)
            for hm in range(4):
                for mb in range(2):
                    ps = psr.next()
                    k.op(k.pe, lambda e, ps=ps, mb=mb: e.matmul(ps.ap, lhsT=memkT[:, hm, mb * 128:(mb + 1) * 128],
                                                                rhs=qmT[:, hm, :], start=True, stop=True),
                         [memkT, qmT], [ps])
                    k.op(k.act, lambda e, ps=ps, mb=mb: e.activation(out=pT[mb].ap, in_=ps.ap, func=AF.Exp, scale=SCALE),
                         [ps], [pT[mb]])
                pso = psr.next()
                psz = psr.next()
                for mb in range(2):
                    k.op(k.pe, lambda e, mb=mb: e.matmul(pso.ap, lhsT=memv[:, mb, hm * 128:(hm + 1) * 128], rhs=pT[mb].ap,
                                                         start=(mb == 0), stop=(mb == 1)), [memv, pT[mb]], [pso])
                for mb in range(2):
                    k.op(k.pe, lambda e, mb=mb: e.matmul(psz.ap, lhsT=c["ones_b"].ap, rhs=pT[mb].ap,
                                                         start=(mb == 0), stop=(mb == 1)), [c["ones_b"], pT[mb]], [psz])
                k.op(k.dve, lambda e: e.reciprocal(rstd.ap, psz.ap), [psz], [rstd])
                k.op(k.dve, lambda e: e.tensor_tensor(rdT[:, hm, :], pso.ap, rstd.ap, ALU.mult), [pso, rstd], [rdT])
            o = Tl(readT_o, "readT_o")
            k.dma(readT_o[:, t0:t0 + TT].rearrange("(kc p) n -> p kc n", p=128), rdT.ap, reads=[rdT], writes=[o])
            outs.append(o)
        if final:
            ps = psr.next()
            for kc in range(KC):
                sq = sqring.next()
                k.op(k.act, lambda e, kc=kc, sq=sq: e.activation(out=sq.ap, in_=hT[:, kc, :], func=AF.Square), [hT], [sq])
                k.op(k.pe, lambda e, kc=kc, sq=sq: e.matmul(ps.ap, lhsT=c["ones_f"].ap, rhs=sq.ap, start=(kc == 0),
                                                            stop=(kc == KC - 1)), [sq, c["ones_f"]], [ps])
            k.op(k.act, lambda e: e.activation(out=rstd.ap, in_=ps.ap, func=AF.Sqrt, bias=c["eps"].ap, scale=1.0 / D),
                 [ps, c["eps"]], [rstd])
            k.op(k.dve, lambda e: e.reciprocal(rstd.ap, rstd.ap), [rstd], [rstd])
            for kc in range(KC):
                st = string.next()
                k.op(k.dve, lambda e, kc=kc, st=st: e.scalar_tensor_tensor(out=st.ap, in0=hT[:, kc, :],
                                                                          scalar=norm_f[:, kc:kc + 1], in1=rstd.ap,
                                                                          op0=ALU.mult, op1=ALU.mult),
                     [hT, norm_f, rstd], [st])
                o = Tl(outT_o, "outT_o")
                k.dma(outT_o[kc * 128:(kc + 1) * 128, t0:t0 + TT], st.ap, reads=[st], writes=[o])
                outs.append(o)
    k.finish(outs)
    k.close()
    return nc


NEGBIG = -30000.0


def build_fox(nh=3, seq=SEQ):
    nc = bass.Bass("TRN2", target_bir_lowering=False)
    k = K(nc)
    dt = nc.dram_tensor
    qT_d = dt("qT", [nh, 128, seq], F32, kind="ExternalInput").ap()
    kT_d = dt("kT", [nh, 128, seq], F32, kind="ExternalInput").ap()
    v_d = dt("v", [nh, seq, 128], F32, kind="ExternalInput").ap()
    fl_d = dt("fl", [nh, seq], F32, kind="ExternalInput").ap()
    bf_d = dt("b_f", [nh], F32, kind="ExternalInput").ap()
    oT_o = dt("oT", [nh, 128, seq], F32, kind="ExternalOutput").ap()
    nb = seq // 128
    nqt = seq // TT
    c = make_consts(k)
    qTb = k.sb("qTb", [128, seq], BF16)
    kTb = k.sb("kTb", [128, seq], BF16)
    vb = k.sb("vb", [128, nb, 128], BF16)
    rowA = k.sb("rowA", [1, seq], F32)
    rowB = k.sb("rowB", [1, seq], F32)
    nbf = k.sb("nbf", [1, 4], F32)
    ncfT = k.sb("ncfT", [128, nb], F32)
    cfq = k.sb("cfq", [128, TT], F32)
    cfqm = [k.sb("cfqm%d" % i, [128, TT], F32) for i in range(4)]
    neg = [k.sb("neg%d" % i, [128, TT], F32) for i in range(4)]
    ering = Ring([k.sb("ein%d" % i, [128, TT], F32) for i in range(5)])
    pring = Ring([k.sb("pT%d" % i, [128, TT], BF16) for i in range(5)])
    oring = Ring([k.sb("ost%d" % i, [128, TT], F32) for i in range(2)])
    rz = k.sb("rz", [128, TT], F32)
    psS = Ring([k.ps("psS%d" % i, [128, TT]) for i in range(4)])
    psO = Ring([k.ps("psO%d" % i, [128, TT]) for i in range(2)])
    psZ = Ring([k.ps("psZ%d" % i, [128, TT]) for i in range(2)])
    outs = []
    for d in range(4):
        k.op(k.pool, lambda e, d=d: e.memset(neg[d].ap, 0.0), [], [neg[d]])
        k.op(k.pool, lambda e, d=d: e.affine_select(out=neg[d].ap, in_=neg[d].ap, pattern=[[1, TT]],
                                                    compare_op=ALU.is_ge, fill=NEGBIG, base=-d * 128,
                                                    channel_multiplier=-1), [neg[d]], [neg[d]])
    for h in range(nh):
        k.dma(qTb.ap, qT_d[h], writes=[qTb], q=k.pool)
        k.dma(kTb.ap, kT_d[h], writes=[kTb], q=k.pool)
        k.dma(vb.ap, v_d[h].rearrange("(j p) d -> p j d", p=128), writes=[vb], q=k.pool)
        k.dma(rowA.ap, fl_d[h:h + 1, :], writes=[rowA])
        k.dma(nbf[0:1, 0:1], bf_d[h:h + 1].rearrange("(a b) -> a b", a=1), writes=[nbf])
        k.op(k.dve, lambda e: e.tensor_scalar(nbf[0:1, 1:2], nbf[0:1, 0:1], -1.0, None, ALU.mult), [nbf], [nbf])
        k.op(k.act, lambda e: e.activation(out=rowB.ap, in_=rowA.ap, func=AF.Exp, bias=nbf[0:1, 1:2], scale=-1.0),
             [rowA, nbf], [rowB])
        k.op(k.act, lambda e: e.activation(out=rowB.ap, in_=rowB.ap, func=AF.Ln, bias=c["ones_f"][0:1, 0:1], scale=1.0),
             [rowB, c["ones_f"]], [rowB])
        k.op(k.dve, lambda e: e.tensor_tensor_scan(rowA.ap, c["ones_f"][0:1, 0:1].broadcast_to([1, seq]), rowB.ap, 0.0,
                                                   ALU.mult, ALU.subtract), [rowB, c["ones_f"]], [rowA])
        ps = psS.next()
        for j in range(nb):
            k.op(k.pe, lambda e, j=j: e.matmul(ps[:, j:j + 1], lhsT=rowA[0:1, j * 128:(j + 1) * 128],
                                               rhs=c["ones_f"][0:1, 0:1], start=True, stop=True),
                 [rowA, c["ones_f"]], [ps])
        k.op(k.dve, lambda e: e.tensor_scalar(ncfT.ap, ps[:, 0:nb], -1.0, None, ALU.mult), [ps], [ncfT])
        for qt in range(nqt):
            q0 = qt * TT
            ps = psS.next()
            k.op(k.pe, lambda e: e.matmul(ps.ap, lhsT=c["ones_f"][0:1, :], rhs=rowA[0:1, q0:q0 + TT], start=True, stop=True),
                 [rowA, c["ones_f"]], [ps])
            k.op(k.act, lambda e: e.copy(cfq.ap, ps.ap), [ps], [cfq])
            for d in range(4):
                k.op(k.pool, lambda e, d=d: e.tensor_tensor(cfqm[d].ap, cfq.ap, neg[d].ap, ALU.add), [cfq, neg[d]], [cfqm[d]])
            po = psO.next()
            pz = psZ.next()
            njb = 4 * (qt + 1)
            LA = 3
            pend = []
            for j in range(njb + LA):
                if j < njb:
                    ps = psS.next()
                    k.op(k.pe, lambda e, j=j, ps=ps: e.matmul(ps.ap, lhsT=kTb[:, j * 128:(j + 1) * 128], rhs=qTb[:, q0:q0 + TT],
                                                              start=True, stop=True), [kTb, qTb], [ps])
                    add = cfq if j < 4 * qt else cfqm[j - 4 * qt]
                    ein = ering.next()
                    k.op(k.dve, lambda e, ps=ps, add=add, ein=ein: e.scalar_tensor_tensor(
                        out=ein.ap, in0=ps.ap, scalar=SCALE, in1=add.ap, op0=ALU.mult, op1=ALU.add), [ps, add], [ein])
                    pT = pring.next()
                    k.op(k.act, lambda e, j=j, ein=ein, pT=pT: e.activation(out=pT.ap, in_=ein.ap, func=AF.Exp,
                                                                           bias=ncfT[:, j:j + 1], scale=1.0),
                         [ein, ncfT], [pT])
                    pend.append(pT)
                if j >= LA:
                    jj = j - LA
                    pT = pend[jj]
                    k.op(k.pe, lambda e, jj=jj, pT=pT: e.matmul(po.ap, lhsT=vb[:, jj, :], rhs=pT.ap, start=(jj == 0),
                                                                stop=(jj == njb - 1)), [vb, pT], [po])
                    k.op(k.pe, lambda e, jj=jj, pT=pT: e.matmul(pz.ap, lhsT=c["ones_b"].ap, rhs=pT.ap, start=(jj == 0),
                                                                stop=(jj == njb - 1)), [c["ones_b"], pT], [pz])
            k.op(k.dve, lambda e: e.reciprocal(rz.ap, pz.ap), [pz], [rz])
            ost = oring.next()
            k.op(k.dve, lambda e, ost=ost: e.tensor_tensor(ost.ap, po.ap, rz.ap, ALU.mult), [po, rz], [ost])
            o = Tl(oT_o, "oT_o")
            k.dma(oT_o[h, :, q0:q0 + TT], ost.ap, reads=[ost], writes=[o])
            outs.append(o)
    k.finish(outs)
    k.close()
    return nc


CL = 64


def build_gdn(nh=3, seq=SEQ):
    nc = bass.Bass("TRN2", target_bir_lowering=False)
    k = K(nc)
    dt = nc.dram_tensor
    nch = seq // CL
    ngrp = seq // TT
    qkvT_d = dt("qkvT", [nh, 3, 128, seq], F32, kind="ExternalInput").ap()
    gateT_d = dt("gateT", [nh, 128, seq], F32, kind="ExternalInput").ap()
    cw_d = dt("cw", [nh, 3, 128, 4], F32, kind="ExternalInput").ap()
    abrow_d = dt("abrow", [nh, 2, seq], F32, kind="ExternalInput").ap()
    abcol_d = dt("abcol", [nh, 2, CL, nch], F32, kind="ExternalInput").ap()
    hp_d = dt("hp", [nh, 2], F32, kind="ExternalInput").ap()
    onorm_d = dt("o_norm", [128], F32, kind="ExternalInput").ap()
    oT_o = dt("oT", [nh, 128, seq], F32, kind="ExternalOutput").ap()

    c = make_consts(k)
    ident = c["ident"]
    raw = k.sb("raw", [128, seq], F32)
    acc = k.sb("acc", [128, seq], F32)
    qT = k.sb("qT", [128, seq], BF16)
    kT = k.sb("kT", [128, seq], BF16)
    vT = k.sb("vT", [128, seq], BF16)
    cw = k.sb("cw_sb", [128, 3, 4], F32)
    hp = k.sb("hp_sb", [128, 4], F32)
    onorm = k.sb("onorm_sb", [128, 1], F32)
    acol = k.sb("acol", [CL, nch], F32)
    bcol = k.sb("bcol", [CL, nch], F32)
    gccol = k.sb("gccol", [CL, nch], F32)
    egcol = k.sb("egcol", [CL, nch], F32)
    edcol = k.sb("edcol", [CL, nch], F32)
    glast = k.sb("glast", [128, nch], F32)
    triU = k.sb("triU", [CL, CL], F32)
    maskU = k.sb("maskU", [CL, CL], F32)
    maskL = k.sb("maskL", [CL, CL], F32)
    sU01 = k.sb("sU01", [CL, CL], F32)
    S = k.sb("S", [128, 128], F32)
    Sb = k.sb("Sb", [128, 128], BF16)
    sq = Ring([k.sb("sq%d" % i, [128, TT], F32) for i in range(2)])
    rstd = k.sb("rstd", [128, TT], F32)
    qdG = k.sb("qdG", [128, TT], BF16)
    kbG = k.sb("kbG", [128, TT], BF16)
    GBs = k.sb("GBs", [CL, TT], F32)
    gst = Ring([k.sb("gst%d" % i, [128, TT], F32) for i in range(2)])
    ost = Ring([k.sb("ost%d" % i, [128, TT], F32) for i in range(2)])

    def small(name, shape, dtp, n=9):
        return Ring([k.sb("%s%d" % (name, i), shape, dtp) for i in range(n)])
    vb_r = small("vb", [CL, 128], BF16)
    kbd_r = small("kbd", [CL, 128], BF16)
    kdec_r = small("kdec", [CL, 128], BF16)
    tmp_r = small("tmp", [CL, CL], F32, 4)
    dec_r = small("dec", [CL, CL], F32)
    decT_r = small("decT", [CL, CL], F32)
    decTs_r = small("decTs", [CL, CL], F32)
    P_r = small("P", [CL, CL], F32, 17)
    Pt_r = small("Pt", [CL, CL], F32, 17)
    Tt_r = small("Tt", [CL, CL], F32, 17)
    Ttb_r = small("Ttb", [CL, CL], BF16)
    attT_r = small("attT", [CL, CL], BF16)
    nwT_r = small("nwT", [128, CL], BF16)
    vnew_r = small("vnew", [CL, 128], BF16, 2)
    psA = Ring([k.ps("psA%d" % i, [128, TT]) for i in range(2)])
    psB = Ring([k.ps("psB%d" % i, [128, 128]) for i in range(4)])
    psT = Ring([k.ps("psT%d" % i, [CL, 128], BF16) for i in range(2)])
    outs = []

    k.op(k.pool, lambda e: e.memset(triU.ap, 1.0), [], [triU])
    k.op(k.pool, lambda e: e.affine_select(out=triU.ap, in_=triU.ap, pattern=[[1, CL]], compare_op=ALU.is_ge, fill=0.0,
                                           base=0, channel_multiplier=-1), [triU], [triU])
    k.op(k.pool, lambda e: e.memset(maskU.ap, 0.0), [], [maskU])
    k.op(k.pool, lambda e: e.affine_select(out=maskU.ap, in_=maskU.ap, pattern=[[1, CL]], compare_op=ALU.is_ge,
                                           fill=NEGBIG, base=0, channel_multiplier=-1), [maskU], [maskU])
    k.op(k.pool, lambda e: e.memset(maskL.ap, 0.0), [], [maskL])
    k.op(k.pool, lambda e: e.affine_select(out=maskL.ap, in_=maskL.ap, pattern=[[-1, CL]], compare_op=ALU.is_ge,
                                           fill=-NEGBIG, base=-1, channel_multiplier=1), [maskL], [maskL])
    k.op(k.pool, lambda e: e.memset(sU01.ap, 1.0), [], [sU01])
    k.op(k.pool, lambda e: e.affine_select(out=sU01.ap, in_=sU01.ap, pattern=[[1, CL]], compare_op=ALU.is_ge, fill=0.0,
                                           base=-1, channel_multiplier=-1), [sU01], [sU01])
    with nc.allow_non_contiguous_dma(reason="small params"):
        k.dma(onorm.ap, onorm_d.rearrange("(p a) -> p a", a=1), writes=[onorm])

    for h in range(nh):
        with nc.allow_non_contiguous_dma(reason="small params"):
            k.dma(cw.ap, cw_d[h].rearrange("t p j -> p t j"), writes=[cw])
            k.dma(hp[:, 0:2], hp_d[h:h + 1, :].partition_broadcast(128), writes=[hp])
        k.op(k.act, lambda e: e.activation(out=hp[:, 2:3], in_=hp[:, 0:1], func=AF.Exp), [hp], [hp])
        k.op(k.dve, lambda e: e.tensor_scalar(hp[:, 2:3], hp[:, 2:3], -1.0, None, ALU.mult), [hp], [hp])
        k.dma(acol.ap, abcol_d[h, 0], writes=[acol])
        k.dma(bcol.ap, abcol_d[h, 1], writes=[bcol])
        k.op(k.act, lambda e: e.activation(out=bcol.ap, in_=bcol.ap, func=AF.Sigmoid), [bcol], [bcol])
        k.op(k.act, lambda e: e.activation(out=acol.ap, in_=acol.ap, func=AF.Exp, bias=hp[0:CL, 1:2], scale=1.0), [acol, hp], [acol])
        k.op(k.act, lambda e: e.activation(out=acol.ap, in_=acol.ap, func=AF.Ln, bias=c["ones_f"][0:CL, 0:1], scale=1.0),
             [acol, c["ones_f"]], [acol])
        k.op(k.dve, lambda e: e.tensor_scalar(acol.ap, acol.ap, hp[0:CL, 2:3], None, ALU.mult), [acol, hp], [acol])
        ps = psB.next()
        k.op(k.pe, lambda e: e.matmul(ps[0:CL, 0:nch], lhsT=triU.ap, rhs=acol.ap, start=True, stop=True), [triU, acol], [ps])
        k.op(k.dve, lambda e: e.tensor_copy(gccol.ap, ps[0:CL, 0:nch]), [ps], [gccol])
        ps2 = psB.next()
        k.op(k.pe, lambda e: e.matmul(ps2[:, 0:nch], lhsT=c["ones_f"][0:CL, :], rhs=acol.ap, start=True, stop=True),
             [c["ones_f"], acol], [ps2])
        k.op(k.act, lambda e: e.activation(out=glast.ap, in_=ps2[:, 0:nch], func=AF.Exp), [ps2], [glast])
        k.op(k.dve, lambda e: e.tensor_tensor(edcol.ap, ps2[0:CL, 0:nch], gccol.ap, ALU.subtract), [ps2, gccol], [edcol])
        k.op(k.act, lambda e: e.activation(out=edcol.ap, in_=edcol.ap, func=AF.Exp), [edcol], [edcol])
        k.op(k.act, lambda e: e.activation(out=egcol.ap, in_=gccol.ap, func=AF.Exp), [gccol], [egcol])
        k.op(k.dve, lambda e: e.tensor_tensor(egcol.ap, egcol.ap, bcol.ap, ALU.mult), [egcol, bcol], [egcol])
        for ti, dst in enumerate((qT, kT, vT)):
            k.dma(raw.ap, qkvT_d[h, ti], writes=[raw])
            k.op(k.dve, lambda e: e.tensor_scalar(acc.ap, raw.ap, cw[:, ti, 3:4], None, ALU.mult), [raw, cw], [acc])
            for s in (1, 2, 3):
                k.op(k.dve, lambda e, s=s: e.scalar_tensor_tensor(out=acc[:, s:seq], in0=raw[:, 0:seq - s],
                                                                 scalar=cw[:, ti, 3 - s:4 - s], in1=acc[:, s:seq],
                                                                 op0=ALU.mult, op1=ALU.add), [raw, cw, acc], [acc])
            if ti == 2:
                k.op(k.act, lambda e: e.activation(out=vT.ap, in_=acc.ap, func=AF.Silu), [acc], [vT])
                continue
            k.op(k.act, lambda e: e.activation(out=acc.ap, in_=acc.ap, func=AF.Silu), [acc], [acc])
            for g in range(ngrp):
                sl = slice(g * TT, (g + 1) * TT)
                s_ = sq.next()
                k.op(k.act, lambda e, s_=s_, sl=sl: e.activation(out=s_.ap, in_=acc[:, sl], func=AF.Square), [acc], [s_])
                ps = psA.next()
                k.op(k.pe, lambda e, s_=s_, ps=ps: e.matmul(ps.ap, lhsT=c["ones_f"].ap, rhs=s_.ap, start=True, stop=True),
                     [s_, c["ones_f"]], [ps])
                k.op(k.act, lambda e, ps=ps: e.activation(out=rstd.ap, in_=ps.ap, func=AF.Sqrt, bias=c["eps"].ap, scale=1.0),
                     [ps, c["eps"]], [rstd])
                k.op(k.dve, lambda e: e.reciprocal(rstd.ap, rstd.ap), [rstd], [rstd])
                if ti == 0:
                    k.op(k.dve, lambda e, sl=sl: e.scalar_tensor_tensor(out=qT[:, sl], in0=acc[:, sl], scalar=SCALE, in1=rstd.ap,
                                                                       op0=ALU.mult, op1=ALU.mult), [acc, rstd], [qT])
                else:
                    k.op(k.dve, lambda e, sl=sl: e.tensor_tensor(kT[:, sl], acc[:, sl], rstd.ap, ALU.mult), [acc, rstd], [kT])
        k.dma(raw[32:33, :], abrow_d[h, 0:1, :], writes=[raw])
        k.dma(raw[0:1, :], abrow_d[h, 1:2, :], writes=[raw])
        k.op(k.act, lambda e: e.activation(out=raw[0:1, :], in_=raw[0:1, :], func=AF.Sigmoid), [raw], [raw])
        k.op(k.act, lambda e: e.activation(out=raw[32:33, :], in_=raw[32:33, :], func=AF.Exp, bias=hp[32:33, 1:2], scale=1.0),
             [raw, hp], [raw])
        k.op(k.act, lambda e: e.activation(out=raw[32:33, :], in_=raw[32:33, :], func=AF.Ln, bias=c["ones_f"][32:33, 0:1],
                                           scale=1.0), [raw, c["ones_f"]], [raw])
        k.op(k.dve, lambda e: e.tensor_scalar(raw[32:33, :], raw[32:33, :], hp[32:33, 2:3], None, ALU.mult), [raw, hp], [raw])
        k.op(k.pool, lambda e: e.memset(acc[32:33, :], 1.0), [acc], [acc])
        k.op(k.pool, lambda e: e.memset(acc[32:33, :].rearrange("p (n c) -> p n c", c=CL)[:, :, 0:1], 0.0), [acc], [acc])
        k.op(k.dve, lambda e: e.tensor_tensor_scan(raw[32:33, :], acc[32:33, :], raw[32:33, :], 0.0, ALU.mult, ALU.add),
             [raw, acc], [raw])
        k.op(k.act, lambda e: e.activation(out=raw[64:65, :], in_=raw[32:33, :], func=AF.Exp), [raw], [raw])
        k.op(k.dve, lambda e: e.memset(S.ap, 0.0), [], [S])
        k.op(k.act, lambda e: e.copy(Sb.ap, S.ap), [S], [Sb])
        for g in range(ngrp):
            sl = slice(g * TT, (g + 1) * TT)
            ps = psA.next()
            k.op(k.pe, lambda e, ps=ps: e.matmul(ps.ap, lhsT=c["ones_f"][64:65, :], rhs=raw[64:65, sl], start=True, stop=True),
                 [raw, c["ones_f"]], [ps])
            k.op(k.dve, lambda e, ps=ps: e.tensor_tensor(qdG.ap, qT[:, sl], ps.ap, ALU.mult), [qT, ps], [qdG])
            ps = psA.next()
            k.op(k.pe, lambda e, ps=ps: e.matmul(ps.ap, lhsT=c["ones_f"][0:1, :], rhs=raw[0:1, sl], start=True, stop=True),
                 [raw, c["ones_f"]], [ps])
            k.op(k.dve, lambda e, ps=ps: e.tensor_tensor(kbG.ap, kT[:, sl], ps.ap, ALU.mult), [kT, ps], [kbG])
            ps = psA.next()
            k.op(k.pe, lambda e, ps=ps: e.matmul(ps[0:CL, :], lhsT=c["ones_f"][32:33, 0:CL], rhs=raw[32:33, sl], start=True, stop=True),
                 [raw, c["ones_f"]], [ps])
            k.op(k.act, lambda e, ps=ps: e.copy(GBs.ap, ps[0:CL, :]), [ps], [GBs])
            NI = 8
            for sg in range(TT // CL // NI):
                cis = [sg * NI + u for u in range(NI)]
                ns = [g * (TT // CL) + ci for ci in cis]
                css = [slice(g * TT + ci * CL, g * TT + (ci + 1) * CL) for ci in cis]
                gss = [slice(ci * CL, (ci + 1) * CL) for ci in cis]
                X = [dict() for _ in cis]
                for u in range(NI):
                    n, cs, T = ns[u], css[u], X[u]
                    pk = psT.next()
                    k.op(k.pe, lambda e: e.transpose(pk.ap, kT[:, cs], c["ident_b"].ap), [kT, c["ident_b"]], [pk])
                    T["kbd"], T["kdec"], T["vb"] = kbd_r.next(), kdec_r.next(), vb_r.next()
                    k.op(k.dve, lambda e: e.tensor_scalar(T["kbd"].ap, pk.ap, egcol[:, n:n + 1], None, ALU.mult), [pk, egcol], [T["kbd"]])
                    k.op(k.act, lambda e: e.activation(out=T["kdec"].ap, in_=pk.ap, func=AF.Copy, scale=edcol[:, n:n + 1]),
                         [pk, edcol], [T["kdec"]])
                    pv = psT.next()
                    k.op(k.pe, lambda e: e.transpose(pv.ap, vT[:, cs], c["ident_b"].ap), [vT, c["ident_b"]], [pv])
                    k.op(k.act, lambda e: e.activation(out=T["vb"].ap, in_=pv.ap, func=AF.Copy, scale=bcol[:, n:n + 1]),
                         [pv, bcol], [T["vb"]])
                for u in range(NI):
                    n, gs, T = ns[u], gss[u], X[u]
                    t1 = tmp_r.next()
                    k.op(k.dve, lambda e: e.scalar_tensor_tensor(out=t1.ap, in0=GBs[:, gs], scalar=gccol[:, n:n + 1], in1=maskU.ap,
                                                                 op0=ALU.subtract, op1=ALU.add), [GBs, gccol, maskU], [t1])
                    T["decT"] = decT_r.next()
                    k.op(k.act, lambda e: e.activation(out=T["decT"].ap, in_=t1.ap, func=AF.Exp), [t1], [T["decT"]])
                    t2 = tmp_r.next()
                    k.op(k.dve, lambda e: e.scalar_tensor_tensor(out=t2.ap, in0=GBs[:, gs], scalar=gccol[:, n:n + 1], in1=maskL.ap,
                                                                 op0=ALU.subtract, op1=ALU.add), [GBs, gccol, maskL], [t2])
                    T["dec"] = dec_r.next()
                    k.op(k.act, lambda e: e.activation(out=T["dec"].ap, in_=t2.ap, func=AF.Exp, scale=-1.0), [t2], [T["dec"]])
                    T["decTs"] = decTs_r.next()
                    k.op(k.pool, lambda e: e.tensor_tensor(T["decTs"].ap, T["decT"].ap, sU01.ap, ALU.mult), [T["decT"], sU01], [T["decTs"]])
                for u in range(NI):
                    cs, gs, T = css[u], gss[u], X[u]
                    pL = psB.next()
                    k.op(k.pe, lambda e: e.matmul(pL[0:CL, 0:CL], lhsT=kbG[:, gs], rhs=kT[:, cs], start=True, stop=True), [kbG, kT], [pL])
                    T["P"] = P_r.next()
                    k.op(k.dve, lambda e: e.scalar_tensor_tensor(out=T["P"].ap, in0=pL[0:CL, 0:CL], scalar=-1.0, in1=T["dec"].ap,
                                                                 op0=ALU.mult, op1=ALU.mult), [pL, T["dec"]], [T["P"]])
                    pLt = psB.next()
                    k.op(k.pe, lambda e: e.matmul(pLt[0:CL, 0:CL], lhsT=kT[:, cs], rhs=kbG[:, gs], start=True, stop=True), [kbG, kT], [pLt])
                    T["Pt"] = Pt_r.next()
                    k.op(k.dve, lambda e: e.scalar_tensor_tensor(out=T["Pt"].ap, in0=pLt[0:CL, 0:CL], scalar=-1.0, in1=T["decTs"].ap,
                                                                 op0=ALU.mult, op1=ALU.mult), [pLt, T["decTs"]], [T["Pt"]])
                    pA = psB.next()
                    k.op(k.pe, lambda e: e.matmul(pA[0:CL, 0:CL], lhsT=kT[:, cs], rhs=qT[:, cs], start=True, stop=True), [qT, kT], [pA])
                    T["attT"] = attT_r.next()
                    k.op(k.dve, lambda e: e.tensor_tensor(T["attT"].ap, pA[0:CL, 0:CL], T["decT"].ap, ALU.mult), [pA, T["decT"]], [T["attT"]])
                    T["Tt"] = Tt_r.next()
                    k.op(k.pool, lambda e: e.tensor_tensor(T["Tt"].ap, T["Pt"].ap, ident[0:CL, 0:CL], ALU.add), [T["Pt"], ident], [T["Tt"]])
                for lev in range(1, 6):
                    for u in range(NI):
                        T = X[u]
                        p1 = psB.next()
                        k.op(k.pe, lambda e: e.matmul(p1[0:CL, 0:CL], lhsT=T["Pt"].ap, rhs=T["P"].ap, start=True, stop=True),
                             [T["Pt"], T["P"]], [p1])
                        T["Pn"] = P_r.next()
                        k.op(k.act, lambda e: e.copy(T["Pn"].ap, p1[0:CL, 0:CL]), [p1], [T["Pn"]])
                    if lev < 5:
                        for u in range(NI):
                            T = X[u]
                            p2 = psB.next()
                            k.op(k.pe, lambda e: e.matmul(p2[0:CL, 0:CL], lhsT=T["P"].ap, rhs=T["Pt"].ap, start=True, stop=True),
                                 [T["Pt"], T["P"]], [p2])
                            T["Ptn"] = Pt_r.next()
                            k.op(k.act, lambda e: e.copy(T["Ptn"].ap, p2[0:CL, 0:CL]), [p2], [T["Ptn"]])
                    for u in range(NI):
                        T = X[u]
                        p3 = psB.next()
                        k.op(k.pe, lambda e: e.matmul(p3[0:CL, 0:CL], lhsT=T["Pn"].ap, rhs=T["Tt"].ap, start=True, stop=True),
                             [T["Pn"], T["Tt"]], [p3])
                        Ttn = Tt_r.next()
                        k.op(k.dve, lambda e: e.tensor_tensor(Ttn.ap, T["Tt"].ap, p3[0:CL, 0:CL], ALU.add), [T["Tt"], p3], [Ttn])
                        T["P"], T["Pt"], T["Tt"] = T["Pn"], T.get("Ptn"), Ttn
                for u in range(NI):
                    T = X[u]
                    T["Ttb"] = Ttb_r.next()
                    k.op(k.act, lambda e: e.copy(T["Ttb"].ap, T["Tt"].ap), [T["Tt"]], [T["Ttb"]])
                    pw = psB.next()
                    k.op(k.pe, lambda e: e.matmul(pw[:, 0:CL], lhsT=T["kbd"].ap, rhs=T["Ttb"].ap, start=True, stop=True),
                         [T["kbd"], T["Ttb"]], [pw])
                    T["nwT"] = nwT_r.next()
                    k.op(k.act, lambda e: e.activation(out=T["nwT"].ap, in_=pw[:, 0:CL], func=AF.Copy, scale=-1.0), [pw], [T["nwT"]])
                for u in range(NI):
                    n, cs, gs, T = ns[u], css[u], gss[u], X[u]
                    pvn = psB.next()
                    k.op(k.pe, lambda e: e.matmul(pvn[0:CL, :], lhsT=T["Ttb"].ap, rhs=T["vb"].ap, start=True, stop=False),
                         [T["Ttb"], T["vb"]], [pvn])
                    k.op(k.pe, lambda e: e.matmul(pvn[0:CL, :], lhsT=T["nwT"].ap, rhs=Sb.ap, start=False, stop=True), [T["nwT"], Sb], [pvn])
                    vnew = vnew_r.next()
                    k.op(k.act, lambda e: e.copy(vnew.ap, pvn[0:CL, :]), [pvn], [vnew])
                    po = psB.next()
                    k.op(k.pe, lambda e: e.matmul(po[:, 0:CL], lhsT=Sb.ap, rhs=qdG[:, gs], start=True, stop=False), [Sb, qdG], [po])
                    k.op(k.pe, lambda e: e.matmul(po[:, 0:CL], lhsT=vnew.ap, rhs=T["attT"].ap, start=False, stop=True), [vnew, T["attT"]], [po])
                    k.op(k.dve, lambda e: e.tensor_copy(acc[:, cs], po[:, 0:CL]), [po], [acc])
                    pd = psB.next()
                    k.op(k.pe, lambda e: e.matmul(pd.ap, lhsT=T["kdec"].ap, rhs=vnew.ap, start=True, stop=True), [T["kdec"], vnew], [pd])
                    k.op(k.dve, lambda e: e.scalar_tensor_tensor(out=S.ap, in0=S.ap, scalar=glast[:, n:n + 1], in1=pd.ap,
                                                                 op0=ALU.mult, op1=ALU.add), [S, glast, pd], [S])
                    k.op(k.act, lambda e: e.copy(Sb.ap, S.ap), [S], [Sb])
        for g in range(ngrp):
            sl = slice(g * TT, (g + 1) * TT)
            gt = gst.next()
            k.dma(gt.ap, gateT_d[h, :, sl], writes=[gt])
            k.op(k.act, lambda e, gt=gt: e.activation(out=gt.ap, in_=gt.ap, func=AF.Silu), [gt], [gt])
            s_ = sq.next()
            k.op(k.act, lambda e, s_=s_: e.activation(out=s_.ap, in_=acc[:, sl], func=AF.Square), [acc], [s_])
            ps = psA.next()
            k.op(k.pe, lambda e, s_=s_, ps=ps: e.matmul(ps.ap, lhsT=c["ones_f"].ap, rhs=s_.ap, start=True, stop=True),
                 [s_, c["ones_f"]], [ps])
            k.op(k.act, lambda e, ps=ps: e.activation(out=rstd.ap, in_=ps.ap, func=AF.Sqrt, bias=c["eps"].ap, scale=1.0 / 128),
                 [ps, c["eps"]], [rstd])
            k.op(k.dve, lambda e: e.reciprocal(rstd.ap, rstd.ap), [rstd], [rstd])
            o_ = ost.next()
            k.op(k.dve, lambda e, o_=o_: e.scalar_tensor_tensor(out=o_.ap, in0=acc[:, sl], scalar=onorm[:, 0:1], in1=rstd.ap,
                                                               op0=ALU.mult, op1=ALU.mult), [acc, onorm, rstd], [o_])
            k.op(k.dve, lambda e, o_=o_, gt=gt: e.tensor_tensor(o_.ap, o_.ap, gt.ap, ALU.mult), [o_, gt], [o_])
            o = Tl(oT_o, "oT_o")
            k.dma(oT_o[h, :, sl], o_.ap, reads=[o_], writes=[o])
            outs.append(o)
    k.finish(outs)
    k.close()
    return nc


PCH = 128
TWO_PI = 2.0 * math.pi
C1_2PI = 6.28125
C2_2PI = TWO_PI - 6.28125


def build_s5(ng=24, seq=SEQ):
    nc = bass.Bass("TRN2", target_bir_lowering=False)
    k = K(nc)
    dt = nc.dram_tensor
    nfc = ng // 8
    ntile = seq // TT
    NJ = PCH + 1
    uT_d = dt("uT", [nfc, 128, seq], F32, kind="ExternalInput").ap()
    lre_d = dt("lam_re", [ng, 64], F32, kind="ExternalInput").ap()
    lim_d = dt("lam_im", [ng, 64], F32, kind="ExternalInput").ap()
    ldt_d = dt("log_dt", [1, ng], F32, kind="ExternalInput").ap()
    bre_d = dt("b_re", [ng, 64, 16], F32, kind="ExternalInput").ap()
    bim_d = dt("b_im", [ng, 64, 16], F32, kind="ExternalInput").ap()
    cre_d = dt("c_re", [ng * 16, 64], F32, kind="ExternalInput").ap()
    cim_d = dt("c_im", [ng * 16, 64], F32, kind="ExternalInput").ap()
    dsk_d = dt("d_skip", [nfc * 128], F32, kind="ExternalInput").ap()
    yT_o = dt("yT", [nfc, 128, seq], F32, kind="ExternalOutput").ap()

    c = make_consts(k)
    ident = c["ident"]
    uTb = [k.sb("uTb%d" % i, [128, seq], BF16) for i in range(nfc)]
    GRall = k.sb("GRall", [128, ng, TT], F32)
    GR = [Tl(GRall[:, g, :], "GR%d" % g) for g in range(ng)]
    cosT = k.sb("cosT", [128, ng, NJ], F32)
    sinT = k.sb("sinT", [128, ng, NJ], F32)
    W1 = [k.sb("W1_%d" % g, [128, 128], BF16) for g in range(ng)]
    W2 = [k.sb("W2_%d" % g, [128, 128], BF16) for g in range(ng)]
    CA = [k.sb("CA_%d" % g, [128, 128], BF16) for g in range(ng)]
    CB = [k.sb("CB_%d" % g, [128, 128], BF16) for g in range(ng)]
    ROT = [k.sb("ROT_%d" % g, [128, 128], F32) for g in range(ng)]
    init = [k.sb("init_%d" % g, [128, 1], F32) for g in range(ng)]
    LRe = k.sb("LRe", [128, ng], F32)
    LIm = k.sb("LIm", [128, ng], F32)
    DT = k.sb("DT", [128, ng], F32)
    RHO = k.sb("RHO", [128, ng], F32)
    TH = k.sb("TH", [128, ng], F32)
    pp = [k.sb("pp%d" % i, [128, ng], F32) for i in range(8)]
    sgnA = k.sb("sgnA", [128, 1], F32)
    sgnB = k.sb("sgnB", [128, 1], F32)
    gmask = k.sb("gmask", [128, 8], F32)
    SW = k.sb("SW", [128, 128], F32)
    E1 = k.sb("E1", [128, 128], F32)
    dsk = load_vec_T(k, "dsk", dsk_d, nfc * 128)
    BA = k.sb("BA", [128, ng, 16], F32)
    BB = k.sb("BB", [128, ng, 16], F32)
    BP1 = k.sb("BP1", [128, ng, 16], F32)
    BP2 = k.sb("BP2", [128, ng, 16], F32)
    u32 = Ring([k.sb("u32_%d" % i, [128, TT], F32) for i in range(2 * nfc)])
    xs = Ring([k.sb("xs%d" % i, [128, TT], F32) for i in range(4)])
    tt_r = Ring([k.sb("tt%d" % i, [128, TT], F32) for i in range(4)])
    z_r = Ring([k.sb("z%d" % i, [128, TT], BF16) for i in range(4)])
    ost = Ring([k.sb("ost%d" % i, [128, TT], F32) for i in range(2)])
    psX = Ring([k.ps("psX%d" % i, [128, TT]) for i in range(4)])
    psY = Ring([k.ps("psY%d" % i, [128, TT]) for i in range(2)])
    psI = Ring([k.ps("psI%d" % i, [128, 128]) for i in range(2)])
    outs = []

    with nc.allow_non_contiguous_dma(reason="small params"):
        for half in range(2):
            k.dma(LRe[half * 64:(half + 1) * 64, :], lre_d.rearrange("g p -> p g"), writes=[LRe])
            k.dma(LIm[half * 64:(half + 1) * 64, :], lim_d.rearrange("g p -> p g"), writes=[LIm])
        k.dma(DT.ap, ldt_d.partition_broadcast(128), writes=[DT])
        k.dma(BA[0:64], bre_d.rearrange("g p c -> p g c"), writes=[BA])
        k.dma(BA[64:128], bim_d.rearrange("g p c -> p g c"), writes=[BA])
        k.dma(BB[0:64], bim_d.rearrange("g p c -> p g c"), writes=[BB])
        k.dma(BB[64:128], bre_d.rearrange("g p c -> p g c"), writes=[BB])
    for i in range(nfc):
        k.dma(uTb[i].ap, uT_d[i], writes=[uTb[i]], q=k.pool)
    k.op(k.act, lambda e: e.activation(out=DT.ap, in_=DT.ap, func=AF.Exp), [DT], [DT])
    k.op(k.dve, lambda e: e.tensor_tensor(RHO.ap, LRe.ap, DT.ap, ALU.mult), [LRe, DT], [RHO])
    k.op(k.act, lambda e: e.activation(out=RHO.ap, in_=RHO.ap, func=AF.Exp), [RHO], [RHO])
    k.op(k.dve, lambda e: e.tensor_tensor(TH.ap, LIm.ap, DT.ap, ALU.mult), [LIm, DT], [TH])
    k.op(k.dve, lambda e: e.memset(sgnA.ap, 1.0), [], [sgnA])
    k.op(k.dve, lambda e: e.memset(sgnA[0:64, :], -1.0), [sgnA], [sgnA])
    k.op(k.dve, lambda e: e.tensor_scalar(sgnB.ap, sgnA.ap, -1.0, None, ALU.mult), [sgnA], [sgnB])
    nel = ng * NJ
    flat = GRall.ap.rearrange("p g t -> p (g t)")
    ANG = flat[:, 0:nel].rearrange("p (g j) -> p g j", g=ng)
    TF = flat[:, nel:2 * nel].rearrange("p (g j) -> p g j", g=ng)
    NI = flat[:, 2 * nel:3 * nel].bitcast(mybir.dt.int32).rearrange("p (g j) -> p g j", g=ng)
    JJ = flat[:, 3 * nel:3 * nel + NJ]
    k.op(k.pool, lambda e: e.iota(JJ, [[1, NJ]], base=0, channel_multiplier=0, allow_small_or_imprecise_dtypes=True),
         [], [GRall])
    k.op(k.dve, lambda e: e.tensor_tensor(ANG, TH.ap.unsqueeze(2).broadcast_to([128, ng, NJ]),
                                          JJ.unsqueeze(1).broadcast_to([128, ng, NJ]), ALU.mult), [TH, GRall], [GRall])
    for shift, outT in ((0.0, sinT), (0.5 * math.pi, cosT)):
        k.op(k.dve, lambda e: e.tensor_scalar(TF, ANG, shift, 1.0 / TWO_PI, ALU.add, ALU.mult), [GRall], [GRall])
        k.op(k.dve, lambda e: e.tensor_copy(NI, TF), [GRall], [GRall])
        k.op(k.dve, lambda e: e.tensor_copy(TF, NI), [GRall], [GRall])
        k.op(k.dve, lambda e: e.scalar_tensor_tensor(out=outT.ap, in0=TF, scalar=-C1_2PI, in1=ANG, op0=ALU.mult, op1=ALU.add),
             [GRall], [outT])
        k.op(k.dve, lambda e: e.scalar_tensor_tensor(out=outT.ap, in0=TF, scalar=-C2_2PI, in1=outT.ap, op0=ALU.mult, op1=ALU.add),
             [GRall, outT], [outT])
        k.op(k.dve, lambda e: e.tensor_scalar(outT.ap, outT.ap, shift, math.pi, ALU.add, ALU.min), [outT], [outT])
        k.op(k.dve, lambda e: e.tensor_scalar(outT.ap, outT.ap, -math.pi, None, ALU.max), [outT], [outT])
        k.op(k.act, lambda e: e.activation(out=outT.ap, in_=outT.ap, func=AF.Sin), [outT], [outT])
    cth, sth = cosT[:, :, 1], sinT[:, :, 1]
    are, aim, am1, den, zre, zim, t0_, t1_ = pp

    def tt(out, a, b, op, rd, wr):
        k.op(k.dve, lambda e: e.tensor_tensor(out, a, b, op), rd, wr)
    tt(are.ap, RHO.ap, cth, ALU.mult, [RHO, cosT], [are])
    tt(aim.ap, RHO.ap, sth, ALU.mult, [RHO, sinT], [aim])
    k.op(k.dve, lambda e: e.tensor_scalar(am1.ap, are.ap, -1.0, None, ALU.add), [are], [am1])
    tt(den.ap, LRe.ap, LRe.ap, ALU.mult, [LRe], [den])
    tt(t0_.ap, LIm.ap, LIm.ap, ALU.mult, [LIm], [t0_])
    tt(den.ap, den.ap, t0_.ap, ALU.add, [den, t0_], [den])
    k.op(k.dve, lambda e: e.reciprocal(den.ap, den.ap), [den], [den])
    tt(zre.ap, am1.ap, LRe.ap, ALU.mult, [am1, LRe], [zre])
    tt(t0_.ap, aim.ap, LIm.ap, ALU.mult, [aim, LIm], [t0_])
    tt(zre.ap, zre.ap, t0_.ap, ALU.add, [zre, t0_], [zre])
    tt(zre.ap, zre.ap, den.ap, ALU.mult, [zre, den], [zre])
    tt(zim.ap, aim.ap, LRe.ap, ALU.mult, [aim, LRe], [zim])
    tt(t0_.ap, am1.ap, LIm.ap, ALU.mult, [am1, LIm], [t0_])
    tt(zim.ap, zim.ap, t0_.ap, ALU.subtract, [zim, t0_], [zim])
    tt(zim.ap, zim.ap, den.ap, ALU.mult, [zim, den], [zim])
    k.op(k.dve, lambda e: e.tensor_scalar(t0_.ap, zim.ap, sgnA[:, 0:1], None, ALU.mult), [zim, sgnA], [t0_])
    k.op(k.dve, lambda e: e.tensor_scalar(t1_.ap, zre.ap, sgnB[:, 0:1], None, ALU.mult), [zre, sgnB], [t1_])

    def bc(t):
        return t.ap.unsqueeze(2).broadcast_to([128, ng, 16])
    tmpB = k.sb("tmpB", [128, ng, 16], F32)
    tt(BP1.ap, BA.ap, bc(zre), ALU.mult, [BA, zre], [BP1])
    tt(tmpB.ap, BB.ap, bc(t0_), ALU.mult, [BB, t0_], [tmpB])
    tt(BP1.ap, BP1.ap, tmpB.ap, ALU.add, [BP1, tmpB], [BP1])
    tt(BP2.ap, BB.ap, bc(t1_), ALU.mult, [BB, t1_], [BP2])
    tt(tmpB.ap, BA.ap, bc(zim), ALU.mult, [BA, zim], [tmpB])
    tt(BP2.ap, BP2.ap, tmpB.ap, ALU.add, [BP2, tmpB], [BP2])
    k.op(k.pool, lambda e: e.memset(gmask.ap, 1.0), [], [gmask])
    k.op(k.pool, lambda e: e.affine_select(out=gmask.ap, in_=gmask.ap, pattern=[[-16, 8]], compare_op=ALU.is_ge, fill=0.0,
                                           base=0, channel_multiplier=1), [gmask], [gmask])
    k.op(k.pool, lambda e: e.affine_select(out=gmask.ap, in_=gmask.ap, pattern=[[16, 8]], compare_op=ALU.is_ge, fill=0.0,
                                           base=15, channel_multiplier=-1), [gmask], [gmask])
    k.op(k.pool, lambda e: e.memset(SW.ap, 1.0), [], [SW])
    k.op(k.pool, lambda e: e.affine_select(out=SW.ap, in_=SW.ap, pattern=[[-1, 128]], compare_op=ALU.is_equal, fill=0.0,
                                           base=64, channel_multiplier=1), [SW], [SW])
    k.op(k.pool, lambda e: e.memset(E1.ap, 1.0), [], [E1])
    k.op(k.pool, lambda e: e.affine_select(out=E1.ap, in_=E1.ap, pattern=[[-1, 128]], compare_op=ALU.is_equal, fill=0.0,
                                           base=-64, channel_multiplier=1), [E1], [E1])
    k.op(k.pool, lambda e: e.tensor_tensor(SW.ap, SW.ap, E1.ap, ALU.subtract), [SW, E1], [SW])
    cin = k.sb("cin", [128, 128], F32)
    for fc in range(nfc):
        for src, Wl in ((BP1, W1), (BP2, W2)):
            ps = psI.next()
            k.op(k.pe, lambda e: e.transpose(ps.ap, src[:, fc * 8:(fc + 1) * 8, :].rearrange("p g c -> p (g c)"), ident.ap),
                 [src, ident], [ps])
            for gg in range(8):
                g = fc * 8 + gg
                k.op(k.dve, lambda e: e.tensor_scalar(Wl[g].ap, ps.ap, gmask[:, gg:gg + 1], None, ALU.mult), [ps, gmask], [Wl[g]])
        for which, Cl in ((0, CA), (1, CB)):
            first, second = (cre_d, cim_d) if which == 0 else (cim_d, cre_d)
            k.dma(cin[:, 0:64], first[fc * 128:(fc + 1) * 128, :], writes=[cin])
            k.dma(cin[:, 64:128], second[fc * 128:(fc + 1) * 128, :], writes=[cin])
            if which == 0:
                k.op(k.dve, lambda e: e.tensor_scalar(cin[:, 64:128], cin[:, 64:128], -1.0, None, ALU.mult), [cin], [cin])
            else:
                k.op(k.dve, lambda e: e.tensor_scalar(cin.ap, cin.ap, -1.0, None, ALU.mult), [cin], [cin])
            ps = psI.next()
            k.op(k.pe, lambda e: e.transpose(ps.ap, cin.ap, ident.ap), [cin, ident], [ps])
            for gg in range(8):
                g = fc * 8 + gg
                k.op(k.pool, lambda e: e.memset(Cl[g].ap, 0.0), [], [Cl[g]])
                k.op(k.act, lambda e: e.copy(Cl[g][:, gg * 16:(gg + 1) * 16], ps[:, gg * 16:(gg + 1) * 16]), [ps, Cl[g]], [Cl[g]])
    for g in range(ng):
        k.op(k.dve, lambda e: e.tensor_scalar(E1.ap, ident.ap, cosT[:, g, PCH:PCH + 1], None, ALU.mult), [ident, cosT, E1], [E1])
        k.op(k.dve, lambda e: e.scalar_tensor_tensor(out=ROT[g].ap, in0=SW.ap, scalar=sinT[:, g, PCH:PCH + 1], in1=E1.ap,
                                                     op0=ALU.mult, op1=ALU.add), [SW, sinT, E1], [ROT[g]])
        k.op(k.pool, lambda e: e.memset(init[g].ap, 0.0), [], [init[g]])
    for g in range(ng):
        GR[g].lw = GRall.lw
        GR[g].rd = list(GRall.rd)

    npc = TT // PCH
    for ti in range(ntile):
        t0 = ti * TT
        uts = []
        for fc in range(nfc):
            ut = u32.next()
            k.dma(ut.ap, uT_d[fc, :, t0:t0 + TT], writes=[ut])
            uts.append(ut)
        for g in range(ng):
            fc = g // 8
            xss = []
            for Wl in (W1, W2):
                ps = psX.next()
                k.op(k.pe, lambda e: e.matmul(ps.ap, lhsT=Wl[g].ap, rhs=uTb[fc][:, t0:t0 + TT], start=True, stop=True),
                     [Wl[g], uTb[fc]], [ps])
                x = xs.next()
                k.op(k.act, lambda e: e.copy(x.ap, ps.ap), [ps], [x])
                xss.append(x)
            cb_ = cosT[:, g, 0:PCH].unsqueeze(1).broadcast_to([128, npc, PCH])
            sb_ = sinT[:, g, 0:PCH].unsqueeze(1).broadcast_to([128, npc, PCH])
            ta = tt_r.next()
            tb = tt_r.next()

            def v3(ap):
                return ap.rearrange("p (a b) -> p a b", a=npc)
            eg = k.pool if g % 4 == 0 else k.dve
            k.op(eg, lambda e: e.tensor_tensor(v3(ta.ap), v3(xss[0].ap), cb_, ALU.mult), [xss[0], cosT], [ta])
            k.op(eg, lambda e: e.tensor_tensor(v3(tb.ap), v3(xss[1].ap), sb_, ALU.mult), [xss[1], sinT], [tb])
            k.op(eg, lambda e: e.tensor_tensor(GR[g].ap, ta.ap, tb.ap, ALU.add), [ta, tb], [GR[g]])
        pys = {}
        for pc in range(npc):
            cs = slice(pc * PCH, (pc + 1) * PCH)
            for g in range(ng):
                es = k.dve
                k.op(es, lambda e: e.tensor_tensor_scan(GR[g][:, cs], RHO[:, g:g + 1].broadcast_to([128, PCH]), GR[g][:, cs],
                                                        init[g][:, 0:1], ALU.mult, ALU.add), [GR[g], RHO, init[g]], [GR[g]])
                pi = psI.next()
                k.op(k.pe, lambda e: e.matmul(pi[:, 0:1], lhsT=ROT[g].ap, rhs=GR[g][:, (pc + 1) * PCH - 1:(pc + 1) * PCH],
                                              start=True, stop=True), [ROT[g], GR[g]], [pi])
                k.op(k.act, lambda e: e.copy(init[g].ap, pi[:, 0:1]), [pi], [init[g]])
                if pc < npc - 1:
                    continue
                fc, gg = g // 8, g % 8
                if gg == 0:
                    pys[fc] = psY.next()
                py = pys[fc]
                cb_ = cosT[:, g, 0:PCH].unsqueeze(1).broadcast_to([128, npc, PCH])
                sb_ = sinT[:, g, 0:PCH].unsqueeze(1).broadcast_to([128, npc, PCH])
                z1 = z_r.next()
                z2 = z_r.next()
                k.op(k.dve, lambda e: e.tensor_tensor(v3(z1.ap), v3(GR[g].ap), cb_, ALU.mult), [GR[g], cosT], [z1])
                k.op(k.dve, lambda e: e.tensor_tensor(v3(z2.ap), v3(GR[g].ap), sb_, ALU.mult), [GR[g], sinT], [z2])
                k.op(k.pe, lambda e: e.matmul(py.ap, lhsT=CA[g].ap, rhs=z1.ap, start=(gg == 0), stop=False), [CA[g], z1], [py])
                k.op(k.pe, lambda e: e.matmul(py.ap, lhsT=CB[g].ap, rhs=z2.ap, start=False, stop=(gg == 7)), [CB[g], z2], [py])
                if gg == 7:
                    o_ = ost.next()
                    k.op(k.dve, lambda e: e.scalar_tensor_tensor(out=o_.ap, in0=uts[fc].ap, scalar=dsk[:, fc:fc + 1], in1=py.ap,
                                                                 op0=ALU.mult, op1=ALU.add), [uts[fc], dsk, py], [o_])
                    k.op(k.act, lambda e: e.activation(out=o_.ap, in_=o_.ap, func=AF.Gelu), [o_], [o_])
                    o = Tl(yT_o, "yT_o")
                    k.dma(yT_o[fc, :, t0:t0 + TT], o_.ap, reads=[o_], writes=[o])
                    outs.append(o)
    k.finish(outs)
    k.close()
    return nc


_PROGS = {}


def _prog(key, fn):
    if key not in _PROGS:
        _PROGS[key] = fn()
    return _PROGS[key]


def _run(nc, in_maps):
    res = run_bass_kernel_spmd(nc, in_maps, core_ids=list(range(8)))
    return res.results


KINDS = ["s5", "gdn", "fox"]
_C = np.ascontiguousarray


def kernel(x, mem, mem_norm, w_mem_kv, norm1, w_out, norm2, w_up, w_down, norm_f,
           s5_w_in, s5_lam_re, s5_lam_im, s5_log_dt, s5_b_re, s5_b_im, s5_c_re, s5_c_im,
           s5_d_skip, s5_w_glu, s5_b_glu,
           gdn_w_in, gdn_conv_w, gdn_a_log, gdn_dt_bias, gdn_o_norm,
           fox_w_in, fox_b_f):
    f = lambda a: np.asarray(a, dtype=np.float32)
    x, mem = f(x), f(mem)
    depth = norm1.shape[0]
    W = MIXW
    hT = [_C(x[c // 4, (c % 4) * NTOK:(c % 4 + 1) * NTOK, :].T) for c in range(8)]
    memT = [_C(mem[b].T) for b in range(2)]
    mixT = None
    prev = None
    for i in range(depth + 1):
        nxt = KINDS[i % 3] if i < depth else None
        j = i // 3
        final = (i == depth)
        nc = _prog(("tok", prev, nxt, final), lambda: build_tok(prev, nxt, final))
        in_maps = []
        for c in range(8):
            m = {"hT": hT[c]}
            if prev:
                m.update(mixT=mixT[c], w_out=f(w_out[i - 1]), norm2=f(norm2[i - 1]), w_up=f(w_up[i - 1]), w_down=f(w_down[i - 1]))
                if prev == "s5":
                    jp = (i - 1) // 3
                    m.update(w_glu=f(s5_w_glu[jp]), b_glu=f(s5_b_glu[jp]))
            if nxt:
                w_in = {"s5": s5_w_in, "gdn": gdn_w_in, "fox": fox_w_in}[nxt][j]
                m.update(norm1=f(norm1[i]), w_in=f(w_in), memT=memT[c // 4], mem_norm=f(mem_norm), w_mem_kv=f(w_mem_kv))
            if final:
                m.update(norm_f=f(norm_f))
            in_maps.append(m)
        r = _run(nc, in_maps)
        if final:
            out = np.empty((2, SEQ, D), np.float32)
            for c in range(8):
                out[c // 4, (c % 4) * NTOK:(c % 4 + 1) * NTOK, :] = r[c]["outT"].T
            return out
        hT = [r[c]["hT_out"] for c in range(8)]
        readT = [r[c]["readT"] for c in range(8)]
        projB = [np.concatenate([r[b * 4 + q]["projT"] for q in range(4)], axis=1) for b in range(2)]
        if nxt != "s5":
            smallB = [np.concatenate([r[b * 4 + q]["smallT"] for q in range(4)], axis=1) for b in range(2)]
        mixB = [np.empty((W, SEQ), np.float32) for _ in range(2)]
        if nxt == "s5":
            ncm = _prog(("s5",), build_s5)
            in_maps = []
            for c in range(8):
                b, sub = c // 4, c % 4
                g0 = sub * 24
                in_maps.append({
                    "uT": _C(projB[b][sub * 384:(sub + 1) * 384].reshape(3, 128, SEQ)),
                    "lam_re": _C(f(s5_lam_re[j])[g0:g0 + 24]), "lam_im": _C(f(s5_lam_im[j])[g0:g0 + 24]),
                    "log_dt": _C(f(s5_log_dt[j])[g0:g0 + 24][None, :]),
                    "b_re": _C(f(s5_b_re[j])[g0:g0 + 24]), "b_im": _C(f(s5_b_im[j])[g0:g0 + 24]),
                    "c_re": _C(f(s5_c_re[j])[g0:g0 + 24].reshape(384, 64)), "c_im": _C(f(s5_c_im[j])[g0:g0 + 24].reshape(384, 64)),
                    "d_skip": _C(f(s5_d_skip[j])[sub * 384:(sub + 1) * 384])})
            rm = _run(ncm, in_maps)
            for c in range(8):
                mixB[c // 4][(c % 4) * 384:(c % 4 + 1) * 384] = rm[c]["yT"].reshape(384, SEQ)
        elif nxt == "fox":
            ncm = _prog(("fox",), build_fox)
            in_maps = []
            for c in range(8):
                b, sub = c // 4, c % 4
                hs = [sub * 3 + t for t in range(3)]
                P = projB[b]
                in_maps.append({
                    "qT": _C(np.stack([P[h * 128:(h + 1) * 128] for h in hs])),
                    "kT": _C(np.stack([P[W + h * 128:W + (h + 1) * 128] for h in hs])),
                    "v": _C(np.stack([P[2 * W + h * 128:2 * W + (h + 1) * 128].T for h in hs])),
                    "fl": _C(np.stack([smallB[b][h] for h in hs])),
                    "b_f": _C(f(fox_b_f[j])[hs[0]:hs[0] + 3])})
            rm = _run(ncm, in_maps)
            for c in range(8):
                mixB[c // 4][(c % 4) * 384:(c % 4 + 1) * 384] = rm[c]["oT"].reshape(384, SEQ)
        else:
            cwj, alj, dbj, onj = f(gdn_conv_w[j]), f(gdn_a_log[j]), f(gdn_dt_bias[j]), f(gdn_o_norm[j])
            for rnd, nh in ((0, 2), (1, 1)):
                ncm = _prog(("gdn", nh), lambda: build_gdn(nh=nh))
                in_maps = []
                hsl = []
                for c in range(8):
                    b, sub = c // 4, c % 4
                    hs = [sub * 3 + t for t in range(3)][(0 if rnd == 0 else 2):(2 if rnd == 0 else 3)]
                    hsl.append(hs)
                    P, Sm = projB[b], smallB[b]
                    in_maps.append({
                        "qkvT": _C(np.stack([np.stack([P[t * W + h * 128:t * W + (h + 1) * 128] for t in range(3)]) for h in hs])),
                        "gateT": _C(np.stack([P[3 * W + h * 128:3 * W + (h + 1) * 128] for h in hs])),
                        "cw": _C(np.stack([np.stack([cwj[:, t * W + h * 128:t * W + (h + 1) * 128].T for t in range(3)]) for h in hs])),
                        "abrow": _C(np.stack([np.stack([Sm[h], Sm[12 + h]]) for h in hs])),
                        "abcol": _C(np.stack([np.stack([Sm[h].reshape(-1, CL).T, Sm[12 + h].reshape(-1, CL).T]) for h in hs])),
                        "hp": _C(np.stack([np.stack([alj[h], dbj[h]]) for h in hs]).astype(np.float32)),
                        "o_norm": onj})
                rm = _run(ncm, in_maps)
                for c in range(8):
                    for t, h in enumerate(hsl[c]):
                        mixB[c // 4][h * 128:(h + 1) * 128] = rm[c]["oT"][t]
        mixT = [_C(np.concatenate([mixB[c // 4][:, (c % 4) * NTOK:(c % 4 + 1) * NTOK], readT[c]], axis=0)) for c in range(8)]
        prev = nxt
```
